# Optimizing a Trainium2 kernel written in Bass

```python
import math
import jax, jax.numpy as jnp
from jax import lax
import numpy as np

D_MODEL = 1024
BATCH = 8
SEQ = 4096
DEPTH = 4

GRID_W = 64
CTX_LEN = 256
HEAD_DIM = 64
N_BRANCH = 4
BRANCH_W = D_MODEL // N_BRANCH
N_HEADS = BRANCH_W // HEAD_DIM
BLOCK = 128
ROPE_THETA = 10000.0
EPS = 1e-6
H_A = N_HEADS
DK_A = HEAD_DIM
DV_A = HEAD_DIM
CONV_K = 5
CHUNK = 64
H_B = N_HEADS
DH_B = HEAD_DIM // 2
DV_B = 2 * DH_B
HQ_C = N_HEADS
HKV_C = N_HEADS // 2
HQ_D = N_HEADS
HKV_D = N_HEADS // 2
WINDOW = 128
N_EXPERTS = 16
D_EXPERT = D_MODEL
CAPACITY_FACTOR = 2
IN_SPLITS = (H_A * DK_A, H_A * DK_A, H_A * DV_A, H_A, H_A, H_A, H_A, H_A * DV_A,
             2 * H_B * DH_B, 2 * H_B * DH_B, H_B * DV_B,
             HQ_C * HEAD_DIM, HKV_C * HEAD_DIM, HKV_C * HEAD_DIM,
             HQ_D * HEAD_DIM, HKV_D * HEAD_DIM, HKV_D * HEAD_DIM,
             N_BRANCH * D_MODEL)
D_IN = sum(IN_SPLITS)

kernel_name = 'hybrid_diffusion_gdn_diffattn_gqa_swa_ecmoe'

F32 = jnp.float32


def layer_norm(x, g, b):
    xf = x.astype(F32)
    mu = jnp.mean(xf, -1, keepdims=True)
    var = jnp.mean(jnp.square(xf - mu), -1, keepdims=True)
    return ((xf - mu) * lax.rsqrt(var + EPS) * g + b).astype(x.dtype)


def rms_norm(x, g):
    xf = x.astype(F32)
    return (xf * lax.rsqrt(jnp.mean(xf * xf, -1, keepdims=True) + EPS) * g).astype(x.dtype)


def l2norm(x):
    xf = x.astype(F32)
    return (xf * lax.rsqrt(jnp.sum(xf * xf, -1, keepdims=True) + EPS)).astype(x.dtype)


def split_cols(p):
    out, off = [], 0
    for w in IN_SPLITS:
        out.append(p[..., off:off + w])
        off += w
    return out


def to_heads(p, h):
    b_, n, _ = p.shape
    return p.reshape(b_, n, h, -1).transpose(0, 2, 1, 3)


def from_heads(o):
    b_, h, n, d = o.shape
    return o.transpose(0, 2, 1, 3).reshape(b_, n, h * d)


def to_groups(p, hk, g):
    b_, n, _ = p.shape
    return p.reshape(b_, n, hk, g, -1).transpose(0, 2, 3, 1, 4)


def from_groups(o):
    b_, hk, g, n, d = o.shape
    return o.transpose(0, 3, 1, 2, 4).reshape(b_, n, hk * g * d)


def axial_rope_tables(n, dim):
    rows = n // GRID_W
    row = jnp.repeat(jnp.arange(rows), GRID_W).astype(F32)
    col = jnp.tile(jnp.arange(GRID_W), rows).astype(F32)
    nf = dim // 4
    inv = ROPE_THETA ** (-jnp.arange(nf, dtype=F32) / nf)
    ar = row[:, None] * inv
    ac = col[:, None] * inv
    return (jnp.cos(ar), jnp.sin(ar), jnp.cos(ac), jnp.sin(ac))


def apply_axial_rope(x, tabs):
    cr, sr, cc, sc = tabs
    x1, x2, x3, x4 = jnp.split(x, 4, axis=-1)
    return jnp.concatenate([x1 * cr - x2 * sr, x2 * cr + x1 * sr,
                            x3 * cc - x4 * sc, x4 * cc + x3 * sc], axis=-1).astype(x.dtype)


def sweep_query_blocks(fn, q):
    n, d = q.shape[-2], q.shape[-1]
    nb = n // BLOCK
    qb = jnp.moveaxis(q.reshape(q.shape[:-2] + (nb, BLOCK, d)), -3, 0)
    o = jnp.moveaxis(lax.map(fn, qb), 0, -3)
    return o.reshape(o.shape[:-3] + (n, o.shape[-1]))


def short_conv(u, w):
    half = CONV_K // 2
    y = lax.conv_general_dilated(u, w[:, None, :].astype(u.dtype), window_strides=(1,),
                                 padding=[(half, half)], dimension_numbers=('NWC', 'WIO', 'NWC'),
                                 feature_group_count=u.shape[-1])
    return jax.nn.silu(y)


def gated_delta_chunked(q, k, v, g, beta):
    out_dtype = v.dtype
    b_, h, t, dk = q.shape
    dv = v.shape[-1]
    nc = t // CHUNK
    chunks = lambda a: a.astype(F32).reshape(b_, h, nc, CHUNK, -1)
    q, k, v = chunks(q), chunks(k), chunks(v)
    g = g.astype(F32).reshape(b_, h, nc, CHUNK)
    beta = beta.astype(F32).reshape(b_, h, nc, CHUNK)
    G = jnp.cumsum(g, axis=-1)
    incl = jnp.tril(jnp.ones((CHUNK, CHUNK), bool))
    strict = jnp.tril(jnp.ones((CHUNK, CHUNK), bool), -1)
    decay = jnp.exp(jnp.where(incl, G[..., :, None] - G[..., None, :], -jnp.inf))
    a_mat = jnp.where(strict, beta[..., :, None] * jnp.einsum('bhncd,bhnsd->bhncs', k, k) * decay, 0.0)
    l_mat = a_mat + jnp.eye(CHUNK, dtype=F32)
    rhs = jnp.concatenate([(beta * jnp.exp(G))[..., None] * k, beta[..., None] * v], axis=-1)
    sol = lax.linalg.triangular_solve(l_mat, rhs, left_side=True, lower=True, unit_diagonal=True)
    w_c, u_c = sol[..., :dk], sol[..., dk:]
    p_c = jnp.einsum('bhncd,bhnsd->bhncs', q, k) * decay
    qg = q * jnp.exp(G)[..., None]
    kd = k * jnp.exp(G[..., -1:] - G)[..., None]
    gl = jnp.exp(G[..., -1])

    def step(S, xs):
        w_i, u_i, p_i, qg_i, kd_i, gl_i = xs
        u = u_i - jnp.einsum('bhcd,bhde->bhce', w_i, S)
        o = jnp.einsum('bhcd,bhde->bhce', qg_i, S) + jnp.einsum('bhcs,bhse->bhce', p_i, u)
        S = gl_i[..., None, None] * S + jnp.einsum('bhcd,bhce->bhde', kd_i, u)
        return S, o

    xs = tuple(jnp.moveaxis(a, 2, 0) for a in (w_c, u_c, p_c, qg, kd, gl))
    _, o = lax.scan(step, jnp.zeros((b_, h, dk, dv), F32), xs)
    return jnp.moveaxis(o, 0, 2).reshape(b_, h, t, dv).astype(out_dtype)


def gdn_branch(p_lat, p_ctx, conv_w, a_log, dt_bias, norm_g, with_ctx):
    def prep(p):
        q, k, v, a_f, b_f, a_b, b_b, gate = p
        qkv = short_conv(jnp.concatenate([q, k, v], axis=-1), conv_w)
        q, k, v = jnp.split(qkv, 3, axis=-1)
        q = l2norm(to_heads(q, H_A)) * (DK_A ** -0.5)
        k = l2norm(to_heads(k, H_A))
        v = to_heads(v, H_A)

        def decay_gate(a, b, d):
            lg = -jnp.exp(a_log[d].astype(F32)) * jax.nn.softplus(a.astype(F32) + dt_bias[d])
            return jnp.swapaxes(lg, 1, 2), jnp.swapaxes(jax.nn.sigmoid(b.astype(F32)), 1, 2)
        return q, k, v, decay_gate(a_f, b_f, 0), decay_gate(a_b, b_b, 1), gate

    qc, kc, vc, fwd_c, bwd_c, gate_c = prep(p_ctx)
    ql, kl, vl, fwd_l, bwd_l, gate_l = prep(p_lat)
    lc = qc.shape[2]
    cat = lambda a, b: jnp.concatenate([a, b], axis=2)
    rev = lambda a: jnp.flip(a, axis=2)
    o_f = gated_delta_chunked(cat(qc, ql), cat(kc, kl), cat(vc, vl),
                              cat(fwd_c[0], fwd_l[0]), cat(fwd_c[1], fwd_l[1]))
    o_b = gated_delta_chunked(cat(rev(qc), rev(ql)), cat(rev(kc), rev(kl)), cat(rev(vc), rev(vl)),
                              cat(rev(bwd_c[0]), rev(bwd_l[0])), cat(rev(bwd_c[1]), rev(bwd_l[1])))

    def finish(o, gate):
        b_, h, n, d = o.shape
        o = rms_norm(jnp.swapaxes(o, 1, 2), norm_g) * jax.nn.silu(gate.reshape(b_, n, h, d))
        return o.reshape(b_, n, h * d)

    o_lat = finish(o_f[:, :, lc:] + rev(o_b[:, :, lc:]), gate_l)
    o_ctx = finish(o_f[:, :, :lc] + rev(o_b[:, :, :lc]), gate_c) if with_ctx else None
    return o_lat, o_ctx


def diff_core(q, k, v, lam):
    s = jnp.einsum('bhmqd,bhmkd->bhmqk', q, k, preferred_element_type=F32) * (q.shape[-1] ** -0.5)
    p = jax.nn.softmax(s, axis=-1)
    p = p[:, :, 0] - lam * p[:, :, 1]
    return jnp.einsum('bhqk,bhkd->bhqd', p.astype(v.dtype), v)


def diff_branch(p_lat, p_ctx, lam_vecs, lam_init, norm_g, rope, with_ctx):
    lv = lam_vecs.astype(F32)
    lam = jnp.exp(jnp.sum(lv[0] * lv[1])) - jnp.exp(jnp.sum(lv[2] * lv[3])) + lam_init

    def qkv(p):
        q, k, v = p
        b_, n, _ = q.shape
        maps = lambda t: t.reshape(b_, n, H_B, 2, DH_B).transpose(0, 2, 3, 1, 4)
        return maps(q), maps(k), to_heads(v, H_B)

    ql, kl, vl = qkv(p_lat)
    ql, kl = apply_axial_rope(ql, rope), apply_axial_rope(kl, rope)
    qc, kc, vc = qkv(p_ctx)
    k_all = jnp.concatenate([kc, kl], axis=3)
    v_all = jnp.concatenate([vc, vl], axis=2)
    finish = lambda o: from_heads(rms_norm(o, norm_g) * (1.0 - lam_init))
    o_lat = finish(sweep_query_blocks(lambda qb: diff_core(qb, k_all, v_all, lam), ql))
    o_ctx = finish(diff_core(qc, kc, vc, lam)) if with_ctx else None
    return o_lat, o_ctx


def gqa_core(q, k, v):
    s = jnp.einsum('bhgqd,bhkd->bhgqk', q, k, preferred_element_type=F32) * (q.shape[-1] ** -0.5)
    p = jax.nn.softmax(s, axis=-1).astype(v.dtype)
    return jnp.einsum('bhgqk,bhkd->bhgqd', p, v)


def gqa_branch(p_lat, p_ctx, qk_g, rope, with_ctx):
    grp = HQ_C // HKV_C

    def qkv(p):
        q, k, v = p
        b_, n, _ = q.shape
        q = rms_norm(q.reshape(b_, n, HQ_C, HEAD_DIM), qk_g[0]).reshape(b_, n, HKV_C, grp, HEAD_DIM)
        k = rms_norm(k.reshape(b_, n, HKV_C, HEAD_DIM), qk_g[1])
        return q.transpose(0, 2, 3, 1, 4), k.transpose(0, 2, 1, 3), to_heads(v, HKV_C)

    ql, kl, vl = qkv(p_lat)
    ql, kl = apply_axial_rope(ql, rope), apply_axial_rope(kl, rope)
    qc, kc, vc = qkv(p_ctx)
    k_all = jnp.concatenate([kc, kl], axis=2)
    v_all = jnp.concatenate([vc, vl], axis=2)
    o_lat = from_groups(sweep_query_blocks(lambda qb: gqa_core(qb, k_all, v_all), ql))
    o_ctx = from_groups(gqa_core(qc, kc, vc)) if with_ctx else None
    return o_lat, o_ctx


def sink_attention(q, k, v, sink):
    s = jnp.einsum('bhgqd,bhkd->bhgqk', q, k, preferred_element_type=F32) * (q.shape[-1] ** -0.5)
    s_sink = jnp.broadcast_to(sink.astype(F32)[None, :, :, None, None], s.shape[:-1] + (1,))
    p = jax.nn.softmax(jnp.concatenate([s, s_sink], axis=-1), axis=-1)[..., :-1]
    return jnp.einsum('bhgqk,bhkd->bhgqd', p.astype(v.dtype), v)


def window_attention_latent(q, k_lat, v_lat, k_ctx, v_ctx, sink):
    b_, hk, grp, n, d = q.shape
    nb = n // BLOCK
    lc = k_ctx.shape[2]

    def band(t):
        tp = jnp.pad(t, ((0, 0), (0, 0), (BLOCK, BLOCK), (0, 0))).reshape(b_, hk, nb + 2, BLOCK, d)
        return jnp.concatenate([tp[:, :, :-2], tp[:, :, 1:-1], tp[:, :, 2:]], axis=3)

    kb, vb = band(k_lat), band(v_lat)
    qb = q.reshape(b_, hk, grp, nb, BLOCK, d)
    scale = d ** -0.5
    s_loc = jnp.einsum('bhgnqd,bhnkd->bhgnqk', qb, kb, preferred_element_type=F32) * scale
    r = jnp.arange(BLOCK)[:, None]
    j = jnp.arange(3 * BLOCK)[None, :]
    s_abs = (jnp.arange(nb)[:, None, None] - 1) * BLOCK + j
    valid = (jnp.abs(BLOCK + r - j) <= WINDOW) & (s_abs >= 0) & (s_abs < n)
    s_loc = jnp.where(valid, s_loc, -jnp.inf)
    s_ctx = jnp.einsum('bhgnqd,bhkd->bhgnqk', qb, k_ctx, preferred_element_type=F32) * scale
    s_sink = jnp.broadcast_to(sink.astype(F32)[None, :, :, None, None, None], s_ctx.shape[:-1] + (1,))
    p = jax.nn.softmax(jnp.concatenate([s_ctx, s_loc, s_sink], axis=-1), axis=-1).astype(v_lat.dtype)
    o = (jnp.einsum('bhgnqk,bhkd->bhgnqd', p[..., :lc], v_ctx)
         + jnp.einsum('bhgnqk,bhnkd->bhgnqd', p[..., lc:lc + 3 * BLOCK], vb))
    return o.reshape(b_, hk, grp, n, d)


def window_branch(p_lat, p_ctx, sink, rope, with_ctx):
    grp = HQ_D // HKV_D
    sink = sink.reshape(HKV_D, grp)
    qkv = lambda p: (to_groups(p[0], HKV_D, grp), to_heads(p[1], HKV_D), to_heads(p[2], HKV_D))
    ql, kl, vl = qkv(p_lat)
    ql, kl = apply_axial_rope(ql, rope), apply_axial_rope(kl, rope)
    qc, kc, vc = qkv(p_ctx)
    o_lat = from_groups(window_attention_latent(ql, kl, vl, kc, vc, sink))
    o_ctx = from_groups(sink_attention(qc, kc, vc, sink)) if with_ctx else None
    return o_lat, o_ctx


def merge_branches(outs, gate_logits, w_br, w_out):
    b_, n, _ = gate_logits.shape
    gl = gate_logits.reshape(b_, n, N_BRANCH, D_MODEL)
    m = sum(jax.nn.sigmoid(gl[:, :, i]) * (outs[i] @ w_br[i]) for i in range(N_BRANCH))
    return m @ w_out


def token_mixers(h_lat, h_ctx, w_in, conv_w, a_log, dt_bias, gdn_g, lam_vecs, lam_init, diff_g,
                 qk_g, sink, w_br, w_out, rope_hd, rope_b, with_ctx):
    pl = split_cols(h_lat @ w_in)
    pc = split_cols(h_ctx @ w_in)
    a_lat, a_ctx = gdn_branch(pl[0:8], pc[0:8], conv_w, a_log, dt_bias, gdn_g, with_ctx)
    b_lat, b_ctx = diff_branch(pl[8:11], pc[8:11], lam_vecs, lam_init, diff_g, rope_b, with_ctx)
    g_lat, g_ctx = gqa_branch(pl[11:14], pc[11:14], qk_g, rope_hd, with_ctx)
    w_lat, w_ctx = window_branch(pl[14:17], pc[14:17], sink, rope_hd, with_ctx)
    y_lat = merge_branches((a_lat, b_lat, g_lat, w_lat), pl[17], w_br, w_out)
    y_ctx = merge_branches((a_ctx, b_ctx, g_ctx, w_ctx), pc[17], w_br, w_out) if with_ctx else None
    return y_lat, y_ctx


def expert_choice_ffn(h, w_router, w_gate, w_up, w_down):
    b_, n, dm = h.shape
    cap = max(1, CAPACITY_FACTOR * n // N_EXPERTS)
    aff = jax.nn.softmax(jnp.einsum('bnd,de->bne', h, w_router, preferred_element_type=F32), axis=-1)
    gval, idx = lax.top_k(jnp.swapaxes(aff, 1, 2), cap)
    xg = jax.vmap(lambda hb, ib: hb[ib])(h, idx)
    a = jnp.einsum('becd,edf->becf', xg, w_gate)
    u = jnp.einsum('becd,edf->becf', xg, w_up)
    y = jnp.einsum('becf,efd->becd', jax.nn.silu(a) * u, w_down)
    y = y * gval[..., None].astype(y.dtype)
    return jax.vmap(lambda yb, ib: jnp.zeros((n, dm), yb.dtype).at[ib.reshape(-1)].add(yb.reshape(-1, dm)))(y, idx)


def setup_inputs(seed: int = 0) -> dict:
    key = jax.random.key(seed)
    ks = jax.random.split(key, 25)
    D = D_MODEL
    beta_dn = (8.0 * DEPTH) ** -0.25
    nrm = lambda k, shape, s: jax.random.normal(k, shape, F32) * s
    gain = lambda k, shape: 1.0 + 0.02 * jax.random.normal(k, shape, F32)
    dt = jnp.exp(jax.random.uniform(ks[9], (DEPTH, 2, H_A), F32, math.log(1e-3), math.log(1e-1)))
    return {
        'x': nrm(ks[0], (BATCH, SEQ, D), 1.0),
        'c': nrm(ks[1], (BATCH, D), 1.0),
        'ctx': nrm(ks[2], (BATCH, CTX_LEN, D), 1.0),
        'c_ctx': nrm(ks[3], (D,), 1.0),
        'w_mod': nrm(ks[4], (DEPTH, D, 6 * D), 0.5 * D ** -0.5),
        'b_mod': nrm(ks[5], (DEPTH, 6 * D), 0.02),
        'w_in': nrm(ks[6], (DEPTH, D, D_IN), D ** -0.5),
        'conv_a': nrm(ks[7], (DEPTH, CONV_K, 3 * H_A * DK_A), CONV_K ** -0.5),
        'a_log': jnp.log(jax.random.uniform(ks[8], (DEPTH, 2, H_A), F32, 1.0, 16.0)),
        'dt_bias': dt + jnp.log(-jnp.expm1(-dt)),
        'gdn_norm': gain(ks[10], (DEPTH, DV_A)),
        'diff_lambda': nrm(ks[11], (DEPTH, 4, DH_B), 0.1),
        'diff_norm': gain(ks[12], (DEPTH, DV_B)),
        'qk_norm_c': gain(ks[13], (DEPTH, 2, HEAD_DIM)),
        'sink_d': nrm(ks[14], (DEPTH, HQ_D), 0.5),
        'w_br': nrm(ks[15], (DEPTH, N_BRANCH, BRANCH_W, D), beta_dn * BRANCH_W ** -0.5),
        'w_out': nrm(ks[16], (DEPTH, D, D), beta_dn * D ** -0.5),
        'ln1_g': gain(ks[17], (DEPTH, D)),
        'ln1_b': nrm(ks[18], (DEPTH, D), 0.02),
        'w_router': nrm(ks[19], (DEPTH, D, N_EXPERTS), D ** -0.5),
        'w_gate_e': nrm(ks[20], (DEPTH, N_EXPERTS, D, D_EXPERT), D ** -0.5),
        'w_up_e': nrm(ks[21], (DEPTH, N_EXPERTS, D, D_EXPERT), D ** -0.5),
        'w_down_e': nrm(ks[22], (DEPTH, N_EXPERTS, D_EXPERT, D), beta_dn * D_EXPERT ** -0.5),
        'ln2_g': gain(ks[23], (DEPTH, D)),
        'ln2_b': nrm(ks[24], (DEPTH, D), 0.02),
    }


def reference(x, c, ctx, c_ctx, w_mod, b_mod, w_in, conv_a, a_log, dt_bias, gdn_norm,
              diff_lambda, diff_norm, qk_norm_c, sink_d, w_br, w_out, ln1_g, ln1_b,
              w_router, w_gate_e, w_up_e, w_down_e, ln2_g, ln2_b):
    alpha = (2.0 * DEPTH) ** 0.25
    n = x.shape[1]
    rope_hd = axial_rope_tables(n, HEAD_DIM)
    rope_b = axial_rope_tables(n, DH_B)
    x_lat, x_ctx = x, ctx
    for l in range(DEPTH):
        last = l == DEPTH - 1
        lam_init = 0.8 - 0.6 * math.exp(-0.3 * l)
        mod = jax.nn.silu(c) @ w_mod[l] + b_mod[l]
        mod_c = jax.nn.silu(c_ctx) @ w_mod[l] + b_mod[l]
        sh1, sc1, g1, sh2, sc2, g2 = jnp.split(mod[:, None, :], 6, axis=-1)
        csh1, csc1, cg1, csh2, csc2, cg2 = jnp.split(mod_c, 6)
        y_lat, y_ctx = token_mixers(x_lat * (1 + sc1) + sh1, x_ctx * (1 + csc1) + csh1,
                                    w_in[l], conv_a[l], a_log[l], dt_bias[l], gdn_norm[l],
                                    diff_lambda[l], lam_init, diff_norm[l], qk_norm_c[l], sink_d[l],
                                    w_br[l], w_out[l], rope_hd, rope_b, not last)
        x_lat = layer_norm(alpha * x_lat + g1 * y_lat, ln1_g[l], ln1_b[l])
        f_lat = expert_choice_ffn(x_lat * (1 + sc2) + sh2, w_router[l], w_gate_e[l], w_up_e[l], w_down_e[l])
        x_lat = layer_norm(alpha * x_lat + g2 * f_lat, ln2_g[l], ln2_b[l])
        if not last:
            x_ctx = layer_norm(alpha * x_ctx + cg1 * y_ctx, ln1_g[l], ln1_b[l])
            f_ctx = expert_choice_ffn(x_ctx * (1 + csc2) + csh2, w_router[l], w_gate_e[l], w_up_e[l], w_down_e[l])
            x_ctx = layer_norm(alpha * x_ctx + cg2 * f_ctx, ln2_g[l], ln2_b[l])
    return x_lat
```

```python
import math
from contextlib import ExitStack
import numpy as np
import concourse.bass as bass
import concourse.mybir as mybir
from concourse.bass_utils import run_bass_kernel_spmd

F32 = mybir.dt.float32
BF16 = mybir.dt.bfloat16
FP16 = mybir.dt.float16
AF = mybir.ActivationFunctionType
ALU = mybir.AluOpType
AX = mybir.AxisListType

D = 1024
NLAT = 4096
NCTX = 256
T = NLAT + NCTX
NT = T // 128
DEPTH = 4
D_IN = 6928
EPS = 1e-6
ALPHA = (2.0 * DEPTH) ** 0.25
NEG = -30000.0


class Sched:
    EPOCH = 16000
    ENGS = ("pe", "act", "dve", "pool", "sp")

    def __init__(self, nc):
        self.nc = nc
        self.ops = []

    def add(self, eng, fn, reads=(), writes=(), dma=False):
        writes = tuple(writes) + tuple(k for k in reads if isinstance(k, str) and k.startswith("ps") and k[2:].isdigit())
        self.ops.append([eng, fn, tuple(reads), tuple(writes), dma])
        return len(self.ops) - 1

    def barrier(self):
        self.ops.append(["sp", "BAR", (), (), False])

    def emit(self, stack, dma_slots=None):
        nc = self.nc
        ops = self.ops
        n = len(ops)
        dma_slots = dma_slots or {"sp": 24, "pool": 24, "act": 8}
        last_w = {}
        readers = {}
        deps = [None] * n
        needed = [False] * n
        last_eng_op = {e: None for e in self.ENGS}
        last_slot_op = {e: {} for e in dma_slots}
        slot_rr = {e: 0 for e in dma_slots}
        slot_of = [None] * n
        last_bar = None
        seen_since_bar = {e: True for e in self.ENGS}
        for i, (eng, fn, R, W, dma) in enumerate(ops):
            d = set()
            if fn == "BAR":
                for e in self.ENGS:
                    if last_eng_op[e] is not None:
                        d.add(last_eng_op[e])
                for e in dma_slots:
                    for s, j in last_slot_op[e].items():
                        d.add(j)
                last_bar = i
                seen_since_bar = {e: False for e in self.ENGS}
            else:
                for k in R:
                    if k in last_w:
                        d.add(last_w[k])
                for k in W:
                    if k in last_w:
                        d.add(last_w[k])
                    for r in readers.get(k, ()):
                        d.add(r)
                if last_bar is not None and not seen_since_bar[eng]:
                    d.add(last_bar)
                seen_since_bar[eng] = True
            d.discard(i)
            dl = []
            for j in d:
                je, jf, _, _, jd = ops[j]
                if (not jd) and je == "pe" and eng == "pe" and not dma and jf != "BAR" and fn != "BAR":
                    continue
                dl.append(j)
                needed[j] = True
            deps[i] = sorted(dl)
            for k in R:
                readers.setdefault(k, []).append(i)
            for k in W:
                last_w[k] = i
                readers[k] = []
            if dma:
                s = slot_rr[eng]
                slot_rr[eng] = (s + 1) % dma_slots[eng]
                slot_of[i] = s
                last_slot_op[eng][s] = i
            else:
                last_eng_op[eng] = i
        sig = [None] * n
        eng_count = {e: 0 for e in self.ENGS}
        eng_sems = {e: [] for e in self.ENGS}
        slot_sems = {}
        slot_state = {}
        for e, cnt in dma_slots.items():
            slot_sems[e] = [stack.enter_context(nc.semaphore(f"d_{e}_{s}")) for s in range(cnt)]
            slot_state[e] = [0] * cnt
        prewait = [None] * n
        for i, (eng, fn, R, W, dma) in enumerate(ops):
            if dma:
                s = slot_of[i]
                prev = slot_state[eng][s]
                if prev > 0:
                    prewait[i] = (slot_sems[eng][s], prev)
                slot_state[eng][s] = prev + 16
                sig[i] = ("dma", slot_sems[eng][s], prev + 16)
            elif needed[i]:
                c = eng_count[eng]
                ep = c // self.EPOCH
                while len(eng_sems[eng]) <= ep:
                    eng_sems[eng].append(stack.enter_context(nc.semaphore(f"s_{eng}_{len(eng_sems[eng])}")))
                sig[i] = ("eng", eng_sems[eng][ep], (c % self.EPOCH) + 1, c)
                eng_count[eng] = c + 1
        plan = {e: [] for e in self.ENGS}
        waited_eng = {e: {f: -1 for f in self.ENGS} for e in self.ENGS}
        waited_dma = {e: {} for e in self.ENGS}
        for i, (eng, fn, R, W, dma) in enumerate(ops):
            waits = []
            if prewait[i] is not None:
                waits.append(prewait[i])
            for j in deps[i]:
                sj = sig[j]
                if sj[0] == "dma":
                    key = id(sj[1])
                    if waited_dma[eng].get(key, 0) >= sj[2]:
                        continue
                    waited_dma[eng][key] = sj[2]
                    waits.append((sj[1], sj[2]))
                else:
                    je = ops[j][0]
                    if waited_eng[eng][je] >= sj[3]:
                        continue
                    waited_eng[eng][je] = sj[3]
                    waits.append((sj[1], sj[2]))
            plan[eng].append((i, waits))
        self.n_waits = sum(len(w) for e in plan for _, w in plan[e])

        def run_engine(e_obj, ename):
            for i, waits in plan[ename]:
                for (sem, val) in waits:
                    e_obj.wait_ge(sem, val)
                fn = ops[i][1]
                if fn is None:
                    continue
                inst = e_obj.nop() if fn == "BAR" else fn(e_obj)
                sj = sig[i]
                if sj is not None:
                    inst.then_inc(sj[1], 16 if sj[0] == "dma" else 1)

        with nc.Block() as block:
            @block.tensor
            def _(e):
                run_engine(e, "pe")

            @block.scalar
            def _(e):
                run_engine(e, "act")

            @block.vector
            def _(e):
                run_engine(e, "dve")

            @block.gpsimd
            def _(e):
                run_engine(e, "pool")

            @block.sync
            def _(e):
                run_engine(e, "sp")


C_QA, C_KA, C_VA, C_G16, C_GATEA = 0, 256, 512, 768, 784
C_QB, C_KB, C_VB = 1040, 1296, 1552
C_QC, C_KC, C_VC = 1808, 2064, 2192
C_QD, C_KD, C_VD = 2320, 2576, 2704
C_GL = 2832

GROUPS = [(0, 256)] + [(256 + 512 * g, 512) for g in range(8)]


def host_consts():
    c = {}
    c["ident"] = np.eye(128, dtype=np.float32)
    pos = np.arange(NLAT)
    row = (pos // 64).astype(np.float32)
    col = (pos % 64).astype(np.float32)

    def tabs(dim, reps):
        nf = dim // 4
        inv = (np.float32(10000.0) ** (-np.arange(nf, dtype=np.float32) / np.float32(nf))).astype(np.float32)
        ar = row[None, :] * inv[:, None]
        ac = col[None, :] * inv[:, None]
        cos = np.concatenate([np.cos(ar), np.cos(ar), np.cos(ac), np.cos(ac)], 0).astype(np.float32)
        sin = np.concatenate([np.sin(ar), np.sin(ar), np.sin(ac), np.sin(ac)], 0).astype(np.float32)
        return np.tile(cos, (reps, 1)), np.tile(sin, (reps, 1))

    c["cos64"], c["sin64"] = tabs(64, 2)
    c["cos32"], c["sin32"] = tabs(32, 4)

    def rotm(dim):
        q = dim // 4
        m = np.zeros((128, 128), np.float32)
        for b0 in range(0, 128, dim):
            for i in range(q):
                m[b0 + q + i, b0 + i] = -1.0
                m[b0 + i, b0 + q + i] = 1.0
                m[b0 + 3 * q + i, b0 + 2 * q + i] = -1.0
                m[b0 + 2 * q + i, b0 + 3 * q + i] = 1.0
        return m

    c["rot64"] = rotm(64)
    c["rot32"] = rotm(32)
    bo = np.zeros((128, 128), np.float32)
    bo[0:64, 0:64] = 1.0
    bo[64:128, 64:128] = 1.0
    c["bones64"] = bo
    r = np.arange(128)
    c["tri_ge"] = (r[:, None] >= r[None, :]).astype(np.float32)
    c["tri_le"] = (r[:, None] <= r[None, :]).astype(np.float32)
    sel = np.zeros((128, 64), np.float32)
    sel[64, :] = 1.0
    c["sel64"] = sel
    cc, ss = np.meshgrid(np.arange(64), np.arange(64), indexing="ij")
    def m8(fwd_ok, bwd_ok):
        m = np.zeros((64, 8, 64), np.float32)
        for p in range(8):
            ok = fwd_ok if p < 4 else bwd_ok
            m[:, p, :] = np.where(ok, 0.0, NEG)
        return m.reshape(64, 512)
    c["mask_a"] = m8(ss < cc, ss > cc)
    c["mask_at"] = m8(cc < ss, cc > ss)
    c["mask_pt"] = m8(cc <= ss, cc >= ss)
    c["eye8"] = np.tile(np.eye(64, dtype=np.float32)[:, None, :], (1, 8, 1)).reshape(64, 512)
    cm = np.ones((4, T), np.float32)
    cm[:, ::64] = 0.0
    c["chunkmask"] = cm
    c["iota512"] = np.tile(np.arange(512, dtype=np.float32)[None, :], (128, 1))
    c["iota4"] = (r[:, None] + 128 * np.arange(4)[None, :]).astype(np.float32)
    return c


CONST_SHAPES = {"ident": [128, 128], "cos64": [128, NLAT], "sin64": [128, NLAT], "cos32": [128, NLAT],
                "sin32": [128, NLAT], "rot64": [128, 128], "rot32": [128, 128], "bones64": [128, 128],
                "tri_ge": [128, 128], "tri_le": [128, 128], "sel64": [128, 64],
                "iota512": [128, 512], "iota4": [128, 4], "mask_a": [64, 512], "mask_at": [64, 512],
                "mask_pt": [64, 512], "eye8": [64, 512], "chunkmask": [4, T]}


class Builder:
    def __init__(self, nlayers, debug=()):
        self.nl = nlayers
        self.debug = set(debug)
        nc = self.nc = bass.Bass("TRN2", target_bir_lowering=False)
        self.S = Sched(nc)
        L = nlayers
        di = lambda name, shape, dt=F32: nc.dram_tensor(name, shape, dt, kind="ExternalInput").ap()
        self.x = di("x", [NLAT, D])
        self.ctx = di("ctx", [NCTX, D])
        self.cT = di("cT", [128, 16])
        self.w_mod = di("w_mod", [L, D, 6 * D])
        self.b_modT = di("b_modT", [L, 128, 48])
        self.w_in = di("w_in", [L, D, D_IN])
        self.conv_aT = di("conv_aT", [L, 128, 6, 5])
        self.a_log = di("a_log", [L, 8])
        self.dt_bias = di("dt_bias", [L, 8])
        self.gdn_norm = di("gdn_norm", [L, 64])
        self.diff_lambda = di("diff_lambda", [L, 128])
        self.diff_norm = di("diff_norm", [L, 64])
        self.qk_norm_c = di("qk_norm_c", [L, 128])
        self.sink_d = di("sink_d", [L, 4])
        self.w_br = di("w_br", [L, 4, 256, D])
        self.w_out = di("w_out", [L, D, D])
        self.ln1_g = di("ln1_g", [L, D])
        self.ln1_b = di("ln1_b", [L, D])
        self.w_router = di("w_router", [L, D, 16])
        self.w_gate_e = di("w_gate_e", [L, 16, D, D])
        self.w_up_e = di("w_up_e", [L, 16, D, D])
        self.w_down_e = di("w_down_e", [L, 16, D, D])
        self.ln2_g = di("ln2_g", [L, D])
        self.ln2_b = di("ln2_b", [L, D])
        self.cst = {k: di("c_" + k, s) for k, s in CONST_SHAPES.items()}
        self.out = nc.dram_tensor("out", [NLAT, D], F32, kind="ExternalOutput").ap()
        self.dbg_out = {}

    def uniq(self, name):
        self._uid = getattr(self, "_uid", 0) + 1
        return f"{name}_u{self._uid}"

    def scratch(self, name, shape, dt=F32):
        if name in self.debug:
            ap = self.nc.dram_tensor(name, shape, dt, kind="ExternalOutput").ap()
            self.dbg_out[name] = ap
            return ap
        return self.nc.dram_tensor(name, shape, dt).ap()

    def build(self):
        nc, S = self.nc, self.S
        with ExitStack() as top:
            self.top = top
            self.ps = [top.enter_context(nc.psum_tensor(f"ps{i}", [128, 512], F32)) for i in range(8)]
            sbt = lambda name, shape, dt=F32: top.enter_context(nc.sbuf_tensor(name, shape, dt))
            self.ident = sbt("ident", [128, 128])
            self.identb = sbt("identb", [128, 128], BF16)
            S.add("sp", lambda e: e.dma_start(out=self.ident[:], in_=self.cst["ident"][:, :]), writes=["ident"], dma=True)
            S.add("pool", lambda e: e.dma_start(out=self.identb[:], in_=self.cst["ident"][:, :]), writes=["identb"], dma=True)
            self.modT = sbt("modT", [128, 48, 2])
            self.modP = sbt("modP", [128, 2, 8, 2])
            self.xres = self.scratch("xres", [T, D])
            self.hT = self.scratch("hT", [8, 128, T], BF16)
            self.modrow = self.scratch("modrow", [96, 128])
            self.aqkv = self.scratch("aqkv", [768, T])
            self.ag16 = self.scratch("ag16", [16, T])
            self.gateA = self.scratch("gateA", [T, 256], BF16)
            self.vtok = self.scratch("vtok", [T, 512], BF16)
            self.qkB = self.scratch("qkB", [6, 128, T], BF16)
            self.qkC = self.scratch("qkC", [3, 128, T], BF16)
            self.qkD = self.scratch("qkD", [3, 128, T], BF16)
            self.oT = self.scratch("oT", [8, 128, T], BF16)
            self.mTd = self.scratch("mTd", [4, 128, T], BF16)
            self.h2tok = self.scratch("h2tok", [T, D], BF16)
            self.aff = self.scratch("aff", [T, 16])
            self.Rd = self.scratch("Rd", [3, 2, 8, T])
            self.TSd = self.scratch("TSd", [40, T])
            self.qkhd = self.scratch("qkhd", [2, 4, 64, T])
            self.kvtok = self.scratch("kvtok", [T, 512])
            self.ofb = self.scratch("ofb", [2, T, 256])
            self.ptokd = self.scratch("ptokd", [128, NT, 16])
            self.posd = self.scratch("posd", [NT, 16, 128], FP16)
            self.gvd = self.scratch("gvd", [NT, 16, 128], BF16)
            self.ygL = self.scratch("ygL", [16, 512, D], BF16)
            self.ygC = self.scratch("ygC", [16, 32, D], BF16)
            self.f0d = self.scratch("f0d", [T, 512])
            if "inject_a" in self.debug:
                self.a_inj = nc.dram_tensor("a_inj", [2, 128, T], F32, kind="ExternalInput").ap()
            S.add("sp", lambda e: e.dma_start(out=self.xres[0:NCTX, :], in_=self.ctx[:, :]), writes=["xres"], dma=True)
            for i in range(4):
                S.add("sp", lambda e, i=i: e.dma_start(out=self.xres[NCTX + i * 1024:NCTX + (i + 1) * 1024, :], in_=self.x[i * 1024:(i + 1) * 1024, :]), writes=["xres"], dma=True)
            for l in range(self.nl):
                self.layer(l)
            for i in range(4):
                S.add("sp", lambda e, i=i: e.dma_start(out=self.out[i * 1024:(i + 1) * 1024, :], in_=self.xres[NCTX + i * 1024:NCTX + (i + 1) * 1024, :]), reads=["xres"], writes=["out"], dma=True)
            S.add("sp", None, reads=["out"] + [k for k in self.dbg_out])
            S.emit(top)
        return nc

    def layer(self, l):
        self.phase_mod(l)
        self.S.barrier()
        self.phase_inproj(l)
        self.S.barrier()
        if "skip_attn" not in self.debug:
            self.phase_attn(l)
            self.S.barrier()
        if "skip_gdn" not in self.debug:
            self.phase_gdn(l)
        if "inject_a" in self.debug:
            with ExitStack() as st:
                tmpa = st.enter_context(self.nc.sbuf_tensor(self.uniq("tmpa"), [128, 2, T], BF16))
                self.S.add("pool", lambda e: e.dma_start(out=tmpa[:], in_=self.a_inj.rearrange("c p t -> p c t")), writes=["tmpa"], dma=True)
                self.S.add("sp", lambda e: e.dma_start(out=self.oT[0:2].rearrange("c p t -> p c t"), in_=tmpa[:]), reads=["tmpa"], writes=["oT"], dma=True)
            self.S.barrier()
        if "skip_merge" not in self.debug:
            self.phase_merge(l)
        if "skip_moe" not in self.debug:
            self.phase_moe(l)

    def phase_mod(self, l):
        nc, S = self.nc, self.S
        ps = self.ps
        with ExitStack() as st:
            sbt = lambda name, shape, dt=F32: st.enter_context(nc.sbuf_tensor(self.uniq(name), shape, dt))
            cs = sbt("cs", [128, 16])
            sc = sbt("silu_c", [128, 8, 2])
            bm = sbt("bm", [128, 48])
            wm = [sbt(f"wm{i}", [128, 8, 512]) for i in range(2)]
            mrow = sbt("mrow", [96, 128])
            S.add("sp", lambda e: e.dma_start(out=cs[:], in_=self.cT[:, :]), writes=["cs"], dma=True)
            S.add("sp", lambda e: e.dma_start(out=bm[:], in_=self.b_modT[l]), writes=["bm"], dma=True)
            S.add("act", lambda e: e.activation(out=sc[:].rearrange("p k s -> p s k"), in_=cs[:].rearrange("p (s k) -> p s k", s=2), func=AF.Silu), reads=["cs"], writes=["sc"])
            wv = self.w_mod[l].rearrange("(kc p) n -> p kc n", p=128)
            for pc in range(12):
                w = wm[pc % 2]
                wk = f"wm{pc % 2}"
                S.add("sp" if pc % 2 == 0 else "act", lambda e, w=w, pc=pc: e.dma_start(out=w[:], in_=wv[:, :, pc * 512:(pc + 1) * 512]), writes=[wk], dma=True)
                for mc in range(4):
                    idx = pc * 4 + mc
                    for kc in range(8):
                        S.add("pe", lambda e, w=w, mc=mc, kc=kc, idx=idx: e.matmul(ps[0][:, idx * 2:idx * 2 + 2], lhsT=w[:, kc, mc * 128:(mc + 1) * 128], rhs=sc[:, kc, :], start=(kc == 0), stop=(kc == 7)), reads=[wk, "sc"], writes=["ps0"])
            S.add("dve", lambda e: e.tensor_tensor(out=self.modT[:], in0=ps[0][:, 0:96].rearrange("p (i s) -> p i s", s=2), in1=bm[:].unsqueeze(2).broadcast_to([128, 48, 2]), op=ALU.add), reads=["ps0", "bm"], writes=["modT"])
            S.add("dve", lambda e: e.tensor_scalar(out=self.modP[:, 0], in0=self.modT[:, 8:16, :], scalar1=1.0, scalar2=None, op0=ALU.add), reads=["modT"], writes=["modP"])
            S.add("dve", lambda e: e.tensor_scalar(out=self.modP[:, 1], in0=self.modT[:, 32:40, :], scalar1=1.0, scalar2=None, op0=ALU.add), reads=["modT"], writes=["modP"])
            S.add("pe", lambda e: e.transpose(ps[1][0:96, 0:128], self.modT[:].rearrange("p i s -> p (i s)"), self.ident[:]), reads=["modT", "ident"], writes=["ps1"])
            S.add("dve", lambda e: e.tensor_copy(out=mrow[:], in_=ps[1][0:96, 0:128]), reads=["ps1"], writes=["mrow"])
            S.add("sp", lambda e: e.dma_start(out=self.modrow[:, :], in_=mrow[:]), reads=["mrow"], writes=["modrow"], dma=True)

    def bcast_mod(self, eng, tile, kind, s, key):
        src = self.modrow.rearrange("(i s) p -> s i p", s=2)[s, kind * 8:(kind + 1) * 8, :].partition_broadcast(128)
        self.S.add(eng, lambda e: e.dma_start(out=tile[:].rearrange("p (i q) -> p i q", q=128), in_=src), reads=["modrow"], writes=[key], dma=True)

    def phase_inproj(self, l):
        nc, S = self.nc, self.S
        ps = self.ps
        with ExitStack() as st:
            sbt = lambda name, shape, dt=F32: st.enter_context(nc.sbuf_tensor(self.uniq(name), shape, dt))
            fm = []
            for j in range(6):
                fm.append(("A%d" % j, [(C_QA + 128 * j, 128)]))
            fm.append(("G16", [(C_G16, 16)]))
            for j, (o, n) in enumerate([(0, 96), (96, 96), (192, 64)]):
                fm.append(("QB%d" % j, [(C_QB + o, n)]))
            for j, (o, n) in enumerate([(0, 96), (96, 96), (192, 64)]):
                fm.append(("KB%d" % j, [(C_KB + o, n)]))
            for g in range(2):
                fm.append(("QC%d" % g, [(C_QC + g * 64, 64), (C_QC + (2 + g) * 64, 64)]))
            fm.append(("KC", [(C_KC, 128)]))
            for g in range(2):
                fm.append(("QD%d" % g, [(C_QD + g * 64, 64), (C_QD + (2 + g) * 64, 64)]))
            fm.append(("KD", [(C_KD, 128)]))
            nfm = len(fm)
            tm_cols = [(C_VB, 256), (C_VC, 128), (C_VD, 128), (C_GATEA, 256)]
            wfm = sbt("wfm", [128, 8, nfm * 128], BF16)
            wtm = sbt("wtm", [128, 8, 768], BF16)
            wv = self.w_in[l].rearrange("(kc p) n -> p kc n", p=128)
            for ci, (nm, segs) in enumerate(fm):
                o = 0
                for (c0, ncol) in segs:
                    S.add("pool", lambda e, ci=ci, o=o, c0=c0, ncol=ncol: e.dma_start(out=wfm[:, :, ci * 128 + o:ci * 128 + o + ncol], in_=wv[:, :, c0:c0 + ncol]), writes=["wfm"], dma=True)
                    o += ncol
            o = 0
            for (c0, ncol) in tm_cols:
                S.add("pool", lambda e, o=o, c0=c0, ncol=ncol: e.dma_start(out=wtm[:, :, o:o + ncol], in_=wv[:, :, c0:c0 + ncol]), writes=["wtm"], dma=True)
                o += ncol
            rot64 = sbt("rot64", [128, 128])
            rot32 = sbt("rot32", [128, 128])
            bones = sbt("bones", [128, 128])
            qkg = sbt("qkg", [128, 2])
            S.add("sp", lambda e: e.dma_start(out=rot64[:], in_=self.cst["rot64"][:, :]), writes=["rot64"], dma=True)
            S.add("sp", lambda e: e.dma_start(out=rot32[:], in_=self.cst["rot32"][:, :]), writes=["rot32"], dma=True)
            S.add("sp", lambda e: e.dma_start(out=bones[:], in_=self.cst["bones64"][:, :]), writes=["bones"], dma=True)
            qkv = self.qk_norm_c[l].rearrange("(g d) -> d g", g=2)
            for h in range(2):
                S.add("sp", lambda e, h=h: e.dma_start(out=qkg[h * 64:(h + 1) * 64, :], in_=qkv, allow_slow_non_contiguous=True), writes=["qkg"], dma=True)
            xt = [sbt(f"xt{i}", [128, 1024]) for i in range(2)]
            hTs = [sbt(f"hTs{i}", [128, 8, 512], BF16) for i in range(2)]
            tabs = [sbt(f"tabs{i}", [128, 4, 512]) for i in range(2)]
            ev = [sbt(f"ev{i}", [128, 512]) for i in range(3)]
            ev2 = [sbt(f"evb{i}", [128, 512]) for i in range(2)]
            sq = sbt("sq", [128, 512])
            rstd = sbt("rstd", [128, 512])
            ob = [sbt(f"ob{i}", [128, 512], BF16) for i in range(3)]
            otm = [sbt(f"otm{i}", [128, 768], BF16) for i in range(2)]
            cnt = {"ev": 0, "ob": 0, "ps": 0, "evb": 0}
            tile_i = 0
            for gi, (t0, W) in enumerate(GROUPS):
                s = 1 if gi == 0 else 0
                hb = hTs[gi % 2]
                hk = f"hTs{gi % 2}"
                ntile = W // 128
                for j in range(ntile):
                    xb = xt[tile_i % 2]
                    xk = f"xt{tile_i % 2}"
                    tile_i += 1
                    tt = t0 + j * 128
                    S.add("sp", lambda e, xb=xb, tt=tt: e.dma_start(out=xb[:], in_=self.xres[tt:tt + 128, :]), reads=["xres"], writes=[xk], dma=True)
                    for half in range(2):
                        pb = 6 + half
                        for q in range(4):
                            fc = half * 4 + q
                            S.add("pe", lambda e, xb=xb, fc=fc, pb=pb, q=q: e.transpose(ps[pb][:, q * 128:(q + 1) * 128], xb[:, fc * 128:(fc + 1) * 128], self.ident[:]), reads=[xk, "ident"], writes=[f"ps{pb}"])
                        for q in range(4):
                            fc = half * 4 + q
                            S.add("act", lambda e, hb=hb, fc=fc, pb=pb, q=q, j=j, s=s: e.activation(out=hb[:, fc, j * 128:(j + 1) * 128], in_=ps[pb][:, q * 128:(q + 1) * 128], func=AF.Identity, scale=self.modP[:, 0, fc, s:s + 1], bias=self.modT[:, fc, s:s + 1]), reads=[f"ps{pb}", "modP", "modT"], writes=[hk])
                S.add("sp", lambda e, hb=hb, t0=t0, W=W: e.dma_start(out=self.hT[:, :, t0:t0 + W].rearrange("c p t -> p c t"), in_=hb[:, :, 0:W]), reads=[hk], writes=["hT"], dma=True)
                tb = tabs[gi % 2]
                tk = f"tabs{gi % 2}"
                if gi > 0:
                    l0 = t0 - NCTX
                    for ti, nm in enumerate(["cos64", "sin64", "cos32", "sin32"]):
                        S.add("sp", lambda e, tb=tb, ti=ti, nm=nm, l0=l0: e.dma_start(out=tb[:, ti, :], in_=self.cst[nm][:, l0:l0 + 512]), writes=[tk], dma=True)
                ipend = []
                for ci, (nm, segs) in enumerate(fm):
                    M = sum(nc_ for _, nc_ in segs)
                    pb = cnt["ps"] % 4
                    cnt["ps"] += 1
                    pk = f"ps{pb}"
                    for kc in range(8):
                        S.add("pe", lambda e, ci=ci, M=M, kc=kc, hb=hb, W=W, pb=pb: e.matmul(ps[pb][0:M, 0:W], lhsT=wfm[:, kc, ci * 128:ci * 128 + M], rhs=hb[:, kc, 0:W], start=(kc == 0), stop=(kc == 7)), reads=["wfm", hk], writes=[pk])
                    while ipend:
                        ipend.pop(0)()
                    if nm[0] == "A" or nm == "G16":
                        e_i = cnt["ev"] % 3
                        cnt["ev"] += 1
                        eb, ek = ev[e_i], f"ev{e_i}"
                        S.add("act", lambda e, eb=eb, M=M, W=W, pb=pb: e.activation(out=eb[0:M, 0:W], in_=ps[pb][0:M, 0:W], func=AF.Copy), reads=[pk], writes=[ek])
                        if nm == "G16":
                            S.add("sp", lambda e, eb=eb, t0=t0, W=W: e.dma_start(out=self.ag16[:, t0:t0 + W], in_=eb[0:16, 0:W]), reads=[ek], writes=["ag16"], dma=True)
                        else:
                            j = int(nm[1])
                            S.add("sp", lambda e, eb=eb, t0=t0, W=W, j=j: e.dma_start(out=self.aqkv[j * 128:(j + 1) * 128, t0:t0 + W], in_=eb[:, 0:W]), reads=[ek], writes=["aqkv"], dma=True)
                        continue
                    def post2(nm=nm, M=M, W=W, pb=pb, pk=pk, t0=t0, gi=gi, tb=tb, tk=tk):
                        isC = nm[1] == "C"
                        isB = nm[1] == "B"
                        e_i = cnt["ev"] % 3
                        cnt["ev"] += 1
                        eb, ek = ev[e_i], f"ev{e_i}"
                        o_i = cnt["ob"] % 3
                        cnt["ob"] += 1
                        obt, obk = ob[o_i], f"ob{o_i}"
                        if isC:
                            gcol = 1 if nm == "KC" else 0
                            S.add("act", lambda e, M=M, W=W, pb=pb: e.activation(out=sq[0:M, 0:W], in_=ps[pb][0:M, 0:W], func=AF.Square), reads=[pk], writes=["sq"])
                            S.add("pe", lambda e, M=M, W=W: e.matmul(ps[4][0:M, 0:W], lhsT=bones[0:M, 0:M], rhs=sq[0:M, 0:W], start=True, stop=True), reads=["sq", "bones"], writes=["ps4"])
                            S.add("act", lambda e, M=M, W=W: e.activation(out=rstd[0:M, 0:W], in_=ps[4][0:M, 0:W], func=AF.Sqrt, scale=1.0 / 64.0, bias=EPS), reads=["ps4"], writes=["rstd"])
                            S.add("dve", lambda e, M=M, W=W: e.reciprocal(out=rstd[0:M, 0:W], in_=rstd[0:M, 0:W]), reads=["rstd"], writes=["rstd"])
                            S.add("dve", lambda e, eb=eb, M=M, W=W, pb=pb, gcol=gcol: e.scalar_tensor_tensor(out=eb[0:M, 0:W], in0=ps[pb][0:M, 0:W], scalar=qkg[0:M, gcol:gcol + 1], in1=rstd[0:M, 0:W], op0=ALU.mult, op1=ALU.mult), reads=[pk, "qkg", "rstd"], writes=[ek])
                        else:
                            S.add("act", lambda e, eb=eb, M=M, W=W, pb=pb: e.activation(out=eb[0:M, 0:W], in_=ps[pb][0:M, 0:W], func=AF.Copy), reads=[pk], writes=[ek])
                        if gi == 0:
                            S.add("dve", lambda e, obt=obt, eb=eb, M=M, W=W: e.tensor_copy(out=obt[0:M, 0:W], in_=eb[0:M, 0:W]), reads=[ek], writes=[obk])
                        else:
                            rot = rot32 if isB else rot64
                            rk = "rot32" if isB else "rot64"
                            ct, sn = (2, 3) if isB else (0, 1)
                            S.add("pe", lambda e, rot=rot, eb=eb, M=M, W=W: e.matmul(ps[5][0:M, 0:W], lhsT=rot[0:M, 0:M], rhs=eb[0:M, 0:W], start=True, stop=True), reads=[ek, rk], writes=["ps5"])
                            b_i = cnt["evb"] % 2
                            cnt["evb"] += 1
                            e2, e2k = ev2[b_i], f"evb{b_i}"
                            S.add("dve", lambda e, e2=e2, tb=tb, sn=sn, M=M, W=W: e.tensor_tensor(out=e2[0:M, 0:W], in0=ps[5][0:M, 0:W], in1=tb[0:M, sn, 0:W], op=ALU.mult), reads=["ps5", tk], writes=[e2k])
                            S.add("pool", lambda e, eb=eb, tb=tb, ct=ct, M=M, W=W: e.tensor_tensor(out=eb[0:M, 0:W], in0=eb[0:M, 0:W], in1=tb[0:M, ct, 0:W], op=ALU.mult), reads=[ek, tk], writes=[ek])
                            S.add("dve", lambda e, obt=obt, eb=eb, e2=e2, M=M, W=W: e.tensor_tensor(out=obt[0:M, 0:W], in0=eb[0:M, 0:W], in1=e2[0:M, 0:W], op=ALU.add), reads=[ek, e2k], writes=[obk])
                        if isB:
                            dst = self.qkB[(0 if nm[0] == "Q" else 3) + int(nm[2])]
                            dk = "qkB"
                        elif isC:
                            dst = self.qkC[2 if nm == "KC" else int(nm[2])]
                            dk = "qkC"
                        else:
                            dst = self.qkD[2 if nm == "KD" else int(nm[2])]
                            dk = "qkD"
                        S.add("sp", lambda e, dst=dst, obt=obt, M=M, W=W, t0=t0: e.dma_start(out=dst[0:M, t0:t0 + W], in_=obt[0:M, 0:W]), reads=[obk], writes=[dk], dma=True)
                    ipend.append(post2)
                while ipend:
                    ipend.pop(0)()
                for j in range(ntile):
                    tt = t0 + j * 128
                    ot = otm[j % 2]
                    otk = f"otm{j % 2}"
                    for (c0, ncol, pb) in [(0, 512, 4), (512, 256, 5)]:
                        for kc in range(8):
                            S.add("pe", lambda e, hb=hb, kc=kc, j=j, c0=c0, ncol=ncol, pb=pb: e.matmul(ps[pb][:, 0:ncol], lhsT=hb[:, kc, j * 128:(j + 1) * 128], rhs=wtm[:, kc, c0:c0 + ncol], start=(kc == 0), stop=(kc == 7)), reads=[hk, "wtm"], writes=[f"ps{pb}"])
                    S.add("dve", lambda e, ot=ot: e.tensor_copy(out=ot[:, 0:512], in_=ps[4][:, 0:512]), reads=["ps4"], writes=[otk])
                    S.add("act", lambda e, ot=ot: e.activation(out=ot[:, 512:768], in_=ps[5][:, 0:256], func=AF.Silu), reads=["ps5"], writes=[otk])
                    S.add("sp", lambda e, ot=ot, tt=tt: e.dma_start(out=self.vtok[tt:tt + 128, :], in_=ot[:, 0:512]), reads=[otk], writes=["vtok"], dma=True)
                    S.add("sp", lambda e, ot=ot, tt=tt: e.dma_start(out=self.gateA[tt:tt + 128, :], in_=ot[:, 512:768]), reads=[otk], writes=["gateA"], dma=True)


    def phase_attn(self, l):
        nc, S = self.nc, self.S
        ps = self.ps
        with_ctx = l < DEPTH - 1
        lam_init = 0.8 - 0.6 * math.exp(-0.3 * l)
        with ExitStack() as st:
            sbt = lambda name, shape, dt=F32: st.enter_context(nc.sbuf_tensor(self.uniq(name), shape, dt))
            dl = sbt("dl", [64, 128])
            pr = sbt("pr", [64, 64])
            lam = sbt("lam", [64, 4])
            gB = sbt("gB", [64, 1])
            esink = sbt("esink", [64, 4])
            sel = sbt("sel", [128, 64])
            ones64 = sbt("ones64", [64, 64])
            trige = sbt("trige", [128, 128], BF16)
            trile = sbt("trile", [128, 128], BF16)
            S.add("sp", lambda e: e.dma_start(out=dl[:], in_=self.diff_lambda[l:l + 1, :].partition_broadcast(64)), writes=["dl"], dma=True)
            S.add("sp", lambda e: e.dma_start(out=gB[:], in_=self.diff_norm[l].rearrange("(d o) -> d o", o=1)), writes=["gB"], dma=True)
            S.add("sp", lambda e: e.dma_start(out=esink[:], in_=self.sink_d[l:l + 1, :].partition_broadcast(64)), writes=["esink"], dma=True)
            S.add("sp", lambda e: e.dma_start(out=sel[:], in_=self.cst["sel64"][:, :]), writes=["sel"], dma=True)
            S.add("pool", lambda e: e.dma_start(out=trige[:], in_=self.cst["tri_ge"][:, :]), writes=["trige"], dma=True)
            S.add("pool", lambda e: e.dma_start(out=trile[:], in_=self.cst["tri_le"][:, :]), writes=["trile"], dma=True)
            S.add("dve", lambda e: e.memset(ones64[:], 1.0), writes=["ones64"])
            S.add("dve", lambda e: e.tensor_tensor(out=pr[:].rearrange("p (a b) -> p a b", a=2), in0=dl[:].rearrange("p (a two b) -> p a two b", a=2, two=2)[:, :, 0, :], in1=dl[:].rearrange("p (a two b) -> p a two b", a=2, two=2)[:, :, 1, :], op=ALU.mult), reads=["dl"], writes=["pr"])
            S.add("dve", lambda e: e.tensor_reduce(out=lam[:, 0:2], in_=pr[:].rearrange("p (a b) -> p a b", a=2), axis=AX.X, op=ALU.add), reads=["pr"], writes=["lam"])
            S.add("act", lambda e: e.activation(out=lam[:, 0:2], in_=lam[:, 0:2], func=AF.Exp), reads=["lam"], writes=["lam"])
            S.add("dve", lambda e: e.scalar_tensor_tensor(out=lam[:, 2:3], in0=lam[:, 1:2], scalar=-lam_init, in1=lam[:, 0:1], op0=ALU.add, op1=ALU.subtract), reads=["lam"], writes=["lam"])
            S.add("dve", lambda e: e.tensor_scalar(out=gB[:], in0=gB[:], scalar1=1.0 - lam_init, scalar2=None, op0=ALU.mult), reads=["gB"], writes=["gB"])
            S.add("act", lambda e: e.activation(out=esink[:], in_=esink[:], func=AF.Exp), reads=["esink"], writes=["esink"])
            pT = [sbt(f"pT{i}", [128, 512], BF16) for i in range(8)]
            dsb = [sbt(f"dsb{i}", [128, 512]) for i in range(2)]
            rden = [sbt(f"rden{i}", [64, 512]) for i in range(2)]
            om = [sbt(f"om{i}", [64, 512]) for i in range(2)]
            od = sbt("od", [64, 512])
            sqb = sbt("sqb", [64, 512])
            rs = sbt("rs", [64, 512])
            obf = [sbt(f"obf{i}", [64, 512], BF16) for i in range(2)]
            vaug = sbt("vaug", [128, NT, 4, 128], BF16)
            qT = sbt("qT", [128, 8, T], BF16)
            kT = sbt("kT", [128, 3, T], BF16)
            cnt = {"s": 0, "o": 0}
            S.add("pool", lambda e: e.memset(vaug[:, :, :, 64:128], 1.0), writes=["vaug1"])

            def load_mixer(src, qrows, krows, vcol, nh):
                S.add("dve", lambda e: e.memset(qT[:], 0.0), writes=["qT"])
                S.add("pool", lambda e: e.memset(kT[64:128, :, :], 0.0), writes=["kT"])
                for (slot, ch, r0, nr) in qrows:
                    S.add("sp", lambda e, slot=slot, ch=ch, r0=r0, nr=nr: e.dma_start(out=qT[r0:r0 + nr, slot, :], in_=src[ch, r0:r0 + nr, :]), reads=[src.tensor.name], writes=["qT"], dma=True)
                for (slot, ch, nr) in krows:
                    S.add("act", lambda e, slot=slot, ch=ch, nr=nr: e.dma_start(out=kT[0:nr, slot, :], in_=src[ch, 0:nr, :]), reads=[src.tensor.name], writes=["kT"], dma=True)
                vv = self.vtok[:, vcol:vcol + nh * 64].rearrange("(t p) (h d) -> p t h d", p=128, d=64)
                for h_ in range(nh):
                    for (ta, tb_) in [(0, 17), (17, NT)]:
                        S.add("sp", lambda e, h_=h_, ta=ta, tb_=tb_: e.dma_start(out=vaug[:, ta:tb_, h_, 0:64], in_=vv[:, ta:tb_, h_, :]), reads=["vtok"], writes=["vaug"], dma=True)

            def attend(q_ap_fn, k_ap_fn, vh, scale, ktiles, W, acc_b, first_start=True):
                n = len(ktiles)
                LA = 3
                sbs = []
                for i in range(n + LA):
                    if i < n:
                        kt = ktiles[i]
                        sb_ = cnt["s"] % 4
                        pi_ = cnt["s"] % 8
                        cnt["s"] += 1
                        sbs.append(pi_)
                        S.add("pe", lambda e, kt=kt, sb_=sb_: e.matmul(ps[sb_][:, 0:W], lhsT=k_ap_fn(kt), rhs=q_ap_fn(), start=True, stop=True), reads=["qT", "kT"], writes=[f"ps{sb_}"])
                        S.add("act", lambda e, sb_=sb_, pi_=pi_: e.activation(out=pT[pi_][:, 0:W], in_=ps[sb_][:, 0:W], func=AF.Exp, scale=scale), reads=[f"ps{sb_}"], writes=[f"pT{pi_}"])
                    if i >= LA:
                        i2 = i - LA
                        kt2, sb2 = ktiles[i2], sbs[i2]
                        S.add("pe", lambda e, kt2=kt2, sb2=sb2, i2=i2: e.matmul(ps[acc_b][:, 0:W], lhsT=vaug[:, kt2, vh, :], rhs=pT[sb2][:, 0:W], start=(i2 == 0), stop=(i2 == n - 1)), reads=[f"pT{sb2}", "vaug", "vaug1"], writes=[f"ps{acc_b}"])

            def finish_den(acc_b, W, slot, extra=None):
                d_ = dsb[slot]
                S.add("act", lambda e: e.activation(out=d_[64:128, 0:W], in_=ps[acc_b][64:128, 0:W], func=AF.Copy), reads=[f"ps{acc_b}"], writes=[f"dsb{slot}"])
                S.add("pe", lambda e: e.matmul(ps[6 + slot][0:64, 0:W], lhsT=sel[64:128, :], rhs=d_[64:128, 0:W], start=True, stop=True), reads=[f"dsb{slot}", "sel"], writes=[f"ps{6 + slot}"])
                if extra is not None:
                    S.add("dve", lambda e: e.tensor_scalar(out=rden[slot][:, 0:W], in0=ps[6 + slot][0:64, 0:W], scalar1=extra, scalar2=None, op0=ALU.add), reads=[f"ps{6 + slot}", "esink"], writes=[f"rden{slot}"])
                    S.add("dve", lambda e: e.reciprocal(out=rden[slot][:, 0:W], in_=rden[slot][:, 0:W]), reads=[f"rden{slot}"], writes=[f"rden{slot}"])
                else:
                    S.add("dve", lambda e: e.reciprocal(out=rden[slot][:, 0:W], in_=ps[6 + slot][0:64, 0:W]), reads=[f"ps{6 + slot}"], writes=[f"rden{slot}"])

            def store_out(tile_ap, chunk, r0, t0, W, key):
                S.add("sp", lambda e: e.dma_start(out=self.oT[chunk, r0:r0 + 64, t0:t0 + W], in_=tile_ap), reads=[key], writes=["oT"], dma=True)

            groups = [g for gi, g in enumerate(GROUPS) if gi > 0 or with_ctx]
            all_kt = list(range(NT))
            load_mixer(self.qkB, [(blk, blk // 3, (blk % 3) * 32, 32) for blk in range(8)], [(0, 3, 96), (1, 4, 96), (2, 5, 64)], 0, 4)
            scB = 32 ** -0.5
            for h in range(4):
                for (t0, W) in groups:
                    kts = all_kt if t0 >= NCTX else [0, 1]
                    for m in range(2):
                        blk = h * 2 + m
                        ch, off = blk // 3, (blk % 3) * 32
                        attend(lambda blk=blk, t0=t0, W=W: qT[:, blk, t0:t0 + W],
                               lambda kt, ch=ch: kT[:, ch, kt * 128:(kt + 1) * 128],
                               h, scB, kts, W, 4 + m)
                    for m in range(2):
                        finish_den(4 + m, W, m)
                        S.add("dve", lambda e, m=m, W=W: e.tensor_tensor(out=om[m][:, 0:W], in0=ps[4 + m][0:64, 0:W], in1=rden[m][:, 0:W], op=ALU.mult), reads=[f"ps{4 + m}", f"rden{m}"], writes=[f"om{m}"])
                    S.add("dve", lambda e, W=W: e.scalar_tensor_tensor(out=od[:, 0:W], in0=om[1][:, 0:W], scalar=lam[:, 2:3], in1=om[0][:, 0:W], op0=ALU.mult, op1=ALU.add), reads=["om0", "om1", "lam"], writes=["od"])
                    S.add("act", lambda e, W=W: e.activation(out=sqb[:, 0:W], in_=od[:, 0:W], func=AF.Square), reads=["od"], writes=["sqb"])
                    S.add("pe", lambda e, W=W: e.matmul(ps[6][0:64, 0:W], lhsT=ones64[:], rhs=sqb[:, 0:W], start=True, stop=True), reads=["sqb", "ones64"], writes=["ps6"])
                    S.add("act", lambda e, W=W: e.activation(out=rs[:, 0:W], in_=ps[6][0:64, 0:W], func=AF.Sqrt, scale=1.0 / 64.0, bias=EPS), reads=["ps6"], writes=["rs"])
                    S.add("dve", lambda e, W=W: e.reciprocal(out=rs[:, 0:W], in_=rs[:, 0:W]), reads=["rs"], writes=["rs"])
                    oi = cnt["o"] % 2
                    cnt["o"] += 1
                    S.add("dve", lambda e, W=W, oi=oi: e.scalar_tensor_tensor(out=obf[oi][:, 0:W], in0=od[:, 0:W], scalar=gB[:, 0:1], in1=rs[:, 0:W], op0=ALU.mult, op1=ALU.mult), reads=["od", "gB", "rs"], writes=[f"obf{oi}"])
                    store_out(obf[oi][:, 0:W], 2 + h // 2, (h % 2) * 64, t0, W, f"obf{oi}")
            load_mixer(self.qkC, [(j_ * 2 + g_, g_, j_ * 64, 64) for j_ in range(2) for g_ in range(2)], [(0, 2, 128)], 256, 2)
            scC = 64 ** -0.5
            hi = 0
            for j in range(2):
                for g in range(2):
                    for (t0, W) in groups:
                        kts = all_kt if t0 >= NCTX else [0, 1]
                        ab = 4 + (hi % 2)
                        sl = hi % 2
                        hi += 1
                        attend(lambda j=j, g=g, t0=t0, W=W: qT[:, j * 2 + g, t0:t0 + W],
                               lambda kt: kT[:, 0, kt * 128:(kt + 1) * 128],
                               j, scC, kts, W, ab)
                        finish_den(ab, W, sl)
                        oi = cnt["o"] % 2
                        cnt["o"] += 1
                        S.add("dve", lambda e, W=W, ab=ab, sl=sl, oi=oi: e.tensor_tensor(out=obf[oi][:, 0:W], in0=ps[ab][0:64, 0:W], in1=rden[sl][:, 0:W], op=ALU.mult), reads=[f"ps{ab}", f"rden{sl}"], writes=[f"obf{oi}"])
                        store_out(obf[oi][:, 0:W], 4 + j, g * 64, t0, W, f"obf{oi}")
            load_mixer(self.qkD, [(j_ * 2 + g_, g_, j_ * 64, 64) for j_ in range(2) for g_ in range(2)], [(0, 2, 128)], 384, 2)
            for j in range(2):
                for g in range(2):
                    qh = j * 2 + g
                    for (t0, W) in groups:
                        ab = 4 + (hi % 2)
                        sl = hi % 2
                        hi += 1
                        qf = lambda c0, c1, j=j, g=g, t0=t0: qT[:, j * 2 + g, t0 + c0:t0 + c1]
                        kf = lambda kt: kT[:, 0, kt * 128:(kt + 1) * 128]
                        items = [(0, 0, W, []), (1, 0, W, [])]
                        if t0 >= NCTX:
                            qt0 = (t0 - NCTX) // 128
                            for kt in range(qt0 - 1, qt0 + 5):
                                if kt < 0 or kt > 31:
                                    continue
                                qlo, qhi = max(kt - 1, qt0), min(kt + 1, qt0 + 3)
                                if qlo > qhi:
                                    continue
                                masks = []
                                for qt in range(qlo, qhi + 1):
                                    if qt == kt + 1:
                                        masks.append(("trige", (qt - qt0) * 128))
                                    elif qt == kt - 1:
                                        masks.append(("trile", (qt - qt0) * 128))
                                items.append((kt + 2, (qlo - qt0) * 128, (qhi - qt0 + 1) * 128, masks))
                        n = len(items)
                        LA = 3
                        sbs = []
                        for ii in range(n + LA):
                            if ii < n:
                                (ktile, c0, c1, masks) = items[ii]
                                sb_ = cnt["s"] % 4
                                pi_ = cnt["s"] % 8
                                cnt["s"] += 1
                                sbs.append(pi_)
                                S.add("pe", lambda e, ktile=ktile, sb_=sb_, c0=c0, c1=c1, qf=qf, kf=kf: e.matmul(ps[sb_][:, c0:c1], lhsT=kf(ktile), rhs=qf(c0, c1), start=True, stop=True), reads=["qT", "kT"], writes=[f"ps{sb_}"])
                                S.add("act", lambda e, sb_=sb_, pi_=pi_, c0=c0, c1=c1: e.activation(out=pT[pi_][:, c0:c1], in_=ps[sb_][:, c0:c1], func=AF.Exp, scale=scC), reads=[f"ps{sb_}"], writes=[f"pT{pi_}"])
                                for (mk, mc0) in masks:
                                    mt = trige if mk == "trige" else trile
                                    S.add("pool", lambda e, pi_=pi_, mt=mt, mc0=mc0: e.tensor_tensor(out=pT[pi_][:, mc0:mc0 + 128], in0=pT[pi_][:, mc0:mc0 + 128], in1=mt[:], op=ALU.mult), reads=[f"pT{pi_}", mk], writes=[f"pT{pi_}"])
                            if ii >= LA:
                                i = ii - LA
                                (ktile, c0, c1, masks) = items[i]
                                sb2 = sbs[i]
                                S.add("pe", lambda e, ktile=ktile, sb2=sb2, c0=c0, c1=c1, i=i, j=j, ab=ab, n=n: e.matmul(ps[ab][:, c0:c1], lhsT=vaug[:, ktile, j, :], rhs=pT[sb2][:, c0:c1], start=(i == 0), stop=(i == n - 1)), reads=[f"pT{sb2}", "vaug", "vaug1"], writes=[f"ps{ab}"])
                        finish_den(ab, W, sl, extra=esink[:, qh:qh + 1])
                        oi = cnt["o"] % 2
                        cnt["o"] += 1
                        S.add("dve", lambda e, W=W, ab=ab, sl=sl, oi=oi: e.tensor_tensor(out=obf[oi][:, 0:W], in0=ps[ab][0:64, 0:W], in1=rden[sl][:, 0:W], op=ALU.mult), reads=[f"ps{ab}", f"rden{sl}"], writes=[f"obf{oi}"])
                        store_out(obf[oi][:, 0:W], 6 + j, g * 64, t0, W, f"obf{oi}")


    def phase_merge(self, l):
        nc, S = self.nc, self.S
        ps = self.ps
        with_ctx = l < DEPTH - 1
        groups = [(gi, g) for gi, g in enumerate(GROUPS) if gi > 0 or with_ctx]
        wv = self.w_in[l].rearrange("(kc p) n -> p kc n", p=128)
        for pa in range(2):
            with ExitStack() as st:
                sbt = lambda name, shape, dt=F32: st.enter_context(nc.sbuf_tensor(self.uniq(name), shape, dt))
                wg = sbt("wg", [128, 8, 2048], BF16)
                wbr = sbt("wbr", [128, 4, 2, 512], BF16)
                for i in range(4):
                    c0 = C_GL + i * 1024 + pa * 512
                    S.add("pool", lambda e, i=i, c0=c0: e.dma_start(out=wg[:, :, i * 512:(i + 1) * 512], in_=wv[:, :, c0:c0 + 512]), writes=["wg"], dma=True)
                    S.add("pool", lambda e, i=i, pa=pa: e.dma_start(out=wbr[:, i], in_=self.w_br[l, i].rearrange("(kc p) n -> p kc n", p=128)[:, :, pa * 512:(pa + 1) * 512]), writes=["wbr"], dma=True)
                hg = [sbt(f"hg{i}", [128, 8, 512], BF16) for i in range(2)]
                og = [sbt(f"og{i}", [128, 8, 512], BF16) for i in range(2)]
                sg = [sbt(f"sg{i}", [128, 512]) for i in range(2)]
                tmp = [sbt(f"tmp{i}", [128, 512]) for i in range(2)]
                macc = sbt("macc", [128, 512])
                dbt = sbt("dbt", [128, 512])
                mT = [sbt(f"mT{i}", [128, 8, 512], BF16) for i in range(2)]
                if pa == 1:
                    wout = sbt("wout", [128, 8, 1024], BF16)
                    for kc in range(0, 8, 2):
                        S.add("pool", lambda e, kc=kc: e.dma_start(out=wout[:, kc:kc + 2, :], in_=self.w_out[l].rearrange("(kc p) n -> p kc n", p=128)[:, kc:kc + 2, :]), writes=["wout"], dma=True)
                    wr = sbt("wr", [128, 8, 16])
                    S.add("sp", lambda e: e.dma_start(out=wr[:], in_=self.w_router[l].rearrange("(kc p) n -> p kc n", p=128)), writes=["wr"], dma=True)
                    bc = {}
                    for nm in ["g1_0", "g1_1", "sc2_0", "sc2_1", "sh2_0", "sh2_1", "lng", "lnb"]:
                        bc[nm] = sbt("bc_" + nm, [128, 1024])
                    for s_ in range(2):
                        self.bcast_mod("sp", bc[f"g1_{s_}"], 2, s_, f"bc_g1_{s_}")
                        self.bcast_mod("sp", bc[f"sc2_{s_}"], 4, s_, f"bc_sc2_{s_}")
                        self.bcast_mod("sp", bc[f"sh2_{s_}"], 3, s_, f"bc_sh2_{s_}")
                        S.add("pool", lambda e, s_=s_: e.tensor_scalar(out=bc[f"sc2_{s_}"][:], in0=bc[f"sc2_{s_}"][:], scalar1=1.0, scalar2=None, op0=ALU.add), reads=[f"bc_sc2_{s_}"], writes=[f"bc_sc2_{s_}"])
                    S.add("sp", lambda e: e.dma_start(out=bc["lng"][:], in_=self.ln1_g[l:l + 1, :].partition_broadcast(128)), writes=["bc_lng"], dma=True)
                    S.add("sp", lambda e: e.dma_start(out=bc["lnb"][:], in_=self.ln1_b[l:l + 1, :].partition_broadcast(128)), writes=["bc_lnb"], dma=True)
                    xt = [sbt(f"mxt{i}", [128, 1024]) for i in range(2)]
                    rt = [sbt(f"mrt{i}", [128, 1024]) for i in range(2)]
                    x1 = [sbt(f"mx1{i}", [128, 1024]) for i in range(2)]
                    h2f = sbt("h2f", [128, 1024])
                    h2b = [sbt(f"h2b{i}", [128, 1024], BF16) for i in range(2)]
                    h2T = sbt("h2Tf", [128, 8, 128])
                    bst = sbt("bst", [128, 2, 6])
                    mv = sbt("mv", [128, 2])
                    rsd = sbt("rsd", [128, 1])
                    ex = sbt("ex", [128, 16])
                    esum = sbt("esum", [128, 1])
                    afft = [sbt(f"afft{i}", [128, 16]) for i in range(2)]
                tcount = 0
                pcnt = 0
                mpend = []
                for (gi, (t0, W)) in groups:
                    s_ = 1 if gi == 0 else 0
                    hb, hk = hg[gi % 2], f"hg{gi % 2}"
                    obf, ok_ = og[gi % 2], f"og{gi % 2}"
                    mt, mk = mT[gi % 2], f"mT{gi % 2}"
                    S.add("sp", lambda e, hb=hb, t0=t0, W=W: e.dma_start(out=hb[:, :, 0:W], in_=self.hT[:, :, t0:t0 + W].rearrange("c p t -> p c t")), reads=["hT"], writes=[hk], dma=True)
                    S.add("act", lambda e, obf=obf, t0=t0, W=W: e.dma_start(out=obf[:, :, 0:W], in_=self.oT[:, :, t0:t0 + W].rearrange("c p t -> p c t")), reads=["oT"], writes=[ok_], dma=True)
                    if pa == 1:
                        S.add("sp", lambda e, mt=mt, t0=t0, W=W: e.dma_start(out=mt[:, 0:4, 0:W], in_=self.mTd[:, :, t0:t0 + W].rearrange("c p t -> p c t")), reads=["mTd"], writes=[mk], dma=True)
                    for fcl in range(4):
                        fc = pa * 4 + fcl
                        for i in range(4):
                            pa_, pb_ = pcnt % 2, 2 + pcnt % 2
                            pcnt += 1
                            for kc in range(8):
                                S.add("pe", lambda e, i=i, fcl=fcl, kc=kc, hb=hb, W=W, pa_=pa_: e.matmul(ps[pa_][:, 0:W], lhsT=wg[:, kc, i * 512 + fcl * 128:i * 512 + (fcl + 1) * 128], rhs=hb[:, kc, 0:W], start=(kc == 0), stop=(kc == 7)), reads=["wg", hk], writes=[f"ps{pa_}"])
                            sgi = sg[i % 2]
                            S.add("act", lambda e, sgi=sgi, W=W, pa_=pa_: e.activation(out=sgi[:, 0:W], in_=ps[pa_][:, 0:W], func=AF.Sigmoid), reads=[f"ps{pa_}"], writes=[f"sg{i % 2}"])
                            for k2 in range(2):
                                S.add("pe", lambda e, i=i, fcl=fcl, k2=k2, obf=obf, W=W, pb_=pb_: e.matmul(ps[pb_][:, 0:W], lhsT=wbr[:, i, k2, fcl * 128:(fcl + 1) * 128], rhs=obf[:, i * 2 + k2, 0:W], start=(k2 == 0), stop=(k2 == 1)), reads=["wbr", ok_], writes=[f"ps{pb_}"])
                            if i == 0:
                                S.add("dve", lambda e, sgi=sgi, W=W, pb_=pb_: e.tensor_tensor(out=macc[:, 0:W], in0=sgi[:, 0:W], in1=ps[pb_][:, 0:W], op=ALU.mult), reads=[f"sg{i % 2}", f"ps{pb_}"], writes=["macc"])
                            else:
                                tm = tmp[i % 2]
                                S.add("dve", lambda e, sgi=sgi, tm=tm, W=W, pb_=pb_: e.tensor_tensor(out=tm[:, 0:W], in0=sgi[:, 0:W], in1=ps[pb_][:, 0:W], op=ALU.mult), reads=[f"sg{i % 2}", f"ps{pb_}"], writes=[f"tmp{i % 2}"])
                                if i < 3:
                                    S.add("pool", lambda e, tm=tm, W=W: e.tensor_tensor(out=macc[:, 0:W], in0=macc[:, 0:W], in1=tm[:, 0:W], op=ALU.add), reads=["macc", f"tmp{i % 2}"], writes=["macc"])
                                else:
                                    S.add("pool", lambda e, tm=tm, W=W, mt=mt, fc=fc: e.tensor_tensor(out=mt[:, fc, 0:W], in0=macc[:, 0:W], in1=tm[:, 0:W], op=ALU.add), reads=["macc", f"tmp{i % 2}"], writes=[mk])
                    if pa == 0:
                        S.add("sp", lambda e, mt=mt, t0=t0, W=W: e.dma_start(out=self.mTd[:, :, t0:t0 + W].rearrange("c p t -> p c t"), in_=mt[:, 0:4, 0:W]), reads=[mk], writes=["mTd"], dma=True)
                        continue
                    for j in range(W // 128):
                        tt = t0 + j * 128
                        b2 = tcount % 2
                        tcount += 1
                        xb, xk = xt[b2], f"mxt{b2}"
                        rb, rk = rt[b2], f"mrt{b2}"
                        x1b, x1k = x1[b2], f"mx1{b2}"
                        S.add("sp", lambda e, xb=xb, tt=tt: e.dma_start(out=xb[:], in_=self.xres[tt:tt + 128, :]), reads=["xres"], writes=[xk], dma=True)
                        for half in range(2):
                            for kc in range(8):
                                S.add("pe", lambda e, mt=mt, kc=kc, j=j, half=half: e.matmul(ps[4 + half][:, :], lhsT=mt[:, kc, j * 128:(j + 1) * 128], rhs=wout[:, kc, half * 512:(half + 1) * 512], start=(kc == 0), stop=(kc == 7)), reads=[mk, "wout"], writes=[f"ps{4 + half}"])
                            S.add("dve", lambda e, rb=rb, half=half, s_=s_: e.tensor_tensor(out=rb[:, half * 512:(half + 1) * 512], in0=ps[4 + half][:, :], in1=bc[f"g1_{s_}"][:, half * 512:(half + 1) * 512], op=ALU.mult), reads=[f"ps{4 + half}", f"bc_g1_{s_}"], writes=[rk])
                        S.add("dve", lambda e, rb=rb, xb=xb: e.scalar_tensor_tensor(out=rb[:], in0=xb[:], scalar=ALPHA, in1=rb[:], op0=ALU.mult, op1=ALU.add), reads=[xk, rk], writes=[rk])
                        self.layer_norm_tile(rb, rk, x1b, x1k, bc["lng"], "bc_lng", bc["lnb"], "bc_lnb", bst, mv, rsd)
                        S.add("sp", lambda e, x1b=x1b, tt=tt: e.dma_start(out=self.xres[tt:tt + 128, :], in_=x1b[:]), reads=[x1k], writes=["xres"], dma=True)
                        def post(b2=b2, tt=tt, x1b=x1b, x1k=x1k, s_=s_):
                            hb2, hb2k = h2b[b2], f"h2b{b2}"
                            S.add("pool", lambda e, x1b=x1b, s_=s_: e.tensor_tensor(out=h2f[:], in0=x1b[:], in1=bc[f"sc2_{s_}"][:], op=ALU.mult), reads=[x1k, f"bc_sc2_{s_}"], writes=["h2f"])
                            S.add("pool", lambda e, hb2=hb2, s_=s_: e.tensor_tensor(out=hb2[:], in0=h2f[:], in1=bc[f"sh2_{s_}"][:], op=ALU.add), reads=["h2f", f"bc_sh2_{s_}"], writes=[hb2k])
                            S.add("sp", lambda e, hb2=hb2, tt=tt: e.dma_start(out=self.h2tok[tt:tt + 128, :], in_=hb2[:]), reads=[hb2k], writes=["h2tok"], dma=True)
                            for half in range(2):
                                for q in range(4):
                                    fc = half * 4 + q
                                    S.add("pe", lambda e, x1b=x1b, fc=fc, half=half, q=q: e.transpose(ps[6 + half][:, q * 128:(q + 1) * 128], x1b[:, fc * 128:(fc + 1) * 128], self.ident[:]), reads=[x1k, "ident"], writes=[f"ps{6 + half}"])
                                for q in range(4):
                                    fc = half * 4 + q
                                    S.add("act", lambda e, fc=fc, half=half, q=q, s_=s_: e.activation(out=h2T[:, fc, :], in_=ps[6 + half][:, q * 128:(q + 1) * 128], func=AF.Identity, scale=self.modP[:, 1, fc, s_:s_ + 1], bias=self.modT[:, 24 + fc, s_:s_ + 1]), reads=[f"ps{6 + half}", "modP", "modT"], writes=["h2Tf"])
                            for kc in range(8):
                                S.add("pe", lambda e, kc=kc: e.matmul(ps[6][:, 0:16], lhsT=h2T[:, kc, :], rhs=wr[:, kc, :], start=(kc == 0), stop=(kc == 7)), reads=["h2Tf", "wr"], writes=["ps6"])
                            af, afk = afft[b2], f"afft{b2}"
                            S.add("act", lambda e: e.activation(out=ex[:], in_=ps[6][:, 0:16], func=AF.Exp, accum_out=esum[:]), reads=["ps6"], writes=["ex", "esum"])
                            S.add("dve", lambda e: e.reciprocal(out=esum[:], in_=esum[:]), reads=["esum"], writes=["esum"])
                            S.add("dve", lambda e, af=af: e.tensor_scalar(out=af[:], in0=ex[:], scalar1=esum[:, 0:1], scalar2=None, op0=ALU.mult), reads=["ex", "esum"], writes=[afk])
                            S.add("sp", lambda e, af=af, tt=tt: e.dma_start(out=self.aff[tt:tt + 128, :], in_=af[:]), reads=[afk], writes=["aff"], dma=True)
                        if mpend:
                            mpend.pop()()
                        mpend.append(post)
                while mpend:
                    mpend.pop()()
            self.S.barrier()

    def layer_norm_tile(self, rb, rk, ob_, ok_, g, gk, b, bk, bst, mv, rsd):
        S = self.S
        for half in range(2):
            S.add("dve", lambda e, half=half: e.bn_stats(out=bst[:, half, :], in_=rb[:, half * 512:(half + 1) * 512]), reads=[rk], writes=["bst"])
        S.add("dve", lambda e: e.bn_aggr(out=mv[:], in_=bst[:].rearrange("p a b -> p (a b)")), reads=["bst"], writes=["mv"])
        S.add("act", lambda e: e.activation(out=rsd[:], in_=mv[:, 1:2], func=AF.Sqrt, bias=EPS, scale=1.0), reads=["mv"], writes=["rsd"])
        S.add("dve", lambda e: e.reciprocal(out=rsd[:], in_=rsd[:]), reads=["rsd"], writes=["rsd"])
        S.add("dve", lambda e: e.tensor_scalar(out=rb[:], in0=rb[:], scalar1=mv[:, 0:1], scalar2=rsd[:, 0:1], op0=ALU.subtract, op1=ALU.mult), reads=[rk, "mv", "rsd"], writes=[rk])
        S.add("pool", lambda e: e.tensor_tensor(out=rb[:], in0=rb[:], in1=g[:], op=ALU.mult), reads=[rk, gk], writes=[rk])
        S.add("pool", lambda e: e.tensor_tensor(out=ob_[:], in0=rb[:], in1=b[:], op=ALU.add), reads=[rk, bk], writes=[ok_])


    def phase_moe(self, l):
        nc, S = self.nc, self.S
        ps = self.ps
        with_ctx = l < DEPTH - 1
        sets = [("L", NCTX, NLAT, 512)] + ([("C", 0, NCTX, 32)] if with_ctx else [])
        with ExitStack() as st:
            sbt = lambda name, shape, dt=F32: st.enter_context(nc.sbuf_tensor(self.uniq(name), shape, dt))
            aft = sbt("aft", [128, NT, 16])
            affT = sbt("affT", [16, T])
            junk = sbt("junk", [16, NLAT], BF16)
            ones = sbt("ones", [16, NLAT])
            msk = sbt("msk", [16, T])
            cum = sbt("cum", [16, T])
            posm = sbt("posm", [16, T])
            gv = sbt("gv", [16, T])
            sc_ = {k: sbt("bs_" + k, [16, 1]) for k in ["lo", "hi", "mid", "cnt", "ge", "d1", "d2"]}
            ptok = sbt("ptok", [128, NT, 16])
            S.add("sp", lambda e: e.dma_start(out=aft[:], in_=self.aff.rearrange("(t p) e -> p t e", p=128)), reads=["aff"], writes=["aft"], dma=True)
            S.add("pool", lambda e: e.memset(ones[:], 1.0), writes=["ones"])
            for t4 in range(0, NT, 4):
                nt_ = min(4, NT - t4)
                pb = (t4 // 4) % 2
                for q in range(nt_):
                    S.add("pe", lambda e, t4=t4, q=q, pb=pb: e.transpose(ps[pb][0:16, q * 128:(q + 1) * 128], aft[:, t4 + q, :], self.ident[:]), reads=["aft", "ident"], writes=[f"ps{pb}"])
                S.add("act", lambda e, t4=t4, nt_=nt_, pb=pb: e.activation(out=affT[:, t4 * 128:(t4 + nt_) * 128], in_=ps[pb][0:16, 0:nt_ * 128], func=AF.Copy), reads=[f"ps{pb}"], writes=["affT"])
            for (nm, tk0, n, cap) in sets:
                A = affT[:, tk0:tk0 + n]
                S.add("dve", lambda e: e.memset(sc_["lo"][:], 0.0), writes=["bs_lo"])
                S.add("dve", lambda e: e.memset(sc_["hi"][:], 1.0), writes=["bs_hi"])
                for it in range(30):
                    S.add("dve", lambda e: e.tensor_scalar(out=sc_["mid"][:], in0=sc_["lo"][:], scalar1=sc_["hi"][:, 0:1], scalar2=0.5, op0=ALU.add, op1=ALU.mult), reads=["bs_lo", "bs_hi"], writes=["bs_mid"])
                    S.add("dve", lambda e, A=A, n=n: e.tensor_scalar(out=junk[:, 0:n], in0=A, scalar1=sc_["mid"][:, 0:1], scalar2=0.0, op0=ALU.is_ge, op1=ALU.add, accum_out=sc_["cnt"][:]), reads=["affT", "bs_mid"], writes=["junk", "bs_cnt"])
                    S.add("dve", lambda e, cap=cap: e.tensor_scalar(out=sc_["ge"][:], in0=sc_["cnt"][:], scalar1=cap - 0.5, scalar2=None, op0=ALU.is_ge), reads=["bs_cnt"], writes=["bs_ge"])
                    S.add("dve", lambda e: e.tensor_tensor(out=sc_["d1"][:], in0=sc_["mid"][:], in1=sc_["lo"][:], op=ALU.subtract), reads=["bs_mid", "bs_lo"], writes=["bs_d1"])
                    S.add("dve", lambda e: e.tensor_tensor(out=sc_["d2"][:], in0=sc_["hi"][:], in1=sc_["mid"][:], op=ALU.subtract), reads=["bs_mid", "bs_hi"], writes=["bs_d2"])
                    S.add("dve", lambda e: e.scalar_tensor_tensor(out=sc_["lo"][:], in0=sc_["d1"][:], scalar=sc_["ge"][:, 0:1], in1=sc_["lo"][:], op0=ALU.mult, op1=ALU.add), reads=["bs_d1", "bs_ge", "bs_lo"], writes=["bs_lo"])
                    S.add("dve", lambda e: e.scalar_tensor_tensor(out=sc_["hi"][:], in0=sc_["d2"][:], scalar=sc_["ge"][:, 0:1], in1=sc_["mid"][:], op0=ALU.mult, op1=ALU.add), reads=["bs_d2", "bs_ge", "bs_mid"], writes=["bs_hi"])
                S.add("dve", lambda e, A=A, tk0=tk0, n=n: e.tensor_scalar(out=msk[:, tk0:tk0 + n], in0=A, scalar1=sc_["lo"][:, 0:1], scalar2=None, op0=ALU.is_ge), reads=["affT", "bs_lo"], writes=["msk"])
                S.add("dve", lambda e, tk0=tk0, n=n: e.tensor_tensor_scan(out=cum[:, tk0:tk0 + n], data0=ones[:, 0:n], data1=msk[:, tk0:tk0 + n], initial=0.0, op0=ALU.mult, op1=ALU.add), reads=["ones", "msk"], writes=["cum"])
                S.add("dve", lambda e, tk0=tk0, n=n: e.tensor_tensor(out=cum[:, tk0:tk0 + n], in0=cum[:, tk0:tk0 + n], in1=msk[:, tk0:tk0 + n], op=ALU.mult), reads=["cum", "msk"], writes=["cum"])
                S.add("pool", lambda e, tk0=tk0, n=n: e.tensor_scalar(out=posm[:, tk0:tk0 + n], in0=cum[:, tk0:tk0 + n], scalar1=-1.0, scalar2=None, op0=ALU.add), reads=["cum"], writes=["posm"])
                S.add("pool", lambda e, A=A, tk0=tk0, n=n: e.tensor_tensor(out=gv[:, tk0:tk0 + n], in0=A, in1=msk[:, tk0:tk0 + n], op=ALU.mult), reads=["affT", "msk"], writes=["gv"])
            for t4 in range(0, NT, 32):
                nt_ = min(32, NT - t4)
                pb = 2 + (t4 // 32) % 2
                for q in range(nt_):
                    S.add("pe", lambda e, t4=t4, q=q, pb=pb: e.transpose(ps[pb][:, q * 16:(q + 1) * 16], posm[:, (t4 + q) * 128:(t4 + q + 1) * 128], self.ident[0:16, 0:16]), reads=["posm", "ident"], writes=[f"ps{pb}"])
                S.add("act", lambda e, t4=t4, nt_=nt_, pb=pb: e.activation(out=ptok[:, t4:t4 + nt_, :].rearrange("p t e -> p (t e)"), in_=ps[pb][:, 0:nt_ * 16], func=AF.Copy), reads=[f"ps{pb}"], writes=["ptok"])
            S.add("sp", lambda e: e.dma_start(out=self.ptokd[:, :, :], in_=ptok[:]), reads=["ptok"], writes=["ptokd"], dma=True)
            pos16 = sbt("pos16", [16, T], FP16)
            gv16 = sbt("gv16", [16, T], BF16)
            S.add("dve", lambda e: e.tensor_copy(out=pos16[:], in_=posm[:]), reads=["posm"], writes=["pos16"])
            S.add("pool", lambda e: e.tensor_copy(out=gv16[:], in_=gv[:]), reads=["gv"], writes=["gv16"])
            S.add("sp", lambda e: e.dma_start(out=self.posd.rearrange("t e q -> e t q"), in_=pos16[:].rearrange("e (t q) -> e t q", q=128)), reads=["pos16"], writes=["posd"], dma=True)
            S.add("sp", lambda e: e.dma_start(out=self.gvd.rearrange("t e q -> e t q"), in_=gv16[:].rearrange("e (t q) -> e t q", q=128)), reads=["gv16"], writes=["gvd"], dma=True)
        self.S.barrier()
        if "stop_m2" in self.debug:
            return
        with ExitStack() as st:
            sbt = lambda name, shape, dt=F32: st.enter_context(nc.sbuf_tensor(self.uniq(name), shape, dt))
            h2l = sbt("h2l", [128, NT, D], BF16)
            ptok3 = sbt("ptok2", [128, NT, 16])
            iota = sbt("iota", [128, 512])
            selL = sbt("selL", [128, 32, 512], BF16)
            selC = sbt("selC", [128, 2, 32], BF16)
            wgu = [sbt(f"wgu{i}", [128, 8, 512], BF16) for i in range(4)]
            wdn = [sbt(f"wdn{i}", [128, 8, 512], BF16) for i in range(2)]
            xg = {"L": sbt("xgL", [128, 8, 512], BF16), "C": sbt("xgC", [128, 8, 32], BF16)}
            actT = {"L": sbt("actL", [128, 8, 512], BF16), "C": sbt("actC", [128, 8, 32], BF16)}
            sa = [sbt(f"sa{i}", [128, 512]) for i in range(2)]
            yt = [sbt(f"yt{i}", [128, D], BF16) for i in range(2)]
            for t8 in range(0, NT, 6):
                te = min(NT, t8 + 6)
                S.add("sp" if (t8 // 6) % 2 == 0 else "act", lambda e, t8=t8, te=te: e.dma_start(out=h2l[:, t8:te, :], in_=self.h2tok[t8 * 128:te * 128, :].rearrange("(t p) f -> p t f", p=128)), reads=["h2tok"], writes=["h2l"], dma=True)
            S.add("sp", lambda e: e.dma_start(out=ptok3[:], in_=self.ptokd[:, :, :]), reads=["ptokd"], writes=["ptok2"], dma=True)
            S.add("sp", lambda e: e.dma_start(out=iota[:], in_=self.cst["iota512"][:, :]), writes=["iota"], dma=True)
            pc = {"g": 0, "w": 0, "d": 0, "y": 0, "s": 0}
            for ex_ in range(16):
                for hh_ in range(2):
                    S.add("dve", lambda e, ex_=ex_, hh_=hh_: e.tensor_tensor(out=selL[:, hh_ * 16:(hh_ + 1) * 16, :], in0=iota[:].unsqueeze(1).broadcast_to([128, 16, 512]), in1=ptok3[:, 2 + hh_ * 16:18 + hh_ * 16, ex_:ex_ + 1].broadcast_to([128, 16, 512]), op=ALU.is_equal), reads=["iota", "ptok2"], writes=[f"selL{hh_}"])
                if with_ctx:
                    S.add("dve", lambda e, ex_=ex_: e.tensor_tensor(out=selC[:], in0=iota[:, 0:32].unsqueeze(1).broadcast_to([128, 2, 32]), in1=ptok3[:, 0:2, ex_:ex_ + 1].broadcast_to([128, 2, 32]), op=ALU.is_equal), reads=["iota", "ptok2"], writes=["selC"])
                for (nm, tk0, n, cap) in sets:
                    selt, sk = (selL, "selL") if nm == "L" else (selC, "selC")
                    skf = (lambda tt_: f"selL{tt_ // 16}") if nm == "L" else (lambda tt_: "selC")
                    tile0 = tk0 // 128
                    for fc in range(8):
                        pb = pc["g"] % 2
                        pc["g"] += 1
                        ntl = n // 128
                        for tt in range(ntl):
                            S.add("pe", lambda e, fc=fc, tt=tt, pb=pb, selt=selt, tile0=tile0, cap=cap, ntl=ntl: e.matmul(ps[pb][:, 0:cap], lhsT=h2l[:, tile0 + tt, fc * 128:(fc + 1) * 128], rhs=selt[:, tt, :], start=(tt == 0), stop=(tt == ntl - 1)), reads=["h2l", skf(tt)], writes=[f"ps{pb}"])
                        S.add("act", lambda e, fc=fc, pb=pb, nm=nm, cap=cap: e.activation(out=xg[nm][:, fc, :], in_=ps[pb][:, 0:cap], func=AF.Copy), reads=[f"ps{pb}"], writes=["xg" + nm])
                for hf in range(2):
                    wts = []
                    for wi, wsrc in enumerate([self.w_gate_e, self.w_up_e]):
                        wb = pc["w"] % 4
                        pc["w"] += 1
                        S.add("pool", lambda e, wb=wb, wsrc=wsrc, ex_=ex_, hf=hf: e.dma_start(out=wgu[wb][:], in_=wsrc[l, ex_].rearrange("(kc p) n -> p kc n", p=128)[:, :, hf * 512:(hf + 1) * 512]), writes=[f"wgu{wb}"], dma=True)
                        wts.append(wb)
                    for fq in range(4):
                        fpc = hf * 4 + fq
                        for (nm, tk0, n, cap) in sets:
                            for wi in range(2):
                                pb = 2 + wi + 2 * (pc["s"] % 2)
                                for kc in range(8):
                                    S.add("pe", lambda e, wi=wi, kc=kc, fq=fq, pb=pb, nm=nm, cap=cap, wts=tuple(wts): e.matmul(ps[pb][:, 0:cap], lhsT=wgu[wts[wi]][:, kc, fq * 128:(fq + 1) * 128], rhs=xg[nm][:, kc, :], start=(kc == 0), stop=(kc == 7)), reads=[f"wgu{wts[wi]}", "xg" + nm], writes=[f"ps{pb}"])
                            pa_ = 2 + 2 * (pc["s"] % 2)
                            si = pc["s"] % 2
                            pc["s"] += 1
                            S.add("act", lambda e, pa_=pa_, si=si, cap=cap: e.activation(out=sa[si][:, 0:cap], in_=ps[pa_][:, 0:cap], func=AF.Silu), reads=[f"ps{pa_}"], writes=[f"sa{si}"])
                            S.add("dve", lambda e, pa_=pa_, si=si, cap=cap, nm=nm, fpc=fpc: e.tensor_tensor(out=actT[nm][:, fpc, :], in0=sa[si][:, 0:cap], in1=ps[pa_ + 1][:, 0:cap], op=ALU.mult), reads=[f"sa{si}", f"ps{pa_ + 1}"], writes=["act" + nm])
                wds = []
                for hf in range(2):
                    wb = pc["d"] % 2
                    pc["d"] += 1
                    S.add("pool", lambda e, wb=wb, ex_=ex_, hf=hf: e.dma_start(out=wdn[wb][:], in_=self.w_down_e[l, ex_].rearrange("(kc p) n -> p kc n", p=128)[:, :, hf * 512:(hf + 1) * 512]), writes=[f"wdn{wb}"], dma=True)
                    wds.append(wb)
                for (nm, tk0, n, cap) in sets:
                    njc = max(1, cap // 128)
                    M = min(cap, 128)
                    for jc in range(njc):
                        yi = pc["y"] % 2
                        pc["y"] += 1
                        for hf in range(2):
                            pb = hf
                            for kc in range(8):
                                S.add("pe", lambda e, kc=kc, jc=jc, hf=hf, pb=pb, nm=nm, M=M, wds=tuple(wds): e.matmul(ps[pb][0:M, :], lhsT=actT[nm][:, kc, jc * 128:jc * 128 + M], rhs=wdn[wds[hf]][:, kc, :], start=(kc == 0), stop=(kc == 7)), reads=["act" + nm, f"wdn{wds[hf]}"], writes=[f"ps{pb}"])
                            if hf == 0:
                                S.add("act", lambda e, yi=yi, M=M: e.activation(out=yt[yi][0:M, 0:512], in_=ps[0][0:M, :], func=AF.Copy), reads=["ps0"], writes=[f"yt{yi}"])
                            else:
                                S.add("dve", lambda e, yi=yi, M=M: e.tensor_copy(out=yt[yi][0:M, 512:1024], in_=ps[1][0:M, :]), reads=["ps1"], writes=[f"yt{yi}"])
                        if nm == "L":
                            S.add("sp", lambda e, yi=yi, ex_=ex_, jc=jc: e.dma_start(out=self.ygL[ex_, jc * 128:(jc + 1) * 128, :], in_=yt[yi][:, :]), reads=[f"yt{yi}"], writes=["ygL"], dma=True)
                        else:
                            S.add("sp", lambda e, yi=yi, ex_=ex_: e.dma_start(out=self.ygC[ex_, :, :], in_=yt[yi][0:32, :]), reads=[f"yt{yi}"], writes=["ygC"], dma=True)
        self.S.barrier()
        if "stop_m3" in self.debug:
            return
        with ExitStack() as st:
            sbt = lambda name, shape, dt=F32: st.enter_context(nc.sbuf_tensor(self.uniq(name), shape, dt))
            ygh = sbt("ygh", [128, 64, 512], BF16)
            ygc = sbt("ygc", [32, 16, 512], BF16)
            iota4 = sbt("iota4", [128, 4])
            pbt = [sbt(f"pbt{i}", [128, 16, 128], FP16) for i in range(2)]
            gvt = [sbt(f"gvt{i}", [128, 16, 128], BF16) for i in range(2)]
            selT = [sbt(f"selT{i}", [128, 16, 4, 128], BF16) for i in range(2)]
            fh = [sbt(f"fh{i}", [128, 512]) for i in range(2)]
            f0 = [sbt(f"f0{i}", [128, 512]) for i in range(2)]
            xt = [sbt(f"cxt{i}", [128, D]) for i in range(2)]
            rt = [sbt(f"crt{i}", [128, D]) for i in range(2)]
            bc = {nm: sbt("cbc_" + nm, [128, D]) for nm in ["g2_0", "g2_1", "lng", "lnb"]}
            bst = sbt("cbst", [128, 2, 6])
            mv = sbt("cmv", [128, 2])
            rsd = sbt("crsd", [128, 1])
            S.add("sp", lambda e: e.dma_start(out=iota4[:], in_=self.cst["iota4"][:, :]), writes=["iota4"], dma=True)
            for s_ in range(2):
                self.bcast_mod("sp", bc[f"g2_{s_}"], 5, s_, f"cbc_g2_{s_}")
            S.add("sp", lambda e: e.dma_start(out=bc["lng"][:], in_=self.ln2_g[l:l + 1, :].partition_broadcast(128)), writes=["cbc_lng"], dma=True)
            S.add("sp", lambda e: e.dma_start(out=bc["lnb"][:], in_=self.ln2_b[l:l + 1, :].partition_broadcast(128)), writes=["cbc_lnb"], dma=True)
            tcnt = 0
            pend = []
            for (nm, tk0, n, cap) in sets:
                tile0 = tk0 // 128
                ntl = n // 128
                njc = max(1, cap // 128)
                M = min(cap, 128)
                s_ = 0 if nm == "L" else 1
                for hf in range(2):
                    if nm == "C":
                        S.add("sp", lambda e, hf=hf: e.dma_start(out=ygc[:], in_=self.ygC[:, :, hf * 512:(hf + 1) * 512].rearrange("e j f -> j e f")), reads=["ygC"], writes=["ygc"], dma=True)
                    if nm == "L":
                        for e4 in range(0, 16, 4):
                            S.add("sp" if (e4 // 4) % 2 == 0 else "act", lambda e, e4=e4, hf=hf: e.dma_start(out=ygh[:, e4 * 4:(e4 + 4) * 4, :], in_=self.ygL[e4:e4 + 4, :, hf * 512:(hf + 1) * 512].rearrange("e (jc p) f -> p (e jc) f", p=128)), reads=["ygL"], writes=["ygh"], dma=True)
                    for tt in range(ntl):
                        tile = tile0 + tt
                        tok = tile * 128
                        b2 = tcnt % 2
                        tcnt += 1
                        pb_, gv_, st_ = pbt[b2], gvt[b2], selT[b2]
                        S.add("sp", lambda e, pb_=pb_, tile=tile: e.dma_start(out=pb_[:].rearrange("p e q -> p (e q)"), in_=self.posd[tile].rearrange("e q -> (e q)").partition_broadcast(128)), reads=["posd"], writes=[f"pbt{b2}"], dma=True)
                        S.add("act", lambda e, gv_=gv_, tile=tile: e.dma_start(out=gv_[:].rearrange("p e q -> p (e q)"), in_=self.gvd[tile].rearrange("e q -> (e q)").partition_broadcast(128)), reads=["gvd"], writes=[f"gvt{b2}"], dma=True)
                        for jc in range(njc):
                            S.add("dve", lambda e, pb_=pb_, gv_=gv_, st_=st_, jc=jc: e.scalar_tensor_tensor(out=st_[:, :, jc, :], in0=pb_[:], scalar=iota4[:, jc:jc + 1], in1=gv_[:], op0=ALU.is_equal, op1=ALU.mult), reads=[f"pbt{b2}", f"gvt{b2}", "iota4"], writes=[f"selT{b2}"])
                        pbk = 4 + b2
                        nmm = 16 * njc
                        for i_, (ex_, jc) in enumerate([(a_, b_) for a_ in range(16) for b_ in range(njc)]):
                            if nm == "L":
                                S.add("pe", lambda e, st_=st_, ex_=ex_, jc=jc, i_=i_, pbk=pbk, nmm=nmm: e.matmul(ps[pbk][:, :], lhsT=st_[:, ex_, jc, :], rhs=ygh[:, ex_ * 4 + jc, :], start=(i_ == 0), stop=(i_ == nmm - 1)), reads=[f"selT{b2}", "ygh"], writes=[f"ps{pbk}"])
                            else:
                                S.add("pe", lambda e, st_=st_, ex_=ex_, i_=i_, pbk=pbk, nmm=nmm, hf=hf: e.matmul(ps[pbk][:, :], lhsT=st_[0:32, ex_, 0, :], rhs=ygc[:, ex_, :], start=(i_ == 0), stop=(i_ == nmm - 1)), reads=[f"selT{b2}", "ygc"], writes=[f"ps{pbk}"])

                        def post(b2=b2, tok=tok, pbk=pbk, hf=hf, s_=s_):
                            if hf == 0:
                                S.add("act", lambda e: e.activation(out=fh[b2][:], in_=ps[pbk][:, :], func=AF.Copy), reads=[f"ps{pbk}"], writes=[f"fh{b2}"])
                                S.add("sp", lambda e: e.dma_start(out=self.f0d[tok:tok + 128, :], in_=fh[b2][:]), reads=[f"fh{b2}"], writes=["f0d"], dma=True)
                                return
                            xb, xk = xt[b2], f"cxt{b2}"
                            rb, rk = rt[b2], f"crt{b2}"
                            S.add("sp", lambda e: e.dma_start(out=xb[:], in_=self.xres[tok:tok + 128, :]), reads=["xres"], writes=[xk], dma=True)
                            S.add("act", lambda e: e.dma_start(out=f0[b2][:], in_=self.f0d[tok:tok + 128, :]), reads=["f0d"], writes=[f"f0{b2}"], dma=True)
                            S.add("dve", lambda e: e.tensor_tensor(out=rb[:, 0:512], in0=f0[b2][:], in1=bc[f"g2_{s_}"][:, 0:512], op=ALU.mult), reads=[f"f0{b2}", f"cbc_g2_{s_}"], writes=[rk])
                            S.add("dve", lambda e: e.tensor_tensor(out=rb[:, 512:1024], in0=ps[pbk][:, :], in1=bc[f"g2_{s_}"][:, 512:1024], op=ALU.mult), reads=[f"ps{pbk}", f"cbc_g2_{s_}"], writes=[rk])
                            S.add("dve", lambda e: e.scalar_tensor_tensor(out=rb[:], in0=xb[:], scalar=ALPHA, in1=rb[:], op0=ALU.mult, op1=ALU.add), reads=[xk, rk], writes=[rk])
                            self.layer_norm_tile(rb, rk, xb, xk, bc["lng"], "cbc_lng", bc["lnb"], "cbc_lnb", bst, mv, rsd)
                            S.add("sp", lambda e: e.dma_start(out=self.xres[tok:tok + 128, :], in_=xb[:]), reads=[xk], writes=["xres"], dma=True)
                        if pend:
                            pend.pop()()
                        pend.append(post)
                    while pend:
                        pend.pop()()
        self.S.barrier()


    def phase_gdn(self, l):
        nc, S = self.nc, self.S
        ps = self.ps
        with_ctx = l < DEPTH - 1
        NCH = T // 64
        with ExitStack() as st:
            sbt = lambda name, shape, dt=F32: st.enter_context(nc.sbuf_tensor(self.uniq(name), shape, dt))
            cmask = sbt("cmask", [4, T])
            ones4 = sbt("ones4", [4, T])
            prm = sbt("prm", [4, 4])
            S.add("sp", lambda e: e.dma_start(out=cmask[:], in_=self.cst["chunkmask"][:, :]), writes=["cmask"], dma=True)
            S.add("pool", lambda e: e.memset(ones4[:], 1.0), writes=["ones4"])
            for d_ in range(2):
                S.add("sp", lambda e, d_=d_: e.dma_start(out=prm[:, d_:d_ + 1], in_=self.a_log[l, d_ * 4:(d_ + 1) * 4].rearrange("(h o) -> h o", o=1)), writes=["prm"], dma=True)
                S.add("sp", lambda e, d_=d_: e.dma_start(out=prm[:, 2 + d_:3 + d_], in_=self.dt_bias[l, d_ * 4:(d_ + 1) * 4].rearrange("(h o) -> h o", o=1)), writes=["prm"], dma=True)
            S.add("act", lambda e: e.activation(out=prm[:, 0:2], in_=prm[:, 0:2], func=AF.Exp), reads=["prm"], writes=["prm"])
            S.add("dve", lambda e: e.tensor_scalar(out=prm[:, 0:2], in0=prm[:, 0:2], scalar1=-1.0, scalar2=None, op0=ALU.mult), reads=["prm"], writes=["prm"])
            for d_ in range(2):
                a4 = sbt(f"a4_{d_}", [4, T])
                b4 = sbt(f"b4_{d_}", [4, T])
                g4 = sbt(f"g4_{d_}", [4, T])
                F4 = sbt(f"F4_{d_}", [4, T])
                w4 = sbt(f"w4_{d_}", [4, T])
                ak, bk, gk, Fk, wk = f"a4{d_}", f"b4{d_}", f"g4{d_}", f"F4{d_}", f"w4{d_}"
                S.add("sp", lambda e, a4=a4, d_=d_: e.dma_start(out=a4[:], in_=self.ag16[d_ * 8:d_ * 8 + 4, :]), reads=["ag16"], writes=[ak], dma=True)
                S.add("sp", lambda e, b4=b4, d_=d_: e.dma_start(out=b4[:], in_=self.ag16[d_ * 8 + 4:d_ * 8 + 8, :]), reads=["ag16"], writes=[bk], dma=True)
                S.add("act", lambda e, a4=a4, d_=d_: e.activation(out=a4[:], in_=a4[:], func=AF.Exp, bias=prm[:, 2 + d_:3 + d_], scale=1.0), reads=[ak, "prm"], writes=[ak])
                S.add("act", lambda e, a4=a4: e.activation(out=a4[:], in_=a4[:], func=AF.Ln, bias=1.0, scale=1.0), reads=[ak], writes=[ak])
                S.add("dve", lambda e, a4=a4, g4=g4, d_=d_: e.tensor_scalar(out=g4[:], in0=a4[:], scalar1=prm[:, d_:d_ + 1], scalar2=None, op0=ALU.mult), reads=[ak, "prm"], writes=[gk])
                S.add("act", lambda e, b4=b4: e.activation(out=b4[:], in_=b4[:], func=AF.Sigmoid), reads=[bk], writes=[bk])
                S.add("act", lambda e, b4=b4, a4=a4: e.activation(out=a4[:], in_=b4[:], func=AF.Ln), reads=[bk, ak], writes=[ak])
                S.add("dve", lambda e, F4=F4, g4=g4: e.tensor_tensor_scan(out=F4[:], data0=cmask[:], data1=g4[:], initial=0.0, op0=ALU.mult, op1=ALU.add), reads=["cmask", gk], writes=[Fk])
                F3 = F4[:].rearrange("h (n c) -> h n c", c=64)
                if d_ == 1:
                    S.add("dve", lambda e, w4=w4, F3=F3: e.tensor_tensor(out=w4[:].rearrange("h (n c) -> h n c", c=64), in0=F3[:, :, 63:64].broadcast_to([4, NCH, 64]), in1=F3, op=ALU.subtract), reads=[Fk], writes=[wk])
                    S.add("dve", lambda e, w4=w4, g4=g4, F4=F4: e.tensor_tensor(out=F4[:], in0=w4[:], in1=g4[:], op=ALU.add), reads=[wk, gk, Fk], writes=[Fk])
                    tot_ap = F3[:, :, 0:1]
                else:
                    tot_ap = F3[:, :, 63:64]
                G4 = F4
                Rd = self.Rd
                p0 = d_ * 4
                S.add("dve", lambda e, w4=w4, G4=G4, a4=a4: e.tensor_tensor(out=w4[:], in0=G4[:], in1=a4[:], op=ALU.add), reads=[Fk, ak, wk], writes=[wk])
                S.add("sp", lambda e, w4=w4, p0=p0: e.dma_start(out=Rd[0, 0, p0:p0 + 4, :], in_=w4[:]), reads=[wk], writes=["Rd"], dma=True)
                S.add("sp", lambda e, p0=p0: e.dma_start(out=Rd[0, 1, p0:p0 + 4, :], in_=ones4[:]), reads=["ones4"], writes=["Rd"], dma=True)
                S.add("sp", lambda e, p0=p0: e.dma_start(out=Rd[1, 0, p0:p0 + 4, :], in_=ones4[:]), reads=["ones4"], writes=["Rd"], dma=True)
                S.add("sp", lambda e, p0=p0, G4=G4: e.dma_start(out=Rd[2, 0, p0:p0 + 4, :], in_=G4[:]), reads=[Fk], writes=["Rd"], dma=True)
                S.add("sp", lambda e, p0=p0: e.dma_start(out=Rd[2, 1, p0:p0 + 4, :], in_=ones4[:]), reads=["ones4"], writes=["Rd"], dma=True)
                S.add("dve", lambda e, w4=w4, G4=G4: e.tensor_scalar(out=w4[:], in0=G4[:], scalar1=-1.0, scalar2=None, op0=ALU.mult), reads=[Fk, wk], writes=[wk])
                S.add("sp", lambda e, w4=w4, p0=p0: e.dma_start(out=Rd[1, 1, p0:p0 + 4, :], in_=w4[:]), reads=[wk], writes=["Rd"], dma=True)
                TSd = self.TSd
                S.add("sp", lambda e, b4=b4, p0=p0: e.dma_start(out=TSd[16 + p0:16 + p0 + 4, :], in_=b4[:]), reads=[bk], writes=["TSd"], dma=True)
                S.add("sp", lambda e, g4=g4, p0=p0: e.dma_start(out=TSd[32 + p0:32 + p0 + 4, :], in_=g4[:]), reads=[gk], writes=["TSd"], dma=True)
                S.add("dve", lambda e, w4=w4, tot_ap=tot_ap, G4=G4: e.tensor_tensor(out=w4[:].rearrange("h (n c) -> h n c", c=64), in0=tot_ap.broadcast_to([4, NCH, 64]), in1=G4[:].rearrange("h (n c) -> h n c", c=64), op=ALU.subtract), reads=[Fk, wk], writes=[wk])
                S.add("act", lambda e, w4=w4: e.activation(out=w4[:], in_=w4[:], func=AF.Exp), reads=[wk], writes=[wk])
                S.add("sp", lambda e, w4=w4, p0=p0: e.dma_start(out=TSd[24 + p0:24 + p0 + 4, :], in_=w4[:]), reads=[wk], writes=["TSd"], dma=True)
                S.add("act", lambda e, g4=g4, G4=G4: e.activation(out=g4[:], in_=G4[:], func=AF.Exp), reads=[Fk, gk], writes=[gk])
                S.add("sp", lambda e, g4=g4, p0=p0: e.dma_start(out=TSd[0 + p0:0 + p0 + 4, :], in_=g4[:]), reads=[gk], writes=["TSd"], dma=True)
                S.add("dve", lambda e, a4=a4, g4=g4, b4=b4: e.tensor_tensor(out=a4[:], in0=g4[:], in1=b4[:], op=ALU.mult), reads=[gk, bk, ak], writes=[ak])
                S.add("sp", lambda e, a4=a4, p0=p0: e.dma_start(out=TSd[8 + p0:8 + p0 + 4, :], in_=a4[:]), reads=[ak], writes=["TSd"], dma=True)
        self.S.barrier()
        with ExitStack() as st:
            sbt = lambda name, shape, dt=F32: st.enter_context(nc.sbuf_tensor(self.uniq(name), shape, dt))
            cw = sbt("cw", [128, 6, 5])
            bones = sbt("gbones", [128, 128])
            S.add("sp", lambda e: e.dma_start(out=cw[:], in_=self.conv_aT[l]), writes=["cw"], dma=True)
            S.add("sp", lambda e: e.dma_start(out=bones[:], in_=self.cst["bones64"][:, :]), writes=["gbones"], dma=True)
            ub = [sbt(f"ub{i}", [128, T + 8]) for i in range(2)]
            yb = [sbt(f"yb{i}", [128, T]) for i in range(2)]
            sq = [sbt(f"gsq{i}", [128, 512]) for i in range(2)]
            rs = [sbt(f"grs{i}", [128, 512]) for i in range(2)]
            tkb = [sbt(f"tkb{i}", [128, 128]) for i in range(2)]
            for i in range(2):
                S.add("pool", lambda e, i=i: e.memset(ub[i][:], 0.0), writes=[f"ub{i}"])
            tcn = 0
            for c_ in range(6):
                u, uk = ub[c_ % 2], f"ub{c_ % 2}"
                y, yk = yb[c_ % 2], f"yb{c_ % 2}"
                S.add("sp", lambda e, u=u, c_=c_: e.dma_start(out=u[:, 2:2 + NCTX], in_=self.aqkv[c_ * 128:(c_ + 1) * 128, 0:NCTX]), reads=["aqkv"], writes=[uk], dma=True)
                S.add("act", lambda e, u=u, c_=c_: e.dma_start(out=u[:, 6 + NCTX:6 + T], in_=self.aqkv[c_ * 128:(c_ + 1) * 128, NCTX:T]), reads=["aqkv"], writes=[uk], dma=True)
                for (o0, u0, n) in [(0, 0, NCTX), (NCTX, NCTX + 4, NLAT)]:
                    for j in range(5):
                        if j == 0:
                            S.add("dve", lambda e, y=y, u=u, o0=o0, u0=u0, n=n, c_=c_: e.tensor_scalar(out=y[:, o0:o0 + n], in0=u[:, u0:u0 + n], scalar1=cw[:, c_, 0:1], scalar2=None, op0=ALU.mult), reads=[uk, "cw"], writes=[yk])
                        else:
                            S.add("dve", lambda e, y=y, u=u, o0=o0, u0=u0, n=n, c_=c_, j=j: e.scalar_tensor_tensor(out=y[:, o0:o0 + n], in0=u[:, u0 + j:u0 + j + n], scalar=cw[:, c_, j:j + 1], in1=y[:, o0:o0 + n], op0=ALU.mult, op1=ALU.add), reads=[uk, "cw", yk], writes=[yk])
                S.add("act", lambda e, y=y: e.activation(out=y[:], in_=y[:], func=AF.Silu), reads=[yk], writes=[yk])
                if c_ < 4:
                    for (t0, W) in GROUPS:
                        b2 = tcn % 2
                        tcn += 1
                        S.add("act", lambda e, y=y, t0=t0, W=W, b2=b2: e.activation(out=sq[b2][:, 0:W], in_=y[:, t0:t0 + W], func=AF.Square), reads=[yk], writes=[f"gsq{b2}"])
                        S.add("pe", lambda e, W=W, b2=b2: e.matmul(ps[b2][:, 0:W], lhsT=bones[:], rhs=sq[b2][:, 0:W], start=True, stop=True), reads=[f"gsq{b2}", "gbones"], writes=[f"ps{b2}"])
                        S.add("act", lambda e, W=W, b2=b2: e.activation(out=rs[b2][:, 0:W], in_=ps[b2][:, 0:W], func=AF.Sqrt, bias=EPS, scale=1.0), reads=[f"ps{b2}"], writes=[f"grs{b2}"])
                        S.add("dve", lambda e, W=W, b2=b2: e.reciprocal(out=rs[b2][:, 0:W], in_=rs[b2][:, 0:W]), reads=[f"grs{b2}"], writes=[f"grs{b2}"])
                        qs = 0.125 if c_ < 2 else 1.0
                        S.add("dve", lambda e, y=y, t0=t0, W=W, b2=b2, qs=qs: e.scalar_tensor_tensor(out=y[:, t0:t0 + W], in0=y[:, t0:t0 + W], scalar=qs, in1=rs[b2][:, 0:W], op0=ALU.mult, op1=ALU.mult), reads=[yk, f"grs{b2}"], writes=[yk])
                    kind = c_ // 2
                    S.add("sp", lambda e, y=y, kind=kind, c_=c_: e.dma_start(out=self.qkhd[kind, (c_ % 2) * 2:(c_ % 2) * 2 + 2].rearrange("h d t -> (h d) t"), in_=y[:]), reads=[yk], writes=["qkhd"], dma=True)
                if c_ >= 2:
                    kv = 0 if c_ < 4 else 1
                    col0 = kv * 256 + (c_ % 2) * 128
                    for tt in range(NT):
                        pb = 2 + tt % 2
                        tb_ = tkb[tt % 2]
                        S.add("pe", lambda e, y=y, tt=tt, pb=pb: e.transpose(ps[pb][:, 0:128], y[:, tt * 128:(tt + 1) * 128], self.ident[:]), reads=[yk, "ident"], writes=[f"ps{pb}"])
                        S.add("act", lambda e, tb_=tb_, pb=pb: e.activation(out=tb_[:], in_=ps[pb][:, 0:128], func=AF.Copy), reads=[f"ps{pb}"], writes=[f"tkb{tt % 2}"])
                        S.add("sp", lambda e, tb_=tb_, tt=tt, col0=col0: e.dma_start(out=self.kvtok[tt * 128:(tt + 1) * 128, col0:col0 + 128], in_=tb_[:]), reads=[f"tkb{tt % 2}"], writes=["kvtok"], dma=True)
        self.S.barrier()
        order_b = [3, 2, 1, 0] + list(range(NCH - 1, 3, -1))
        with ExitStack() as st:
            sbt = lambda name, shape, dt=F32: st.enter_context(nc.sbuf_tensor(self.uniq(name), shape, dt))
            TS = sbt("TS", [40, T])
            S.add("sp", lambda e: e.dma_start(out=TS[:], in_=self.TSd[:, :]), reads=["TSd"], writes=["TS"], dma=True)
            cm = {}
            for nm in ["mask_a", "mask_at", "mask_pt", "eye8"]:
                cm[nm] = sbt("g_" + nm, [64, 512])
                S.add("sp", lambda e, nm=nm: e.dma_start(out=cm[nm][:], in_=self.cst[nm][:, :]), writes=["g_" + nm], dma=True)
            ones64 = sbt("g_ones64", [128, 64])
            S.add("pool", lambda e: e.memset(ones64[:], 0.0), writes=["g_ones64"])
            S.add("pool", lambda e: e.memset(ones64[0:64, :], 1.0), writes=["g_ones64"])
            class TV:
                def __init__(self, t):
                    self.t = t
                def __getitem__(self, idx):
                    if not isinstance(idx, tuple):
                        idx = (idx,)
                    return self.t[(slice(0, 64),) + tuple(idx[1:])]
                def k(self, *idx):
                    return self.t[(slice(None),) + tuple(idx)]

            def ztile(name, shape):
                t = sbt(name, [128] + list(shape[1:]))
                S.add("pool", lambda e, t=t: e.memset(t[:], 0.0), writes=[name])
                return TV(t)
            NS = 4
            rows_t = [ztile(f"rows{i}", [2, 3, 8, 64]) for i in range(NS)]
            qk_t = [ztile(f"qkt{i}", [64, 2, 8, 64]) for i in range(NS)]
            kv_t = [sbt(f"kvt{i}", [64, 2, 512]) for i in range(NS)]
            scal = [ztile(f"scal{i}", [64, 80]) for i in range(NS)]
            GLs = [sbt(f"GLs{i}", [64, 8]) for i in range(NS)]
            def t8(name, n=1):
                return [ztile(f"{name}{i}", [64, 512]) for i in range(n)]
            rw, rv = t8("rw", 2), t8("rv", 2)
            kd, PT, wT, uc = t8("kd", NS), t8("PT", NS), t8("wT", NS), t8("uc", NS)
            tA = [t8(f"tA{j}_", 3) for j in range(2)]
            Qa, QTa, Pm = [t8(f"Qa{j}_", 2) for j in range(2)], [t8(f"QTa{j}_", 2) for j in range(2)], [t8(f"Pm{j}_", 2) for j in range(2)]
            uu, tq, oo = t8("uu", 2), t8("tq", 2), t8("oo", 2)
            Sst = t8("Sst", 2)
            S.add("dve", lambda e: e.memset(Sst[0][:], 0.0), writes=["Sst0"])
            v3 = lambda ap: ap.rearrange("c (p e) -> c p e", p=8)
            kkS, qkS = t8("kkS", 2), t8("qkS", 2)

            def mk_nb(banks):
                st_ = {"i": 0}

                def f():
                    b = banks[st_["i"] % len(banks)]
                    st_["i"] += 1
                    return b
                return f
            nb_intra = [mk_nb([0, 1, 2]), mk_nb([3, 4, 5])]
            nb_scan = mk_nb([6, 7])

            def col(kind, d_):
                return d_ * 40 + kind * 8 + d_ * 4

            def intra_gen(step):
                cf, cb = step, order_b[step]
                tok = [cf * 64, cb * 64]
                sl = step % NS
                i2 = step % 2
                rt_, qt_, kt_ = rows_t[sl], qk_t[sl], kv_t[sl]
                rk_, qk_, kk_ = f"rows{sl}", f"qkt{sl}", f"kvt{sl}"
                nb = nb_intra[i2]
                for d_ in range(2):
                    t0 = tok[d_]
                    for k3 in range(3):
                        S.add("sp" if d_ == 0 else "act", lambda e, rt_=rt_, k3=k3, d_=d_, t0=t0: e.dma_start(out=rt_.t[0:2, k3, d_ * 4:d_ * 4 + 4, :], in_=self.Rd[k3, :, d_ * 4:d_ * 4 + 4, t0:t0 + 64]), reads=["Rd"], writes=[rk_], dma=True)
                    for kind in range(2):
                        S.add("sp" if kind == 0 else "act", lambda e, qt_=qt_, kind=kind, d_=d_, t0=t0: e.dma_start(out=qt_[:, kind, d_ * 4:d_ * 4 + 4, :], in_=self.qkhd[kind, :, :, t0:t0 + 64].rearrange("h d t -> d h t")), reads=["qkhd"], writes=[qk_], dma=True)
                    S.add("sp", lambda e, kt_=kt_, d_=d_, t0=t0: e.dma_start(out=kt_[:, d_, :], in_=self.kvtok[t0:t0 + 64, :]), reads=["kvtok"], writes=[kk_], dma=True)
                yield
                bs = nb()
                for d_ in range(2):
                    S.add("pe", lambda e, d_=d_, bs=bs, t0=tok[d_]: e.transpose(ps[bs][0:64, d_ * 40:(d_ + 1) * 40], TS[:, t0:t0 + 64], self.ident[0:40, 0:40]), reads=["TS", "ident"], writes=[f"ps{bs}"])
                sc = scal[sl]
                sck = f"scal{sl}"
                S.add("act", lambda e, sc=sc, bs=bs: e.activation(out=sc[:], in_=ps[bs][0:64, 0:80], func=AF.Copy), reads=[f"ps{bs}"], writes=[sck])
                yield
                for d_ in range(2):
                    c0 = col(4, d_)
                    S.add("pe", lambda e, d_=d_, bs=bs, c0=c0, sc=sc: e.matmul(ps[bs][0:64, 96 + d_ * 4:100 + d_ * 4], lhsT=ones64[:], rhs=sc.k(slice(c0, c0 + 4)), start=True, stop=True), reads=[sck, "g_ones64"], writes=[f"ps{bs}"])
                gl = GLs[sl]
                glk = f"GLs{sl}"
                S.add("act", lambda e, gl=gl, bs=bs: e.activation(out=gl[:], in_=ps[bs][0:64, 96:104], func=AF.Exp), reads=[f"ps{bs}"], writes=[glk])
                rw_, rv_, kd_ = rw[i2], rv[i2], kd[sl]
                for d_ in range(2):
                    bcol = lambda kind, d_=d_, sc=sc: sc[:, col(kind, d_):col(kind, d_) + 4].unsqueeze(2).broadcast_to([64, 4, 64])
                    kview = kt_[:, d_, 0:256].rearrange("c (h e) -> c h e", h=4)
                    vview = kt_[:, d_, 256:512].rearrange("c (h e) -> c h e", h=4)
                    S.add("dve", lambda e, rw_=rw_, d_=d_, kview=kview, bc=bcol(1): e.tensor_tensor(out=v3(rw_[:])[:, d_ * 4:d_ * 4 + 4, :], in0=kview, in1=bc, op=ALU.mult), reads=[kk_, sck], writes=[f"rw{i2}"])
                    S.add("pool", lambda e, rv_=rv_, d_=d_, vview=vview, bc=bcol(2): e.tensor_tensor(out=v3(rv_[:])[:, d_ * 4:d_ * 4 + 4, :], in0=vview, in1=bc, op=ALU.mult), reads=[kk_, sck], writes=[f"rv{i2}"])
                    S.add("pool", lambda e, kd_=kd_, d_=d_, kview=kview, bc=bcol(3): e.tensor_tensor(out=v3(kd_[:])[:, d_ * 4:d_ * 4 + 4, :], in0=kview, in1=bc, op=ALU.mult), reads=[kk_, sck], writes=[f"kd{sl}"])
                bKK, bQK = nb(), nb()
                for p in range(8):
                    kT_p = qt_.k(1, p, slice(None))
                    qT_p = qt_.k(0, p, slice(None))
                    cs = slice(p * 64, (p + 1) * 64)
                    S.add("pe", lambda e, kT_p=kT_p, cs=cs, bKK=bKK: e.matmul(ps[bKK][0:64, cs], lhsT=kT_p, rhs=kT_p, start=True, stop=True), reads=[qk_], writes=[f"ps{bKK}"])
                    S.add("pe", lambda e, kT_p=kT_p, qT_p=qT_p, cs=cs, bQK=bQK: e.matmul(ps[bQK][0:64, cs], lhsT=kT_p, rhs=qT_p, start=True, stop=True), reads=[qk_], writes=[f"ps{bQK}"])
                kk_s, qk_s = kkS[i2], qkS[i2]
                S.add("act", lambda e, kk_s=kk_s, bKK=bKK: e.activation(out=kk_s[:], in_=ps[bKK][0:64, :], func=AF.Copy), reads=[f"ps{bKK}"], writes=[f"kkS{i2}"])
                S.add("dve", lambda e, qk_s=qk_s, bQK=bQK: e.tensor_copy(out=qk_s[:], in_=ps[bQK][0:64, :]), reads=[f"ps{bQK}"], writes=[f"qkS{i2}"])
                yield
                Qs, QTs, Ps, tAs = Qa[i2], QTa[i2], Pm[i2], tA[i2]
                qn, qtn, pn, tn = f"Qa{i2}_", f"QTa{i2}_", f"Pm{i2}_", f"tA{i2}_"
                PT_ = PT[sl]
                for ti, (ra, rb_, mk, gsrc, gk_, dst, dk, neg) in enumerate([(0, 1, "mask_a", kk_s, f"kkS{i2}", QTs[0], qtn + "0", -1.0), (1, 0, "mask_at", kk_s, f"kkS{i2}", Qs[0], qn + "0", -1.0), (1, 2, "mask_pt", qk_s, f"qkS{i2}", PT_, f"PT{sl}", 1.0)]):
                    bD = nb()
                    for p in range(8):
                        cs = slice(p * 64, (p + 1) * 64)
                        S.add("pe", lambda e, la=rt_.k(ra, p, slice(None)), rb2=rt_.k(rb_, p, slice(None)), cs=cs, bD=bD: e.matmul(ps[bD][0:64, cs], lhsT=la, rhs=rb2, start=True, stop=True), reads=[rk_], writes=[f"ps{bD}"])
                    tt_ = tAs[ti]
                    S.add("dve", lambda e, tt_=tt_, bD=bD, mk=mk: e.tensor_tensor(out=tt_[:], in0=ps[bD][0:64, :], in1=cm[mk][:], op=ALU.add), reads=[f"ps{bD}", "g_" + mk], writes=[tn + str(ti)])
                    S.add("act", lambda e, tt_=tt_: e.activation(out=tt_[:], in_=tt_[:], func=AF.Exp), reads=[tn + str(ti)], writes=[tn + str(ti)])
                    S.add("pool", lambda e, tt_=tt_, gsrc=gsrc, dst=dst: e.tensor_tensor(out=dst[:], in0=tt_[:], in1=gsrc[:], op=ALU.mult), reads=[tn + str(ti), gk_], writes=[dk])
                    if neg < 0:
                        S.add("pool", lambda e, dst=dst: e.tensor_scalar(out=dst[:], in0=dst[:], scalar1=-1.0, scalar2=None, op0=ALU.mult), reads=[dk], writes=[dk])
                    yield
                S.add("pool", lambda e, Ps=Ps, Qs=Qs: e.tensor_tensor(out=Ps[0][:], in0=cm["eye8"][:], in1=Qs[0][:], op=ALU.add), reads=["g_eye8", qn + "0"], writes=[pn + "0"])
                cur = 0
                for lev in range(1, 6):
                    nxt = 1 - cur
                    Qc, QTc, Qn, QTn = Qs[cur], QTs[cur], Qs[nxt], QTs[nxt]
                    bQ, bQT, bPQ = nb(), nb(), nb()
                    for p in range(8):
                        cs = slice(p * 64, (p + 1) * 64)
                        if lev < 5:
                            S.add("pe", lambda e, Qc=Qc, QTc=QTc, cs=cs, bQ=bQ: e.matmul(ps[bQ][0:64, cs], lhsT=QTc.k(cs), rhs=Qc.k(cs), start=True, stop=True), reads=[qn + str(cur), qtn + str(cur)], writes=[f"ps{bQ}"])
                        S.add("pe", lambda e, Qc=Qc, QTc=QTc, cs=cs, bQT=bQT: e.matmul(ps[bQT][0:64, cs], lhsT=Qc.k(cs), rhs=QTc.k(cs), start=True, stop=True), reads=[qn + str(cur), qtn + str(cur)], writes=[f"ps{bQT}"])
                    if lev < 5:
                        S.add("act", lambda e, Qn=Qn, bQ=bQ: e.activation(out=Qn[:], in_=ps[bQ][0:64, :], func=AF.Copy), reads=[f"ps{bQ}"], writes=[qn + str(nxt)])
                    S.add("dve", lambda e, QTn=QTn, bQT=bQT: e.tensor_copy(out=QTn[:], in_=ps[bQT][0:64, :]), reads=[f"ps{bQT}"], writes=[qtn + str(nxt)])
                    yield
                    Pc, Pn = Ps[cur], Ps[nxt]
                    for p in range(8):
                        cs = slice(p * 64, (p + 1) * 64)
                        S.add("pe", lambda e, QTn=QTn, Pc=Pc, cs=cs, bPQ=bPQ: e.matmul(ps[bPQ][0:64, cs], lhsT=QTn.k(cs), rhs=Pc.k(cs), start=True, stop=True), reads=[qtn + str(nxt), pn + str(cur)], writes=[f"ps{bPQ}"])
                    S.add("dve", lambda e, Pc=Pc, Pn=Pn, bPQ=bPQ: e.tensor_tensor(out=Pn[:], in0=Pc[:], in1=ps[bPQ][0:64, :], op=ALU.add), reads=[pn + str(cur), f"ps{bPQ}"], writes=[pn + str(nxt)])
                    yield
                    cur = nxt
                MT, mtk = Ps[cur], pn + str(cur)
                bW, bU = nb(), nb()
                for p in range(8):
                    cs = slice(p * 64, (p + 1) * 64)
                    S.add("pe", lambda e, rw_=rw_, MT=MT, cs=cs, bW=bW: e.matmul(ps[bW][0:64, cs], lhsT=rw_.k(cs), rhs=MT.k(cs), start=True, stop=True), reads=[f"rw{i2}", mtk], writes=[f"ps{bW}"])
                    S.add("pe", lambda e, rv_=rv_, MT=MT, cs=cs, bU=bU: e.matmul(ps[bU][0:64, cs], lhsT=MT.k(cs), rhs=rv_.k(cs), start=True, stop=True), reads=[f"rv{i2}", mtk], writes=[f"ps{bU}"])
                wT_, uc_ = wT[sl], uc[sl]
                S.add("act", lambda e, wT_=wT_, bW=bW: e.activation(out=wT_[:], in_=ps[bW][0:64, :], func=AF.Copy), reads=[f"ps{bW}"], writes=[f"wT{sl}"])
                S.add("dve", lambda e, uc_=uc_, bU=bU: e.tensor_copy(out=uc_[:], in_=ps[bU][0:64, :]), reads=[f"ps{bU}"], writes=[f"uc{sl}"])
                yield

            def scan_gen(step):
                cf, cb = step, order_b[step]
                tok = [cf * 64, cb * 64]
                sl = step % NS
                i2 = step % 2
                qt_, qk_ = qk_t[sl], f"qkt{sl}"
                sc, sck = scal[sl], f"scal{sl}"
                gl, glk = GLs[sl], f"GLs{sl}"
                kd_, PT_, wT_, uc_ = kd[sl], PT[sl], wT[sl], uc[sl]
                uu_, tq_, oo_ = uu[i2], tq[i2], oo[i2]
                Sc, Sn = Sst[step % 2], Sst[(step + 1) % 2]
                sck_, snk_ = f"Sst{step % 2}", f"Sst{(step + 1) % 2}"
                bWS = nb_scan()
                for p in range(8):
                    cs = slice(p * 64, (p + 1) * 64)
                    S.add("pe", lambda e, wT_=wT_, Sc=Sc, cs=cs, bWS=bWS: e.matmul(ps[bWS][0:64, cs], lhsT=wT_.k(cs), rhs=Sc.k(cs), start=True, stop=True), reads=[f"wT{sl}", sck_], writes=[f"ps{bWS}"])
                S.add("dve", lambda e, uu_=uu_, uc_=uc_, bWS=bWS: e.tensor_tensor(out=uu_[:], in0=uc_[:], in1=ps[bWS][0:64, :], op=ALU.subtract), reads=[f"uc{sl}", f"ps{bWS}"], writes=[f"uu{i2}"])
                bQS = nb_scan()
                for p in range(8):
                    cs = slice(p * 64, (p + 1) * 64)
                    S.add("pe", lambda e, qT_p=qt_.k(0, p, slice(None)), Sc=Sc, cs=cs, bQS=bQS: e.matmul(ps[bQS][0:64, cs], lhsT=qT_p, rhs=Sc.k(cs), start=True, stop=True), reads=[qk_, sck_], writes=[f"ps{bQS}"])
                for d_ in range(2):
                    bc = sc[:, col(0, d_):col(0, d_) + 4].unsqueeze(2).broadcast_to([64, 4, 64])
                    S.add("dve", lambda e, tq_=tq_, d_=d_, bc=bc, bQS=bQS: e.tensor_tensor(out=v3(tq_[:])[:, d_ * 4:d_ * 4 + 4, :], in0=v3(ps[bQS][0:64, :])[:, d_ * 4:d_ * 4 + 4, :], in1=bc, op=ALU.mult), reads=[f"ps{bQS}", sck], writes=[f"tq{i2}"])
                yield
                bKU = nb_scan()
                for p in range(8):
                    cs = slice(p * 64, (p + 1) * 64)
                    S.add("pe", lambda e, kd_=kd_, uu_=uu_, cs=cs, bKU=bKU: e.matmul(ps[bKU][0:64, cs], lhsT=kd_.k(cs), rhs=uu_.k(cs), start=True, stop=True), reads=[f"kd{sl}", f"uu{i2}"], writes=[f"ps{bKU}"])
                S.add("pool", lambda e, Sn=Sn, Sc=Sc, gl=gl: e.tensor_tensor(out=v3(Sn[:]), in0=v3(Sc[:]), in1=gl[:].unsqueeze(2).broadcast_to([64, 8, 64]), op=ALU.mult), reads=[sck_, glk], writes=[snk_])
                S.add("dve", lambda e, Sn=Sn, bKU=bKU: e.tensor_tensor(out=Sn[:], in0=Sn[:], in1=ps[bKU][0:64, :], op=ALU.add), reads=[snk_, f"ps{bKU}"], writes=[snk_])
                bPU = nb_scan()
                for p in range(8):
                    cs = slice(p * 64, (p + 1) * 64)
                    S.add("pe", lambda e, PT_=PT_, uu_=uu_, cs=cs, bPU=bPU: e.matmul(ps[bPU][0:64, cs], lhsT=PT_.k(cs), rhs=uu_.k(cs), start=True, stop=True), reads=[f"PT{sl}", f"uu{i2}"], writes=[f"ps{bPU}"])
                S.add("dve", lambda e, oo_=oo_, tq_=tq_, bPU=bPU: e.tensor_tensor(out=oo_[:], in0=tq_[:], in1=ps[bPU][0:64, :], op=ALU.add), reads=[f"tq{i2}", f"ps{bPU}"], writes=[f"oo{i2}"])
                S.add("sp", lambda e, oo_=oo_, t0=tok[0]: e.dma_start(out=self.ofb[0, t0:t0 + 64, :], in_=oo_[:, 0:256]), reads=[f"oo{i2}"], writes=["ofb"], dma=True)
                S.add("act", lambda e, oo_=oo_, t0=tok[1]: e.dma_start(out=self.ofb[1, t0:t0 + 64, :], in_=oo_[:, 256:512]), reads=[f"oo{i2}"], writes=["ofb"], dma=True)
                yield

            def chain(*gs):
                for g in gs:
                    yield from g

            def interleave(gens):
                gens = list(gens)
                while gens:
                    for g in list(gens):
                        try:
                            next(g)
                        except StopIteration:
                            gens.remove(g)

            interleave([intra_gen(0), intra_gen(1)])
            for k in range(0, NCH, 2):
                gs = [chain(scan_gen(k), scan_gen(k + 1))]
                if k + 2 < NCH:
                    gs += [intra_gen(k + 2), intra_gen(k + 3)]
                interleave(gs)
        self.S.barrier()
        with ExitStack() as st:
            sbt = lambda name, shape, dt=F32: st.enter_context(nc.sbuf_tensor(self.uniq(name), shape, dt))
            gn = sbt("gn", [128, 4, 64])
            S.add("sp", lambda e: e.dma_start(out=gn[:, 0, :], in_=self.gdn_norm[l:l + 1, :].partition_broadcast(128)), writes=["gn"], dma=True)
            for h in range(1, 4):
                S.add("pool", lambda e, h=h: e.tensor_copy(out=gn[:, h, :], in_=gn[:, 0, :]), reads=["gn"], writes=["gn"])
            of = [sbt(f"of{i}", [128, 2, 256]) for i in range(2)]
            gt = [sbt(f"gt{i}", [128, 256], BF16) for i in range(2)]
            os_ = [sbt(f"os{i}", [128, 256]) for i in range(2)]
            fsq = [sbt(f"fsq{i}", [128, 256]) for i in range(2)]
            fss = [sbt(f"fss{i}", [128, 4]) for i in range(2)]
            ab = [sbt(f"fab{i}", [128, 256]) for i in range(2)]
            aT = [sbt(f"faT{i}", [128, 2, 128], BF16) for i in range(2)]
            for tt in range(0 if with_ctx else 2, NT):
                b2 = tt % 2
                tok = tt * 128
                S.add("sp", lambda e, b2=b2, tok=tok: e.dma_start(out=of[b2][:], in_=self.ofb[:, tok:tok + 128, :].rearrange("d t f -> t d f")), reads=["ofb"], writes=[f"of{b2}"], dma=True)
                S.add("act", lambda e, b2=b2, tok=tok: e.dma_start(out=gt[b2][:], in_=self.gateA[tok:tok + 128, :]), reads=["gateA"], writes=[f"gt{b2}"], dma=True)
                S.add("dve", lambda e, b2=b2: e.tensor_tensor(out=os_[b2][:], in0=of[b2][:, 0, :], in1=of[b2][:, 1, :], op=ALU.add), reads=[f"of{b2}"], writes=[f"os{b2}"])
                S.add("act", lambda e, b2=b2: e.activation(out=fsq[b2][:], in_=os_[b2][:], func=AF.Square), reads=[f"os{b2}"], writes=[f"fsq{b2}"])
                S.add("dve", lambda e, b2=b2: e.tensor_reduce(out=fss[b2][:], in_=fsq[b2][:].rearrange("p (h e) -> p h e", h=4), axis=AX.X, op=ALU.add), reads=[f"fsq{b2}"], writes=[f"fss{b2}"])
                S.add("act", lambda e, b2=b2: e.activation(out=fss[b2][:], in_=fss[b2][:], func=AF.Sqrt, scale=1.0 / 64.0, bias=EPS), reads=[f"fss{b2}"], writes=[f"fss{b2}"])
                S.add("dve", lambda e, b2=b2: e.reciprocal(out=fss[b2][:], in_=fss[b2][:]), reads=[f"fss{b2}"], writes=[f"fss{b2}"])
                o3 = os_[b2][:].rearrange("p (h e) -> p h e", h=4)
                S.add("dve", lambda e, b2=b2, o3=o3: e.tensor_tensor(out=o3, in0=o3, in1=fss[b2][:].unsqueeze(2).broadcast_to([128, 4, 64]), op=ALU.mult), reads=[f"os{b2}", f"fss{b2}"], writes=[f"os{b2}"])
                S.add("pool", lambda e, b2=b2, o3=o3: e.tensor_tensor(out=o3, in0=o3, in1=gn[:], op=ALU.mult), reads=[f"os{b2}", "gn"], writes=[f"os{b2}"])
                S.add("pool", lambda e, b2=b2: e.tensor_tensor(out=ab[b2][:], in0=os_[b2][:], in1=gt[b2][:], op=ALU.mult), reads=[f"os{b2}", f"gt{b2}"], writes=[f"fab{b2}"])
                pb = 2 * b2
                for c_ in range(2):
                    S.add("pe", lambda e, b2=b2, c_=c_, pb=pb: e.transpose(ps[pb + c_][:, 0:128], ab[b2][:, c_ * 128:(c_ + 1) * 128], self.ident[:]), reads=[f"fab{b2}", "ident"], writes=[f"ps{pb + c_}"])
                    S.add("act", lambda e, b2=b2, c_=c_, pb=pb: e.activation(out=aT[b2][:, c_, :], in_=ps[pb + c_][:, 0:128], func=AF.Copy), reads=[f"ps{pb + c_}"], writes=[f"faT{b2}"])
                S.add("sp", lambda e, b2=b2, tok=tok: e.dma_start(out=self.oT[0:2, :, tok:tok + 128].rearrange("c p t -> p c t"), in_=aT[b2][:]), reads=[f"faT{b2}"], writes=["oT"], dma=True)
        self.S.barrier()


def prep_inputs(inputs, b, layers):
    L = layers
    f = lambda a: np.ascontiguousarray(a, dtype=np.float32)
    m = {}
    m["x"] = f(inputs["x"][b])
    m["ctx"] = f(inputs["ctx"][b])
    cT = np.concatenate([inputs["c"][b].reshape(8, 128).T, inputs["c_ctx"].reshape(8, 128).T], axis=1)
    m["cT"] = f(cT)
    m["w_mod"] = f(inputs["w_mod"][L])
    m["b_modT"] = f(inputs["b_mod"][L].reshape(len(L), 48, 128).transpose(0, 2, 1))
    m["w_in"] = f(inputs["w_in"][L])
    m["conv_aT"] = f(inputs["conv_a"][L].reshape(len(L), 5, 6, 128).transpose(0, 3, 2, 1))
    m["a_log"] = f(inputs["a_log"][L].reshape(len(L), 8))
    m["dt_bias"] = f(inputs["dt_bias"][L].reshape(len(L), 8))
    m["gdn_norm"] = f(inputs["gdn_norm"][L])
    m["diff_lambda"] = f(inputs["diff_lambda"][L].reshape(len(L), 128))
    m["diff_norm"] = f(inputs["diff_norm"][L])
    m["qk_norm_c"] = f(inputs["qk_norm_c"][L].reshape(len(L), 128))
    m["sink_d"] = f(inputs["sink_d"][L])
    for k in ["w_br", "w_out", "ln1_g", "ln1_b", "w_router", "w_gate_e", "w_up_e", "w_down_e", "ln2_g", "ln2_b"]:
        m[k] = f(inputs[k][L])
    for k, v in host_consts().items():
        m["c_" + k] = f(v)
    return m


def kernel(**inputs):
    layers = list(range(DEPTH))
    bld = Builder(DEPTH)
    nc = bld.build()
    in_maps = [prep_inputs(inputs, b, layers) for b in range(8)]
    res = run_bass_kernel_spmd(nc, in_maps, core_ids=list(range(8)))
    return np.stack([res.results[b]["out"] for b in range(8)], axis=0).astype(np.float32)
```

```python
import math
from contextlib import ExitStack
import numpy as np
import concourse.bass as bass
import concourse.mybir as mybir
from concourse.bass_utils import run_bass_kernel_spmd

F32 = mybir.dt.float32
BF16 = mybir.dt.bfloat16
FP16 = mybir.dt.float16
AF = mybir.ActivationFunctionType
ALU = mybir.AluOpType
AX = mybir.AxisListType

D = 1024
NLAT = 4096
NCTX = 256
T = NLAT + NCTX
NT = T // 128
DEPTH = 4
D_IN = 6928
EPS = 1e-6
ALPHA = (2.0 * DEPTH) ** 0.25
NEG = -30000.0


class Sched:
    EPOCH = 16000
    ENGS = ("pe", "act", "dve", "pool", "sp")

    def __init__(self, nc):
        self.nc = nc
        self.ops = []

    def add(self, eng, fn, reads=(), writes=(), dma=False):
        writes = tuple(writes) + tuple(k for k in reads if isinstance(k, str) and k.startswith("ps") and k[2:].isdigit())
        self.ops.append([eng, fn, tuple(reads), tuple(writes), dma])
        return len(self.ops) - 1

    def barrier(self):
        self.ops.append(["sp", "BAR", (), (), False])

    def emit(self, stack, dma_slots=None):
        nc = self.nc
        ops = self.ops
        n = len(ops)
        dma_slots = dma_slots or {"sp": 24, "pool": 24, "act": 8}
        last_w = {}
        readers = {}
        deps = [None] * n
        needed = [False] * n
        last_eng_op = {e: None for e in self.ENGS}
        last_slot_op = {e: {} for e in dma_slots}
        slot_rr = {e: 0 for e in dma_slots}
        slot_of = [None] * n
        last_bar = None
        seen_since_bar = {e: True for e in self.ENGS}
        for i, (eng, fn, R, W, dma) in enumerate(ops):
            d = set()
            if fn == "BAR":
                for e in self.ENGS:
                    if last_eng_op[e] is not None:
                        d.add(last_eng_op[e])
                for e in dma_slots:
                    for s, j in last_slot_op[e].items():
                        d.add(j)
                last_bar = i
                seen_since_bar = {e: False for e in self.ENGS}
            else:
                for k in R:
                    if k in last_w:
                        d.add(last_w[k])
                for k in W:
                    if k in last_w:
                        d.add(last_w[k])
                    for r in readers.get(k, ()):
                        d.add(r)
                if last_bar is not None and not seen_since_bar[eng]:
                    d.add(last_bar)
                seen_since_bar[eng] = True
            d.discard(i)
            dl = []
            for j in d:
                je, jf, _, _, jd = ops[j]
                if (not jd) and je == "pe" and eng == "pe" and not dma and jf != "BAR" and fn != "BAR":
                    continue
                dl.append(j)
                needed[j] = True
            deps[i] = sorted(dl)
            for k in R:
                readers.setdefault(k, []).append(i)
            for k in W:
                last_w[k] = i
                readers[k] = []
            if dma:
                s = slot_rr[eng]
                slot_rr[eng] = (s + 1) % dma_slots[eng]
                slot_of[i] = s
                last_slot_op[eng][s] = i
            else:
                last_eng_op[eng] = i
        sig = [None] * n
        eng_count = {e: 0 for e in self.ENGS}
        eng_sems = {e: [] for e in self.ENGS}
        slot_sems = {}
        slot_state = {}
        for e, cnt in dma_slots.items():
            slot_sems[e] = [stack.enter_context(nc.semaphore(f"d_{e}_{s}")) for s in range(cnt)]
            slot_state[e] = [0] * cnt
        prewait = [None] * n
        for i, (eng, fn, R, W, dma) in enumerate(ops):
            if dma:
                s = slot_of[i]
                prev = slot_state[eng][s]
                if prev > 0:
                    prewait[i] = (slot_sems[eng][s], prev)
                slot_state[eng][s] = prev + 16
                sig[i] = ("dma", slot_sems[eng][s], prev + 16)
            elif needed[i]:
                c = eng_count[eng]
                ep = c // self.EPOCH
                while len(eng_sems[eng]) <= ep:
                    eng_sems[eng].append(stack.enter_context(nc.semaphore(f"s_{eng}_{len(eng_sems[eng])}")))
                sig[i] = ("eng", eng_sems[eng][ep], (c % self.EPOCH) + 1, c)
                eng_count[eng] = c + 1
        plan = {e: [] for e in self.ENGS}
        waited_eng = {e: {f: -1 for f in self.ENGS} for e in self.ENGS}
        waited_dma = {e: {} for e in self.ENGS}
        for i, (eng, fn, R, W, dma) in enumerate(ops):
            waits = []
            if prewait[i] is not None:
                waits.append(prewait[i])
            for j in deps[i]:
                sj = sig[j]
                if sj[0] == "dma":
                    key = id(sj[1])
                    if waited_dma[eng].get(key, 0) >= sj[2]:
                        continue
                    waited_dma[eng][key] = sj[2]
                    waits.append((sj[1], sj[2]))
                else:
                    je = ops[j][0]
                    if waited_eng[eng][je] >= sj[3]:
                        continue
                    waited_eng[eng][je] = sj[3]
                    waits.append((sj[1], sj[2]))
            plan[eng].append((i, waits))
        self.n_waits = sum(len(w) for e in plan for _, w in plan[e])

        def run_engine(e_obj, ename):
            for i, waits in plan[ename]:
                for (sem, val) in waits:
                    e_obj.wait_ge(sem, val)
                fn = ops[i][1]
                if fn is None:
                    continue
                inst = e_obj.nop() if fn == "BAR" else fn(e_obj)
                sj = sig[i]
                if sj is not None:
                    inst.then_inc(sj[1], 16 if sj[0] == "dma" else 1)

        with nc.Block() as block:
            @block.tensor
            def _(e):
                run_engine(e, "pe")

            @block.scalar
            def _(e):
                run_engine(e, "act")

            @block.vector
            def _(e):
                run_engine(e, "dve")

            @block.gpsimd
            def _(e):
                run_engine(e, "pool")

            @block.sync
            def _(e):
                run_engine(e, "sp")


C_QA, C_KA, C_VA, C_G16, C_GATEA = 0, 256, 512, 768, 784
C_QB, C_KB, C_VB = 1040, 1296, 1552
C_QC, C_KC, C_VC = 1808, 2064, 2192
C_QD, C_KD, C_VD = 2320, 2576, 2704
C_GL = 2832

GROUPS = [(0, 256)] + [(256 + 512 * g, 512) for g in range(8)]


def host_consts():
    c = {}
    c["ident"] = np.eye(128, dtype=np.float32)
    pos = np.arange(NLAT)
    row = (pos // 64).astype(np.float32)
    col = (pos % 64).astype(np.float32)

    def tabs(dim, reps):
        nf = dim // 4
        inv = (np.float32(10000.0) ** (-np.arange(nf, dtype=np.float32) / np.float32(nf))).astype(np.float32)
        ar = row[None, :] * inv[:, None]
        ac = col[None, :] * inv[:, None]
        cos = np.concatenate([np.cos(ar), np.cos(ar), np.cos(ac), np.cos(ac)], 0).astype(np.float32)
        sin = np.concatenate([np.sin(ar), np.sin(ar), np.sin(ac), np.sin(ac)], 0).astype(np.float32)
        return np.tile(cos, (reps, 1)), np.tile(sin, (reps, 1))

    c["cos64"], c["sin64"] = tabs(64, 2)
    c["cos32"], c["sin32"] = tabs(32, 4)

    def rotm(dim):
        q = dim // 4
        m = np.zeros((128, 128), np.float32)
        for b0 in range(0, 128, dim):
            for i in range(q):
                m[b0 + q + i, b0 + i] = -1.0
                m[b0 + i, b0 + q + i] = 1.0
                m[b0 + 3 * q + i, b0 + 2 * q + i] = -1.0
                m[b0 + 2 * q + i, b0 + 3 * q + i] = 1.0
        return m

    c["rot64"] = rotm(64)
    c["rot32"] = rotm(32)
    bo = np.zeros((128, 128), np.float32)
    bo[0:64, 0:64] = 1.0
    bo[64:128, 64:128] = 1.0
    c["bones64"] = bo
    r = np.arange(128)
    c["tri_ge"] = (r[:, None] >= r[None, :]).astype(np.float32)
    c["tri_le"] = (r[:, None] <= r[None, :]).astype(np.float32)
    sel = np.zeros((128, 64), np.float32)
    sel[64, :] = 1.0
    c["sel64"] = sel
    cc, ss = np.meshgrid(np.arange(64), np.arange(64), indexing="ij")
    def m8(fwd_ok, bwd_ok):
        m = np.zeros((64, 8, 64), np.float32)
        for p in range(8):
            ok = fwd_ok if p < 4 else bwd_ok
            m[:, p, :] = np.where(ok, 0.0, NEG)
        return m.reshape(64, 512)
    c["mask_a"] = m8(ss < cc, ss > cc)
    c["mask_at"] = m8(cc < ss, cc > ss)
    c["mask_pt"] = m8(cc <= ss, cc >= ss)
    c["eye8"] = np.tile(np.eye(64, dtype=np.float32)[:, None, :], (1, 8, 1)).reshape(64, 512)
    cm = np.ones((4, T), np.float32)
    cm[:, ::64] = 0.0
    c["chunkmask"] = cm
    c["iota512"] = np.tile(np.arange(512, dtype=np.float32)[None, :], (128, 1))
    c["iota4"] = (r[:, None] + 128 * np.arange(4)[None, :]).astype(np.float32)
    return c


CONST_SHAPES = {"ident": [128, 128], "cos64": [128, NLAT], "sin64": [128, NLAT], "cos32": [128, NLAT],
                "sin32": [128, NLAT], "rot64": [128, 128], "rot32": [128, 128], "bones64": [128, 128],
                "tri_ge": [128, 128], "tri_le": [128, 128], "sel64": [128, 64],
                "iota512": [128, 512], "iota4": [128, 4], "mask_a": [64, 512], "mask_at": [64, 512],
                "mask_pt": [64, 512], "eye8": [64, 512], "chunkmask": [4, T]}


class Builder:
    def __init__(self, nlayers, debug=()):
        self.nl = nlayers
        self.debug = set(debug)
        nc = self.nc = bass.Bass("TRN2", target_bir_lowering=False)
        self.S = Sched(nc)
        L = nlayers
        di = lambda name, shape, dt=F32: nc.dram_tensor(name, shape, dt, kind="ExternalInput").ap()
        self.x = di("x", [NLAT, D])
        self.ctx = di("ctx", [NCTX, D])
        self.cT = di("cT", [128, 16])
        self.w_mod = di("w_mod", [L, D, 6 * D])
        self.b_modT = di("b_modT", [L, 128, 48])
        self.w_in = di("w_in", [L, D, D_IN])
        self.conv_aT = di("conv_aT", [L, 128, 6, 5])
        self.a_log = di("a_log", [L, 8])
        self.dt_bias = di("dt_bias", [L, 8])
        self.gdn_norm = di("gdn_norm", [L, 64])
        self.diff_lambda = di("diff_lambda", [L, 128])
        self.diff_norm = di("diff_norm", [L, 64])
        self.qk_norm_c = di("qk_norm_c", [L, 128])
        self.sink_d = di("sink_d", [L, 4])
        self.w_br = di("w_br", [L, 4, 256, D])
        self.w_out = di("w_out", [L, D, D])
        self.ln1_g = di("ln1_g", [L, D])
        self.ln1_b = di("ln1_b", [L, D])
        self.w_router = di("w_router", [L, D, 16])
        self.w_gate_e = di("w_gate_e", [L, 16, D, D])
        self.w_up_e = di("w_up_e", [L, 16, D, D])
        self.w_down_e = di("w_down_e", [L, 16, D, D])
        self.ln2_g = di("ln2_g", [L, D])
        self.ln2_b = di("ln2_b", [L, D])
        self.cst = {k: di("c_" + k, s) for k, s in CONST_SHAPES.items()}
        self.out = nc.dram_tensor("out", [NLAT, D], F32, kind="ExternalOutput").ap()
        self.dbg_out = {}

    def uniq(self, name):
        self._uid = getattr(self, "_uid", 0) + 1
        return f"{name}_u{self._uid}"

    def scratch(self, name, shape, dt=F32):
        if name in self.debug:
            ap = self.nc.dram_tensor(name, shape, dt, kind="ExternalOutput").ap()
            self.dbg_out[name] = ap
            return ap
        return self.nc.dram_tensor(name, shape, dt).ap()

    def build(self):
        nc, S = self.nc, self.S
        with ExitStack() as top:
            self.top = top
            self.ps = [top.enter_context(nc.psum_tensor(f"ps{i}", [128, 512], F32)) for i in range(8)]
            sbt = lambda name, shape, dt=F32: top.enter_context(nc.sbuf_tensor(name, shape, dt))
            self.ident = sbt("ident", [128, 128])
            self.identb = sbt("identb", [128, 128], BF16)
            S.add("sp", lambda e: e.dma_start(out=self.ident[:], in_=self.cst["ident"][:, :]), writes=["ident"], dma=True)
            S.add("pool", lambda e: e.dma_start(out=self.identb[:], in_=self.cst["ident"][:, :]), writes=["identb"], dma=True)
            self.modT = sbt("modT", [128, 48, 2])
            self.modP = sbt("modP", [128, 2, 8, 2])
            self.xres = self.scratch("xres", [T, D])
            self.hT = self.scratch("hT", [8, 128, T], BF16)
            self.modrow = self.scratch("modrow", [96, 128])
            self.aqkv = self.scratch("aqkv", [768, T])
            self.ag16 = self.scratch("ag16", [16, T])
            self.gateA = self.scratch("gateA", [T, 256], BF16)
            self.vtok = self.scratch("vtok", [T, 512], BF16)
            self.qkB = self.scratch("qkB", [6, 128, T], BF16)
            self.qkC = self.scratch("qkC", [3, 128, T], BF16)
            self.qkD = self.scratch("qkD", [3, 128, T], BF16)
            self.oT = self.scratch("oT", [8, 128, T], BF16)
            self.mTd = self.scratch("mTd", [4, 128, T], BF16)
            self.h2tok = self.scratch("h2tok", [T, D], BF16)
            self.aff = self.scratch("aff", [T, 16])
            self.Rd = self.scratch("Rd", [3, 2, 8, T])
            self.TSd = self.scratch("TSd", [40, T])
            self.qkhd = self.scratch("qkhd", [2, 4, 64, T])
            self.kvtok = self.scratch("kvtok", [T, 512])
            self.ofb = self.scratch("ofb", [2, T, 256])
            self.ptokd = self.scratch("ptokd", [128, NT, 16])
            self.posd = self.scratch("posd", [NT, 16, 128], FP16)
            self.gvd = self.scratch("gvd", [NT, 16, 128], BF16)
            self.ygL = self.scratch("ygL", [16, 512, D], BF16)
            self.ygC = self.scratch("ygC", [16, 32, D], BF16)
            self.f0d = self.scratch("f0d", [T, 512])
            if "inject_a" in self.debug:
                self.a_inj = nc.dram_tensor("a_inj", [2, 128, T], F32, kind="ExternalInput").ap()
            S.add("sp", lambda e: e.dma_start(out=self.xres[0:NCTX, :], in_=self.ctx[:, :]), writes=["xres"], dma=True)
            for i in range(4):
                S.add("sp", lambda e, i=i: e.dma_start(out=self.xres[NCTX + i * 1024:NCTX + (i + 1) * 1024, :], in_=self.x[i * 1024:(i + 1) * 1024, :]), writes=["xres"], dma=True)
            for l in range(self.nl):
                self.layer(l)
            for i in range(4):
                S.add("sp", lambda e, i=i: e.dma_start(out=self.out[i * 1024:(i + 1) * 1024, :], in_=self.xres[NCTX + i * 1024:NCTX + (i + 1) * 1024, :]), reads=["xres"], writes=["out"], dma=True)
            S.add("sp", None, reads=["out"] + [k for k in self.dbg_out])
            S.emit(top)
        return nc

    def layer(self, l):
        self.phase_mod(l)
        self.S.barrier()
        self.phase_inproj(l)
        self.S.barrier()
        if "skip_attn" not in self.debug:
            self.phase_attn(l)
            self.S.barrier()
        if "skip_gdn" not in self.debug:
            self.phase_gdn(l)
        if "inject_a" in self.debug:
            with ExitStack() as st:
                tmpa = st.enter_context(self.nc.sbuf_tensor(self.uniq("tmpa"), [128, 2, T], BF16))
                self.S.add("pool", lambda e: e.dma_start(out=tmpa[:], in_=self.a_inj.rearrange("c p t -> p c t")), writes=["tmpa"], dma=True)
                self.S.add("sp", lambda e: e.dma_start(out=self.oT[0:2].rearrange("c p t -> p c t"), in_=tmpa[:]), reads=["tmpa"], writes=["oT"], dma=True)
            self.S.barrier()
        if "skip_merge" not in self.debug:
            self.phase_merge(l)
        if "skip_moe" not in self.debug:
            self.phase_moe(l)

    def phase_mod(self, l):
        nc, S = self.nc, self.S
        ps = self.ps
        with ExitStack() as st:
            sbt = lambda name, shape, dt=F32: st.enter_context(nc.sbuf_tensor(self.uniq(name), shape, dt))
            cs = sbt("cs", [128, 16])
            sc = sbt("silu_c", [128, 8, 2])
            bm = sbt("bm", [128, 48])
            wm = [sbt(f"wm{i}", [128, 8, 512]) for i in range(2)]
            mrow = sbt("mrow", [96, 128])
            S.add("sp", lambda e: e.dma_start(out=cs[:], in_=self.cT[:, :]), writes=["cs"], dma=True)
            S.add("sp", lambda e: e.dma_start(out=bm[:], in_=self.b_modT[l]), writes=["bm"], dma=True)
            S.add("act", lambda e: e.activation(out=sc[:].rearrange("p k s -> p s k"), in_=cs[:].rearrange("p (s k) -> p s k", s=2), func=AF.Silu), reads=["cs"], writes=["sc"])
            wv = self.w_mod[l].rearrange("(kc p) n -> p kc n", p=128)
            for pc in range(12):
                w = wm[pc % 2]
                wk = f"wm{pc % 2}"
                S.add("sp" if pc % 2 == 0 else "act", lambda e, w=w, pc=pc: e.dma_start(out=w[:], in_=wv[:, :, pc * 512:(pc + 1) * 512]), writes=[wk], dma=True)
                for mc in range(4):
                    idx = pc * 4 + mc
                    for kc in range(8):
                        S.add("pe", lambda e, w=w, mc=mc, kc=kc, idx=idx: e.matmul(ps[0][:, idx * 2:idx * 2 + 2], lhsT=w[:, kc, mc * 128:(mc + 1) * 128], rhs=sc[:, kc, :], start=(kc == 0), stop=(kc == 7)), reads=[wk, "sc"], writes=["ps0"])
            S.add("dve", lambda e: e.tensor_tensor(out=self.modT[:], in0=ps[0][:, 0:96].rearrange("p (i s) -> p i s", s=2), in1=bm[:].unsqueeze(2).broadcast_to([128, 48, 2]), op=ALU.add), reads=["ps0", "bm"], writes=["modT"])
            S.add("dve", lambda e: e.tensor_scalar(out=self.modP[:, 0], in0=self.modT[:, 8:16, :], scalar1=1.0, scalar2=None, op0=ALU.add), reads=["modT"], writes=["modP"])
            S.add("dve", lambda e: e.tensor_scalar(out=self.modP[:, 1], in0=self.modT[:, 32:40, :], scalar1=1.0, scalar2=None, op0=ALU.add), reads=["modT"], writes=["modP"])
            S.add("pe", lambda e: e.transpose(ps[1][0:96, 0:128], self.modT[:].rearrange("p i s -> p (i s)"), self.ident[:]), reads=["modT", "ident"], writes=["ps1"])
            S.add("dve", lambda e: e.tensor_copy(out=mrow[:], in_=ps[1][0:96, 0:128]), reads=["ps1"], writes=["mrow"])
            S.add("sp", lambda e: e.dma_start(out=self.modrow[:, :], in_=mrow[:]), reads=["mrow"], writes=["modrow"], dma=True)

    def bcast_mod(self, eng, tile, kind, s, key):
        src = self.modrow.rearrange("(i s) p -> s i p", s=2)[s, kind * 8:(kind + 1) * 8, :].partition_broadcast(128)
        self.S.add(eng, lambda e: e.dma_start(out=tile[:].rearrange("p (i q) -> p i q", q=128), in_=src), reads=["modrow"], writes=[key], dma=True)

    def phase_inproj(self, l):
        nc, S = self.nc, self.S
        ps = self.ps
        with ExitStack() as st:
            sbt = lambda name, shape, dt=F32: st.enter_context(nc.sbuf_tensor(self.uniq(name), shape, dt))
            fm = []
            for j in range(6):
                fm.append(("A%d" % j, [(C_QA + 128 * j, 128)]))
            fm.append(("G16", [(C_G16, 16)]))
            for j, (o, n) in enumerate([(0, 96), (96, 96), (192, 64)]):
                fm.append(("QB%d" % j, [(C_QB + o, n)]))
            for j, (o, n) in enumerate([(0, 96), (96, 96), (192, 64)]):
                fm.append(("KB%d" % j, [(C_KB + o, n)]))
            for g in range(2):
                fm.append(("QC%d" % g, [(C_QC + g * 64, 64), (C_QC + (2 + g) * 64, 64)]))
            fm.append(("KC", [(C_KC, 128)]))
            for g in range(2):
                fm.append(("QD%d" % g, [(C_QD + g * 64, 64), (C_QD + (2 + g) * 64, 64)]))
            fm.append(("KD", [(C_KD, 128)]))
            nfm = len(fm)
            tm_cols = [(C_VB, 256), (C_VC, 128), (C_VD, 128), (C_GATEA, 256)]
            wfm = sbt("wfm", [128, 8, nfm * 128], BF16)
            wtm = sbt("wtm", [128, 8, 768], BF16)
            wv = self.w_in[l].rearrange("(kc p) n -> p kc n", p=128)
            for ci, (nm, segs) in enumerate(fm):
                o = 0
                for (c0, ncol) in segs:
                    S.add("pool", lambda e, ci=ci, o=o, c0=c0, ncol=ncol: e.dma_start(out=wfm[:, :, ci * 128 + o:ci * 128 + o + ncol], in_=wv[:, :, c0:c0 + ncol]), writes=["wfm"], dma=True)
                    o += ncol
            o = 0
            for (c0, ncol) in tm_cols:
                S.add("pool", lambda e, o=o, c0=c0, ncol=ncol: e.dma_start(out=wtm[:, :, o:o + ncol], in_=wv[:, :, c0:c0 + ncol]), writes=["wtm"], dma=True)
                o += ncol
            rot64 = sbt("rot64", [128, 128])
            rot32 = sbt("rot32", [128, 128])
            bones = sbt("bones", [128, 128])
            qkg = sbt("qkg", [128, 2])
            S.add("sp", lambda e: e.dma_start(out=rot64[:], in_=self.cst["rot64"][:, :]), writes=["rot64"], dma=True)
            S.add("sp", lambda e: e.dma_start(out=rot32[:], in_=self.cst["rot32"][:, :]), writes=["rot32"], dma=True)
            S.add("sp", lambda e: e.dma_start(out=bones[:], in_=self.cst["bones64"][:, :]), writes=["bones"], dma=True)
            qkv = self.qk_norm_c[l].rearrange("(g d) -> d g", g=2)
            for h in range(2):
                S.add("sp", lambda e, h=h: e.dma_start(out=qkg[h * 64:(h + 1) * 64, :], in_=qkv, allow_slow_non_contiguous=True), writes=["qkg"], dma=True)
            xt = [sbt(f"xt{i}", [128, 1024]) for i in range(2)]
            hTs = [sbt(f"hTs{i}", [128, 8, 512], BF16) for i in range(2)]
            tabs = [sbt(f"tabs{i}", [128, 4, 512]) for i in range(2)]
            ev = [sbt(f"ev{i}", [128, 512]) for i in range(3)]
            ev2 = [sbt(f"evb{i}", [128, 512]) for i in range(2)]
            sq = sbt("sq", [128, 512])
            rstd = sbt("rstd", [128, 512])
            ob = [sbt(f"ob{i}", [128, 512], BF16) for i in range(3)]
            otm = [sbt(f"otm{i}", [128, 768], BF16) for i in range(2)]
            cnt = {"ev": 0, "ob": 0, "ps": 0, "evb": 0}
            tile_i = 0
            for gi, (t0, W) in enumerate(GROUPS):
                s = 1 if gi == 0 else 0
                hb = hTs[gi % 2]
                hk = f"hTs{gi % 2}"
                ntile = W // 128
                for j in range(ntile):
                    xb = xt[tile_i % 2]
                    xk = f"xt{tile_i % 2}"
                    tile_i += 1
                    tt = t0 + j * 128
                    S.add("sp", lambda e, xb=xb, tt=tt: e.dma_start(out=xb[:], in_=self.xres[tt:tt + 128, :]), reads=["xres"], writes=[xk], dma=True)
                    for half in range(2):
                        pb = 6 + half
                        for q in range(4):
                            fc = half * 4 + q
                            S.add("pe", lambda e, xb=xb, fc=fc, pb=pb, q=q: e.transpose(ps[pb][:, q * 128:(q + 1) * 128], xb[:, fc * 128:(fc + 1) * 128], self.ident[:]), reads=[xk, "ident"], writes=[f"ps{pb}"])
                        for q in range(4):
                            fc = half * 4 + q
                            S.add("act", lambda e, hb=hb, fc=fc, pb=pb, q=q, j=j, s=s: e.activation(out=hb[:, fc, j * 128:(j + 1) * 128], in_=ps[pb][:, q * 128:(q + 1) * 128], func=AF.Identity, scale=self.modP[:, 0, fc, s:s + 1], bias=self.modT[:, fc, s:s + 1]), reads=[f"ps{pb}", "modP", "modT"], writes=[hk])
                S.add("sp", lambda e, hb=hb, t0=t0, W=W: e.dma_start(out=self.hT[:, :, t0:t0 + W].rearrange("c p t -> p c t"), in_=hb[:, :, 0:W]), reads=[hk], writes=["hT"], dma=True)
                tb = tabs[gi % 2]
                tk = f"tabs{gi % 2}"
                if gi > 0:
                    l0 = t0 - NCTX
                    for ti, nm in enumerate(["cos64", "sin64", "cos32", "sin32"]):
                        S.add("sp", lambda e, tb=tb, ti=ti, nm=nm, l0=l0: e.dma_start(out=tb[:, ti, :], in_=self.cst[nm][:, l0:l0 + 512]), writes=[tk], dma=True)
                ipend = []
                for ci, (nm, segs) in enumerate(fm):
                    M = sum(nc_ for _, nc_ in segs)
                    pb = cnt["ps"] % 4
                    cnt["ps"] += 1
                    pk = f"ps{pb}"
                    for kc in range(8):
                        S.add("pe", lambda e, ci=ci, M=M, kc=kc, hb=hb, W=W, pb=pb: e.matmul(ps[pb][0:M, 0:W], lhsT=wfm[:, kc, ci * 128:ci * 128 + M], rhs=hb[:, kc, 0:W], start=(kc == 0), stop=(kc == 7)), reads=["wfm", hk], writes=[pk])
                    while ipend:
                        ipend.pop(0)()
                    if nm[0] == "A" or nm == "G16":
                        e_i = cnt["ev"] % 3
                        cnt["ev"] += 1
                        eb, ek = ev[e_i], f"ev{e_i}"
                        S.add("act", lambda e, eb=eb, M=M, W=W, pb=pb: e.activation(out=eb[0:M, 0:W], in_=ps[pb][0:M, 0:W], func=AF.Copy), reads=[pk], writes=[ek])
                        if nm == "G16":
                            S.add("sp", lambda e, eb=eb, t0=t0, W=W: e.dma_start(out=self.ag16[:, t0:t0 + W], in_=eb[0:16, 0:W]), reads=[ek], writes=["ag16"], dma=True)
                        else:
                            j = int(nm[1])
                            S.add("sp", lambda e, eb=eb, t0=t0, W=W, j=j: e.dma_start(out=self.aqkv[j * 128:(j + 1) * 128, t0:t0 + W], in_=eb[:, 0:W]), reads=[ek], writes=["aqkv"], dma=True)
                        continue
                    def post2(nm=nm, M=M, W=W, pb=pb, pk=pk, t0=t0, gi=gi, tb=tb, tk=tk):
                        isC = nm[1] == "C"
                        isB = nm[1] == "B"
                        e_i = cnt["ev"] % 3
                        cnt["ev"] += 1
                        eb, ek = ev[e_i], f"ev{e_i}"
                        o_i = cnt["ob"] % 3
                        cnt["ob"] += 1
                        obt, obk = ob[o_i], f"ob{o_i}"
                        if isC:
                            gcol = 1 if nm == "KC" else 0
                            S.add("act", lambda e, M=M, W=W, pb=pb: e.activation(out=sq[0:M, 0:W], in_=ps[pb][0:M, 0:W], func=AF.Square), reads=[pk], writes=["sq"])
                            S.add("pe", lambda e, M=M, W=W: e.matmul(ps[4][0:M, 0:W], lhsT=bones[0:M, 0:M], rhs=sq[0:M, 0:W], start=True, stop=True), reads=["sq", "bones"], writes=["ps4"])
                            S.add("act", lambda e, M=M, W=W: e.activation(out=rstd[0:M, 0:W], in_=ps[4][0:M, 0:W], func=AF.Sqrt, scale=1.0 / 64.0, bias=EPS), reads=["ps4"], writes=["rstd"])
                            S.add("dve", lambda e, M=M, W=W: e.reciprocal(out=rstd[0:M, 0:W], in_=rstd[0:M, 0:W]), reads=["rstd"], writes=["rstd"])
                            S.add("dve", lambda e, eb=eb, M=M, W=W, pb=pb, gcol=gcol: e.scalar_tensor_tensor(out=eb[0:M, 0:W], in0=ps[pb][0:M, 0:W], scalar=qkg[0:M, gcol:gcol + 1], in1=rstd[0:M, 0:W], op0=ALU.mult, op1=ALU.mult), reads=[pk, "qkg", "rstd"], writes=[ek])
                        else:
                            S.add("act", lambda e, eb=eb, M=M, W=W, pb=pb: e.activation(out=eb[0:M, 0:W], in_=ps[pb][0:M, 0:W], func=AF.Copy), reads=[pk], writes=[ek])
                        if gi == 0:
                            S.add("dve", lambda e, obt=obt, eb=eb, M=M, W=W: e.tensor_copy(out=obt[0:M, 0:W], in_=eb[0:M, 0:W]), reads=[ek], writes=[obk])
                        else:
                            rot = rot32 if isB else rot64
                            rk = "rot32" if isB else "rot64"
                            ct, sn = (2, 3) if isB else (0, 1)
                            S.add("pe", lambda e, rot=rot, eb=eb, M=M, W=W: e.matmul(ps[5][0:M, 0:W], lhsT=rot[0:M, 0:M], rhs=eb[0:M, 0:W], start=True, stop=True), reads=[ek, rk], writes=["ps5"])
                            b_i = cnt["evb"] % 2
                            cnt["evb"] += 1
                            e2, e2k = ev2[b_i], f"evb{b_i}"
                            S.add("dve", lambda e, e2=e2, tb=tb, sn=sn, M=M, W=W: e.tensor_tensor(out=e2[0:M, 0:W], in0=ps[5][0:M, 0:W], in1=tb[0:M, sn, 0:W], op=ALU.mult), reads=["ps5", tk], writes=[e2k])
                            S.add("pool", lambda e, eb=eb, tb=tb, ct=ct, M=M, W=W: e.tensor_tensor(out=eb[0:M, 0:W], in0=eb[0:M, 0:W], in1=tb[0:M, ct, 0:W], op=ALU.mult), reads=[ek, tk], writes=[ek])
                            S.add("dve", lambda e, obt=obt, eb=eb, e2=e2, M=M, W=W: e.tensor_tensor(out=obt[0:M, 0:W], in0=eb[0:M, 0:W], in1=e2[0:M, 0:W], op=ALU.add), reads=[ek, e2k], writes=[obk])
                        if isB:
                            dst = self.qkB[(0 if nm[0] == "Q" else 3) + int(nm[2])]
                            dk = "qkB"
                        elif isC:
                            dst = self.qkC[2 if nm == "KC" else int(nm[2])]
                            dk = "qkC"
                        else:
                            dst = self.qkD[2 if nm == "KD" else int(nm[2])]
                            dk = "qkD"
                        S.add("sp", lambda e, dst=dst, obt=obt, M=M, W=W, t0=t0: e.dma_start(out=dst[0:M, t0:t0 + W], in_=obt[0:M, 0:W]), reads=[obk], writes=[dk], dma=True)
                    ipend.append(post2)
                while ipend:
                    ipend.pop(0)()
                for j in range(ntile):
                    tt = t0 + j * 128
                    ot = otm[j % 2]
                    otk = f"otm{j % 2}"
                    for (c0, ncol, pb) in [(0, 512, 4), (512, 256, 5)]:
                        for kc in range(8):
                            S.add("pe", lambda e, hb=hb, kc=kc, j=j, c0=c0, ncol=ncol, pb=pb: e.matmul(ps[pb][:, 0:ncol], lhsT=hb[:, kc, j * 128:(j + 1) * 128], rhs=wtm[:, kc, c0:c0 + ncol], start=(kc == 0), stop=(kc == 7)), reads=[hk, "wtm"], writes=[f"ps{pb}"])
                    S.add("dve", lambda e, ot=ot: e.tensor_copy(out=ot[:, 0:512], in_=ps[4][:, 0:512]), reads=["ps4"], writes=[otk])
                    S.add("act", lambda e, ot=ot: e.activation(out=ot[:, 512:768], in_=ps[5][:, 0:256], func=AF.Silu), reads=["ps5"], writes=[otk])
                    S.add("sp", lambda e, ot=ot, tt=tt: e.dma_start(out=self.vtok[tt:tt + 128, :], in_=ot[:, 0:512]), reads=[otk], writes=["vtok"], dma=True)
                    S.add("sp", lambda e, ot=ot, tt=tt: e.dma_start(out=self.gateA[tt:tt + 128, :], in_=ot[:, 512:768]), reads=[otk], writes=["gateA"], dma=True)


    def phase_attn(self, l):
        nc, S = self.nc, self.S
        ps = self.ps
        with_ctx = l < DEPTH - 1
        lam_init = 0.8 - 0.6 * math.exp(-0.3 * l)
        with ExitStack() as st:
            sbt = lambda name, shape, dt=F32: st.enter_context(nc.sbuf_tensor(self.uniq(name), shape, dt))
            dl = sbt("dl", [64, 128])
            pr = sbt("pr", [64, 64])
            lam = sbt("lam", [64, 4])
            gB = sbt("gB", [64, 1])
            esink = sbt("esink", [64, 4])
            sel = sbt("sel", [128, 64])
            ones64 = sbt("ones64", [64, 64])
            trige = sbt("trige", [128, 128], BF16)
            trile = sbt("trile", [128, 128], BF16)
            S.add("sp", lambda e: e.dma_start(out=dl[:], in_=self.diff_lambda[l:l + 1, :].partition_broadcast(64)), writes=["dl"], dma=True)
            S.add("sp", lambda e: e.dma_start(out=gB[:], in_=self.diff_norm[l].rearrange("(d o) -> d o", o=1)), writes=["gB"], dma=True)
            S.add("sp", lambda e: e.dma_start(out=esink[:], in_=self.sink_d[l:l + 1, :].partition_broadcast(64)), writes=["esink"], dma=True)
            S.add("sp", lambda e: e.dma_start(out=sel[:], in_=self.cst["sel64"][:, :]), writes=["sel"], dma=True)
            S.add("pool", lambda e: e.dma_start(out=trige[:], in_=self.cst["tri_ge"][:, :]), writes=["trige"], dma=True)
            S.add("pool", lambda e: e.dma_start(out=trile[:], in_=self.cst["tri_le"][:, :]), writes=["trile"], dma=True)
            S.add("dve", lambda e: e.memset(ones64[:], 1.0), writes=["ones64"])
            S.add("dve", lambda e: e.tensor_tensor(out=pr[:].rearrange("p (a b) -> p a b", a=2), in0=dl[:].rearrange("p (a two b) -> p a two b", a=2, two=2)[:, :, 0, :], in1=dl[:].rearrange("p (a two b) -> p a two b", a=2, two=2)[:, :, 1, :], op=ALU.mult), reads=["dl"], writes=["pr"])
            S.add("dve", lambda e: e.tensor_reduce(out=lam[:, 0:2], in_=pr[:].rearrange("p (a b) -> p a b", a=2), axis=AX.X, op=ALU.add), reads=["pr"], writes=["lam"])
            S.add("act", lambda e: e.activation(out=lam[:, 0:2], in_=lam[:, 0:2], func=AF.Exp), reads=["lam"], writes=["lam"])
            S.add("dve", lambda e: e.scalar_tensor_tensor(out=lam[:, 2:3], in0=lam[:, 1:2], scalar=-lam_init, in1=lam[:, 0:1], op0=ALU.add, op1=ALU.subtract), reads=["lam"], writes=["lam"])
            S.add("dve", lambda e: e.tensor_scalar(out=gB[:], in0=gB[:], scalar1=1.0 - lam_init, scalar2=None, op0=ALU.mult), reads=["gB"], writes=["gB"])
            S.add("act", lambda e: e.activation(out=esink[:], in_=esink[:], func=AF.Exp), reads=["esink"], writes=["esink"])
            pT = [sbt(f"pT{i}", [128, 512], BF16) for i in range(8)]
            dsb = [sbt(f"dsb{i}", [128, 512]) for i in range(2)]
            rden = [sbt(f"rden{i}", [64, 512]) for i in range(2)]
            om = [sbt(f"om{i}", [64, 512]) for i in range(2)]
            od = sbt("od", [64, 512])
            sqb = sbt("sqb", [64, 512])
            rs = sbt("rs", [64, 512])
            obf = [sbt(f"obf{i}", [64, 512], BF16) for i in range(2)]
            vaug = sbt("vaug", [128, NT, 4, 128], BF16)
            qT = sbt("qT", [128, 8, T], BF16)
            kT = sbt("kT", [128, 3, T], BF16)
            cnt = {"s": 0, "o": 0}
            S.add("pool", lambda e: e.memset(vaug[:, :, :, 64:128], 1.0), writes=["vaug1"])

            def load_mixer(src, qrows, krows, vcol, nh):
                S.add("dve", lambda e: e.memset(qT[:], 0.0), writes=["qT"])
                S.add("pool", lambda e: e.memset(kT[64:128, :, :], 0.0), writes=["kT"])
                for (slot, ch, r0, nr) in qrows:
                    S.add("sp", lambda e, slot=slot, ch=ch, r0=r0, nr=nr: e.dma_start(out=qT[r0:r0 + nr, slot, :], in_=src[ch, r0:r0 + nr, :]), reads=[src.tensor.name], writes=["qT"], dma=True)
                for (slot, ch, nr) in krows:
                    S.add("act", lambda e, slot=slot, ch=ch, nr=nr: e.dma_start(out=kT[0:nr, slot, :], in_=src[ch, 0:nr, :]), reads=[src.tensor.name], writes=["kT"], dma=True)
                vv = self.vtok[:, vcol:vcol + nh * 64].rearrange("(t p) (h d) -> p t h d", p=128, d=64)
                for h_ in range(nh):
                    for (ta, tb_) in [(0, 17), (17, NT)]:
                        S.add("sp", lambda e, h_=h_, ta=ta, tb_=tb_: e.dma_start(out=vaug[:, ta:tb_, h_, 0:64], in_=vv[:, ta:tb_, h_, :]), reads=["vtok"], writes=["vaug"], dma=True)

            def attend(q_ap_fn, k_ap_fn, vh, scale, ktiles, W, acc_b, first_start=True):
                n = len(ktiles)
                LA = 3
                sbs = []
                for i in range(n + LA):
                    if i < n:
                        kt = ktiles[i]
                        sb_ = cnt["s"] % 4
                        pi_ = cnt["s"] % 8
                        cnt["s"] += 1
                        sbs.append(pi_)
                        S.add("pe", lambda e, kt=kt, sb_=sb_: e.matmul(ps[sb_][:, 0:W], lhsT=k_ap_fn(kt), rhs=q_ap_fn(), start=True, stop=True), reads=["qT", "kT"], writes=[f"ps{sb_}"])
                        S.add("act", lambda e, sb_=sb_, pi_=pi_: e.activation(out=pT[pi_][:, 0:W], in_=ps[sb_][:, 0:W], func=AF.Exp, scale=scale), reads=[f"ps{sb_}"], writes=[f"pT{pi_}"])
                    if i >= LA:
                        i2 = i - LA
                        kt2, sb2 = ktiles[i2], sbs[i2]
                        S.add("pe", lambda e, kt2=kt2, sb2=sb2, i2=i2: e.matmul(ps[acc_b][:, 0:W], lhsT=vaug[:, kt2, vh, :], rhs=pT[sb2][:, 0:W], start=(i2 == 0), stop=(i2 == n - 1)), reads=[f"pT{sb2}", "vaug", "vaug1"], writes=[f"ps{acc_b}"])

            def finish_den(acc_b, W, slot, extra=None):
                d_ = dsb[slot]
                S.add("act", lambda e: e.activation(out=d_[64:128, 0:W], in_=ps[acc_b][64:128, 0:W], func=AF.Copy), reads=[f"ps{acc_b}"], writes=[f"dsb{slot}"])
                S.add("pe", lambda e: e.matmul(ps[6 + slot][0:64, 0:W], lhsT=sel[64:128, :], rhs=d_[64:128, 0:W], start=True, stop=True), reads=[f"dsb{slot}", "sel"], writes=[f"ps{6 + slot}"])
                if extra is not None:
                    S.add("dve", lambda e: e.tensor_scalar(out=rden[slot][:, 0:W], in0=ps[6 + slot][0:64, 0:W], scalar1=extra, scalar2=None, op0=ALU.add), reads=[f"ps{6 + slot}", "esink"], writes=[f"rden{slot}"])
                    S.add("dve", lambda e: e.reciprocal(out=rden[slot][:, 0:W], in_=rden[slot][:, 0:W]), reads=[f"rden{slot}"], writes=[f"rden{slot}"])
                else:
                    S.add("dve", lambda e: e.reciprocal(out=rden[slot][:, 0:W], in_=ps[6 + slot][0:64, 0:W]), reads=[f"ps{6 + slot}"], writes=[f"rden{slot}"])

            def store_out(tile_ap, chunk, r0, t0, W, key):
                S.add("sp", lambda e: e.dma_start(out=self.oT[chunk, r0:r0 + 64, t0:t0 + W], in_=tile_ap), reads=[key], writes=["oT"], dma=True)

            groups = [g for gi, g in enumerate(GROUPS) if gi > 0 or with_ctx]
            all_kt = list(range(NT))
            load_mixer(self.qkB, [(blk, blk // 3, (blk % 3) * 32, 32) for blk in range(8)], [(0, 3, 96), (1, 4, 96), (2, 5, 64)], 0, 4)
            scB = 32 ** -0.5
            for h in range(4):
                for (t0, W) in groups:
                    kts = all_kt if t0 >= NCTX else [0, 1]
                    for m in range(2):
                        blk = h * 2 + m
                        ch, off = blk // 3, (blk % 3) * 32
                        attend(lambda blk=blk, t0=t0, W=W: qT[:, blk, t0:t0 + W],
                               lambda kt, ch=ch: kT[:, ch, kt * 128:(kt + 1) * 128],
                               h, scB, kts, W, 4 + m)
                    for m in range(2):
                        finish_den(4 + m, W, m)
                        S.add("dve", lambda e, m=m, W=W: e.tensor_tensor(out=om[m][:, 0:W], in0=ps[4 + m][0:64, 0:W], in1=rden[m][:, 0:W], op=ALU.mult), reads=[f"ps{4 + m}", f"rden{m}"], writes=[f"om{m}"])
                    S.add("dve", lambda e, W=W: e.scalar_tensor_tensor(out=od[:, 0:W], in0=om[1][:, 0:W], scalar=lam[:, 2:3], in1=om[0][:, 0:W], op0=ALU.mult, op1=ALU.add), reads=["om0", "om1", "lam"], writes=["od"])
                    S.add("act", lambda e, W=W: e.activation(out=sqb[:, 0:W], in_=od[:, 0:W], func=AF.Square), reads=["od"], writes=["sqb"])
                    S.add("pe", lambda e, W=W: e.matmul(ps[6][0:64, 0:W], lhsT=ones64[:], rhs=sqb[:, 0:W], start=True, stop=True), reads=["sqb", "ones64"], writes=["ps6"])
                    S.add("act", lambda e, W=W: e.activation(out=rs[:, 0:W], in_=ps[6][0:64, 0:W], func=AF.Sqrt, scale=1.0 / 64.0, bias=EPS), reads=["ps6"], writes=["rs"])
                    S.add("dve", lambda e, W=W: e.reciprocal(out=rs[:, 0:W], in_=rs[:, 0:W]), reads=["rs"], writes=["rs"])
                    oi = cnt["o"] % 2
                    cnt["o"] += 1
                    S.add("dve", lambda e, W=W, oi=oi: e.scalar_tensor_tensor(out=obf[oi][:, 0:W], in0=od[:, 0:W], scalar=gB[:, 0:1], in1=rs[:, 0:W], op0=ALU.mult, op1=ALU.mult), reads=["od", "gB", "rs"], writes=[f"obf{oi}"])
                    store_out(obf[oi][:, 0:W], 2 + h // 2, (h % 2) * 64, t0, W, f"obf{oi}")
            load_mixer(self.qkC, [(j_ * 2 + g_, g_, j_ * 64, 64) for j_ in range(2) for g_ in range(2)], [(0, 2, 128)], 256, 2)
            scC = 64 ** -0.5
            hi = 0
            for j in range(2):
                for g in range(2):
                    for (t0, W) in groups:
                        kts = all_kt if t0 >= NCTX else [0, 1]
                        ab = 4 + (hi % 2)
                        sl = hi % 2
                        hi += 1
                        attend(lambda j=j, g=g, t0=t0, W=W: qT[:, j * 2 + g, t0:t0 + W],
                               lambda kt: kT[:, 0, kt * 128:(kt + 1) * 128],
                               j, scC, kts, W, ab)
                        finish_den(ab, W, sl)
                        oi = cnt["o"] % 2
                        cnt["o"] += 1
                        S.add("dve", lambda e, W=W, ab=ab, sl=sl, oi=oi: e.tensor_tensor(out=obf[oi][:, 0:W], in0=ps[ab][0:64, 0:W], in1=rden[sl][:, 0:W], op=ALU.mult), reads=[f"ps{ab}", f"rden{sl}"], writes=[f"obf{oi}"])
                        store_out(obf[oi][:, 0:W], 4 + j, g * 64, t0, W, f"obf{oi}")
            load_mixer(self.qkD, [(j_ * 2 + g_, g_, j_ * 64, 64) for j_ in range(2) for g_ in range(2)], [(0, 2, 128)], 384, 2)
            for j in range(2):
                for g in range(2):
                    qh = j * 2 + g
                    for (t0, W) in groups:
                        ab = 4 + (hi % 2)
                        sl = hi % 2
                        hi += 1
                        qf = lambda c0, c1, j=j, g=g, t0=t0: qT[:, j * 2 + g, t0 + c0:t0 + c1]
                        kf = lambda kt: kT[:, 0, kt * 128:(kt + 1) * 128]
                        items = [(0, 0, W, []), (1, 0, W, [])]
                        if t0 >= NCTX:
                            qt0 = (t0 - NCTX) // 128
                            for kt in range(qt0 - 1, qt0 + 5):
                                if kt < 0 or kt > 31:
                                    continue
                                qlo, qhi = max(kt - 1, qt0), min(kt + 1, qt0 + 3)
                                if qlo > qhi:
                                    continue
                                masks = []
                                for qt in range(qlo, qhi + 1):
                                    if qt == kt + 1:
                                        masks.append(("trige", (qt - qt0) * 128))
                                    elif qt == kt - 1:
                                        masks.append(("trile", (qt - qt0) * 128))
                                items.append((kt + 2, (qlo - qt0) * 128, (qhi - qt0 + 1) * 128, masks))
                        n = len(items)
                        LA = 3
                        sbs = []
                        for ii in range(n + LA):
                            if ii < n:
                                (ktile, c0, c1, masks) = items[ii]
                                sb_ = cnt["s"] % 4
                                pi_ = cnt["s"] % 8
                                cnt["s"] += 1
                                sbs.append(pi_)
                                S.add("pe", lambda e, ktile=ktile, sb_=sb_, c0=c0, c1=c1, qf=qf, kf=kf: e.matmul(ps[sb_][:, c0:c1], lhsT=kf(ktile), rhs=qf(c0, c1), start=True, stop=True), reads=["qT", "kT"], writes=[f"ps{sb_}"])
                                S.add("act", lambda e, sb_=sb_, pi_=pi_, c0=c0, c1=c1: e.activation(out=pT[pi_][:, c0:c1], in_=ps[sb_][:, c0:c1], func=AF.Exp, scale=scC), reads=[f"ps{sb_}"], writes=[f"pT{pi_}"])
                                for (mk, mc0) in masks:
                                    mt = trige if mk == "trige" else trile
                                    S.add("pool", lambda e, pi_=pi_, mt=mt, mc0=mc0: e.tensor_tensor(out=pT[pi_][:, mc0:mc0 + 128], in0=pT[pi_][:, mc0:mc0 + 128], in1=mt[:], op=ALU.mult), reads=[f"pT{pi_}", mk], writes=[f"pT{pi_}"])
                            if ii >= LA:
                                i = ii - LA
                                (ktile, c0, c1, masks) = items[i]
                                sb2 = sbs[i]
                                S.add("pe", lambda e, ktile=ktile, sb2=sb2, c0=c0, c1=c1, i=i, j=j, ab=ab, n=n: e.matmul(ps[ab][:, c0:c1], lhsT=vaug[:, ktile, j, :], rhs=pT[sb2][:, c0:c1], start=(i == 0), stop=(i == n - 1)), reads=[f"pT{sb2}", "vaug", "vaug1"], writes=[f"ps{ab}"])
                        finish_den(ab, W, sl, extra=esink[:, qh:qh + 1])
                        oi = cnt["o"] % 2
                        cnt["o"] += 1
                        S.add("dve", lambda e, W=W, ab=ab, sl=sl, oi=oi: e.tensor_tensor(out=obf[oi][:, 0:W], in0=ps[ab][0:64, 0:W], in1=rden[sl][:, 0:W], op=ALU.mult), reads=[f"ps{ab}", f"rden{sl}"], writes=[f"obf{oi}"])
                        store_out(obf[oi][:, 0:W], 6 + j, g * 64, t0, W, f"obf{oi}")


    def phase_merge(self, l):
        nc, S = self.nc, self.S
        ps = self.ps
        with_ctx = l < DEPTH - 1
        groups = [(gi, g) for gi, g in enumerate(GROUPS) if gi > 0 or with_ctx]
        wv = self.w_in[l].rearrange("(kc p) n -> p kc n", p=128)
        for pa in range(2):
            with ExitStack() as st:
                sbt = lambda name, shape, dt=F32: st.enter_context(nc.sbuf_tensor(self.uniq(name), shape, dt))
                wg = sbt("wg", [128, 8, 2048], BF16)
                wbr = sbt("wbr", [128, 4, 2, 512], BF16)
                for i in range(4):
                    c0 = C_GL + i * 1024 + pa * 512
                    S.add("pool", lambda e, i=i, c0=c0: e.dma_start(out=wg[:, :, i * 512:(i + 1) * 512], in_=wv[:, :, c0:c0 + 512]), writes=["wg"], dma=True)
                    S.add("pool", lambda e, i=i, pa=pa: e.dma_start(out=wbr[:, i], in_=self.w_br[l, i].rearrange("(kc p) n -> p kc n", p=128)[:, :, pa * 512:(pa + 1) * 512]), writes=["wbr"], dma=True)
                hg = [sbt(f"hg{i}", [128, 8, 512], BF16) for i in range(2)]
                og = [sbt(f"og{i}", [128, 8, 512], BF16) for i in range(2)]
                sg = [sbt(f"sg{i}", [128, 512]) for i in range(2)]
                tmp = [sbt(f"tmp{i}", [128, 512]) for i in range(2)]
                macc = sbt("macc", [128, 512])
                dbt = sbt("dbt", [128, 512])
                mT = [sbt(f"mT{i}", [128, 8, 512], BF16) for i in range(2)]
                if pa == 1:
                    wout = sbt("wout", [128, 8, 1024], BF16)
                    for kc in range(0, 8, 2):
                        S.add("pool", lambda e, kc=kc: e.dma_start(out=wout[:, kc:kc + 2, :], in_=self.w_out[l].rearrange("(kc p) n -> p kc n", p=128)[:, kc:kc + 2, :]), writes=["wout"], dma=True)
                    wr = sbt("wr", [128, 8, 16])
                    S.add("sp", lambda e: e.dma_start(out=wr[:], in_=self.w_router[l].rearrange("(kc p) n -> p kc n", p=128)), writes=["wr"], dma=True)
                    bc = {}
                    for nm in ["g1_0", "g1_1", "sc2_0", "sc2_1", "sh2_0", "sh2_1", "lng", "lnb"]:
                        bc[nm] = sbt("bc_" + nm, [128, 1024])
                    for s_ in range(2):
                        self.bcast_mod("sp", bc[f"g1_{s_}"], 2, s_, f"bc_g1_{s_}")
                        self.bcast_mod("sp", bc[f"sc2_{s_}"], 4, s_, f"bc_sc2_{s_}")
                        self.bcast_mod("sp", bc[f"sh2_{s_}"], 3, s_, f"bc_sh2_{s_}")
                        S.add("pool", lambda e, s_=s_: e.tensor_scalar(out=bc[f"sc2_{s_}"][:], in0=bc[f"sc2_{s_}"][:], scalar1=1.0, scalar2=None, op0=ALU.add), reads=[f"bc_sc2_{s_}"], writes=[f"bc_sc2_{s_}"])
                    S.add("sp", lambda e: e.dma_start(out=bc["lng"][:], in_=self.ln1_g[l:l + 1, :].partition_broadcast(128)), writes=["bc_lng"], dma=True)
                    S.add("sp", lambda e: e.dma_start(out=bc["lnb"][:], in_=self.ln1_b[l:l + 1, :].partition_broadcast(128)), writes=["bc_lnb"], dma=True)
                    xt = [sbt(f"mxt{i}", [128, 1024]) for i in range(2)]
                    rt = [sbt(f"mrt{i}", [128, 1024]) for i in range(2)]
                    x1 = [sbt(f"mx1{i}", [128, 1024]) for i in range(2)]
                    h2f = sbt("h2f", [128, 1024])
                    h2b = [sbt(f"h2b{i}", [128, 1024], BF16) for i in range(2)]
                    h2T = sbt("h2Tf", [128, 8, 128])
                    bst = sbt("bst", [128, 2, 6])
                    mv = sbt("mv", [128, 2])
                    rsd = sbt("rsd", [128, 1])
                    ex = sbt("ex", [128, 16])
                    esum = sbt("esum", [128, 1])
                    afft = [sbt(f"afft{i}", [128, 16]) for i in range(2)]
                tcount = 0
                pcnt = 0
                mpend = []
                for (gi, (t0, W)) in groups:
                    s_ = 1 if gi == 0 else 0
                    hb, hk = hg[gi % 2], f"hg{gi % 2}"
                    obf, ok_ = og[gi % 2], f"og{gi % 2}"
                    mt, mk = mT[gi % 2], f"mT{gi % 2}"
                    S.add("sp", lambda e, hb=hb, t0=t0, W=W: e.dma_start(out=hb[:, :, 0:W], in_=self.hT[:, :, t0:t0 + W].rearrange("c p t -> p c t")), reads=["hT"], writes=[hk], dma=True)
                    S.add("act", lambda e, obf=obf, t0=t0, W=W: e.dma_start(out=obf[:, :, 0:W], in_=self.oT[:, :, t0:t0 + W].rearrange("c p t -> p c t")), reads=["oT"], writes=[ok_], dma=True)
                    if pa == 1:
                        S.add("sp", lambda e, mt=mt, t0=t0, W=W: e.dma_start(out=mt[:, 0:4, 0:W], in_=self.mTd[:, :, t0:t0 + W].rearrange("c p t -> p c t")), reads=["mTd"], writes=[mk], dma=True)
                    for fcl in range(4):
                        fc = pa * 4 + fcl
                        for i in range(4):
                            pa_, pb_ = pcnt % 2, 2 + pcnt % 2
                            pcnt += 1
                            for kc in range(8):
                                S.add("pe", lambda e, i=i, fcl=fcl, kc=kc, hb=hb, W=W, pa_=pa_: e.matmul(ps[pa_][:, 0:W], lhsT=wg[:, kc, i * 512 + fcl * 128:i * 512 + (fcl + 1) * 128], rhs=hb[:, kc, 0:W], start=(kc == 0), stop=(kc == 7)), reads=["wg", hk], writes=[f"ps{pa_}"])
                            sgi = sg[i % 2]
                            S.add("act", lambda e, sgi=sgi, W=W, pa_=pa_: e.activation(out=sgi[:, 0:W], in_=ps[pa_][:, 0:W], func=AF.Sigmoid), reads=[f"ps{pa_}"], writes=[f"sg{i % 2}"])
                            for k2 in range(2):
                                S.add("pe", lambda e, i=i, fcl=fcl, k2=k2, obf=obf, W=W, pb_=pb_: e.matmul(ps[pb_][:, 0:W], lhsT=wbr[:, i, k2, fcl * 128:(fcl + 1) * 128], rhs=obf[:, i * 2 + k2, 0:W], start=(k2 == 0), stop=(k2 == 1)), reads=["wbr", ok_], writes=[f"ps{pb_}"])
                            if i == 0:
                                S.add("dve", lambda e, sgi=sgi, W=W, pb_=pb_: e.tensor_tensor(out=macc[:, 0:W], in0=sgi[:, 0:W], in1=ps[pb_][:, 0:W], op=ALU.mult), reads=[f"sg{i % 2}", f"ps{pb_}"], writes=["macc"])
                            else:
                                tm = tmp[i % 2]
                                S.add("dve", lambda e, sgi=sgi, tm=tm, W=W, pb_=pb_: e.tensor_tensor(out=tm[:, 0:W], in0=sgi[:, 0:W], in1=ps[pb_][:, 0:W], op=ALU.mult), reads=[f"sg{i % 2}", f"ps{pb_}"], writes=[f"tmp{i % 2}"])
                                if i < 3:
                                    S.add("pool", lambda e, tm=tm, W=W: e.tensor_tensor(out=macc[:, 0:W], in0=macc[:, 0:W], in1=tm[:, 0:W], op=ALU.add), reads=["macc", f"tmp{i % 2}"], writes=["macc"])
                                else:
                                    S.add("pool", lambda e, tm=tm, W=W, mt=mt, fc=fc: e.tensor_tensor(out=mt[:, fc, 0:W], in0=macc[:, 0:W], in1=tm[:, 0:W], op=ALU.add), reads=["macc", f"tmp{i % 2}"], writes=[mk])
                    if pa == 0:
                        S.add("sp", lambda e, mt=mt, t0=t0, W=W: e.dma_start(out=self.mTd[:, :, t0:t0 + W].rearrange("c p t -> p c t"), in_=mt[:, 0:4, 0:W]), reads=[mk], writes=["mTd"], dma=True)
                        continue
                    for j in range(W // 128):
                        tt = t0 + j * 128
                        b2 = tcount % 2
                        tcount += 1
                        xb, xk = xt[b2], f"mxt{b2}"
                        rb, rk = rt[b2], f"mrt{b2}"
                        x1b, x1k = x1[b2], f"mx1{b2}"
                        S.add("sp", lambda e, xb=xb, tt=tt: e.dma_start(out=xb[:], in_=self.xres[tt:tt + 128, :]), reads=["xres"], writes=[xk], dma=True)
                        for half in range(2):
                            for kc in range(8):
                                S.add("pe", lambda e, mt=mt, kc=kc, j=j, half=half: e.matmul(ps[4 + half][:, :], lhsT=mt[:, kc, j * 128:(j + 1) * 128], rhs=wout[:, kc, half * 512:(half + 1) * 512], start=(kc == 0), stop=(kc == 7)), reads=[mk, "wout"], writes=[f"ps{4 + half}"])
                            S.add("dve", lambda e, rb=rb, half=half, s_=s_: e.tensor_tensor(out=rb[:, half * 512:(half + 1) * 512], in0=ps[4 + half][:, :], in1=bc[f"g1_{s_}"][:, half * 512:(half + 1) * 512], op=ALU.mult), reads=[f"ps{4 + half}", f"bc_g1_{s_}"], writes=[rk])
                        S.add("dve", lambda e, rb=rb, xb=xb: e.scalar_tensor_tensor(out=rb[:], in0=xb[:], scalar=ALPHA, in1=rb[:], op0=ALU.mult, op1=ALU.add), reads=[xk, rk], writes=[rk])
                        self.layer_norm_tile(rb, rk, x1b, x1k, bc["lng"], "bc_lng", bc["lnb"], "bc_lnb", bst, mv, rsd)
                        S.add("sp", lambda e, x1b=x1b, tt=tt: e.dma_start(out=self.xres[tt:tt + 128, :], in_=x1b[:]), reads=[x1k], writes=["xres"], dma=True)
                        def post(b2=b2, tt=tt, x1b=x1b, x1k=x1k, s_=s_):
                            hb2, hb2k = h2b[b2], f"h2b{b2}"
                            S.add("pool", lambda e, x1b=x1b, s_=s_: e.tensor_tensor(out=h2f[:], in0=x1b[:], in1=bc[f"sc2_{s_}"][:], op=ALU.mult), reads=[x1k, f"bc_sc2_{s_}"], writes=["h2f"])
                            S.add("pool", lambda e, hb2=hb2, s_=s_: e.tensor_tensor(out=hb2[:], in0=h2f[:], in1=bc[f"sh2_{s_}"][:], op=ALU.add), reads=["h2f", f"bc_sh2_{s_}"], writes=[hb2k])
                            S.add("sp", lambda e, hb2=hb2, tt=tt: e.dma_start(out=self.h2tok[tt:tt + 128, :], in_=hb2[:]), reads=[hb2k], writes=["h2tok"], dma=True)
                            for half in range(2):
                                for q in range(4):
                                    fc = half * 4 + q
                                    S.add("pe", lambda e, x1b=x1b, fc=fc, half=half, q=q: e.transpose(ps[6 + half][:, q * 128:(q + 1) * 128], x1b[:, fc * 128:(fc + 1) * 128], self.ident[:]), reads=[x1k, "ident"], writes=[f"ps{6 + half}"])
                                for q in range(4):
                                    fc = half * 4 + q
                                    S.add("act", lambda e, fc=fc, half=half, q=q, s_=s_: e.activation(out=h2T[:, fc, :], in_=ps[6 + half][:, q * 128:(q + 1) * 128], func=AF.Identity, scale=self.modP[:, 1, fc, s_:s_ + 1], bias=self.modT[:, 24 + fc, s_:s_ + 1]), reads=[f"ps{6 + half}", "modP", "modT"], writes=["h2Tf"])
                            for kc in range(8):
                                S.add("pe", lambda e, kc=kc: e.matmul(ps[6][:, 0:16], lhsT=h2T[:, kc, :], rhs=wr[:, kc, :], start=(kc == 0), stop=(kc == 7)), reads=["h2Tf", "wr"], writes=["ps6"])
                            af, afk = afft[b2], f"afft{b2}"
                            S.add("act", lambda e: e.activation(out=ex[:], in_=ps[6][:, 0:16], func=AF.Exp, accum_out=esum[:]), reads=["ps6"], writes=["ex", "esum"])
                            S.add("dve", lambda e: e.reciprocal(out=esum[:], in_=esum[:]), reads=["esum"], writes=["esum"])
                            S.add("dve", lambda e, af=af: e.tensor_scalar(out=af[:], in0=ex[:], scalar1=esum[:, 0:1], scalar2=None, op0=ALU.mult), reads=["ex", "esum"], writes=[afk])
                            S.add("sp", lambda e, af=af, tt=tt: e.dma_start(out=self.aff[tt:tt + 128, :], in_=af[:]), reads=[afk], writes=["aff"], dma=True)
                        if mpend:
                            mpend.pop()()
                        mpend.append(post)
                while mpend:
                    mpend.pop()()
            self.S.barrier()

    def layer_norm_tile(self, rb, rk, ob_, ok_, g, gk, b, bk, bst, mv, rsd):
        S = self.S
        for half in range(2):
            S.add("dve", lambda e, half=half: e.bn_stats(out=bst[:, half, :], in_=rb[:, half * 512:(half + 1) * 512]), reads=[rk], writes=["bst"])
        S.add("dve", lambda e: e.bn_aggr(out=mv[:], in_=bst[:].rearrange("p a b -> p (a b)")), reads=["bst"], writes=["mv"])
        S.add("act", lambda e: e.activation(out=rsd[:], in_=mv[:, 1:2], func=AF.Sqrt, bias=EPS, scale=1.0), reads=["mv"], writes=["rsd"])
        S.add("dve", lambda e: e.reciprocal(out=rsd[:], in_=rsd[:]), reads=["rsd"], writes=["rsd"])
        S.add("dve", lambda e: e.tensor_scalar(out=rb[:], in0=rb[:], scalar1=mv[:, 0:1], scalar2=rsd[:, 0:1], op0=ALU.subtract, op1=ALU.mult), reads=[rk, "mv", "rsd"], writes=[rk])
        S.add("pool", lambda e: e.tensor_tensor(out=rb[:], in0=rb[:], in1=g[:], op=ALU.mult), reads=[rk, gk], writes=[rk])
        S.add("pool", lambda e: e.tensor_tensor(out=ob_[:], in0=rb[:], in1=b[:], op=ALU.add), reads=[rk, bk], writes=[ok_])


    def phase_moe(self, l):
        nc, S = self.nc, self.S
        ps = self.ps
        with_ctx = l < DEPTH - 1
        sets = [("L", NCTX, NLAT, 512)] + ([("C", 0, NCTX, 32)] if with_ctx else [])
        with ExitStack() as st:
            sbt = lambda name, shape, dt=F32: st.enter_context(nc.sbuf_tensor(self.uniq(name), shape, dt))
            aft = sbt("aft", [128, NT, 16])
            affT = sbt("affT", [16, T])
            junk = sbt("junk", [16, NLAT], BF16)
            ones = sbt("ones", [16, NLAT])
            msk = sbt("msk", [16, T])
            cum = sbt("cum", [16, T])
            posm = sbt("posm", [16, T])
            gv = sbt("gv", [16, T])
            sc_ = {k: sbt("bs_" + k, [16, 1]) for k in ["lo", "hi", "mid", "cnt", "ge", "d1", "d2"]}
            ptok = sbt("ptok", [128, NT, 16])
            S.add("sp", lambda e: e.dma_start(out=aft[:], in_=self.aff.rearrange("(t p) e -> p t e", p=128)), reads=["aff"], writes=["aft"], dma=True)
            S.add("pool", lambda e: e.memset(ones[:], 1.0), writes=["ones"])
            for t4 in range(0, NT, 4):
                nt_ = min(4, NT - t4)
                pb = (t4 // 4) % 2
                for q in range(nt_):
                    S.add("pe", lambda e, t4=t4, q=q, pb=pb: e.transpose(ps[pb][0:16, q * 128:(q + 1) * 128], aft[:, t4 + q, :], self.ident[:]), reads=["aft", "ident"], writes=[f"ps{pb}"])
                S.add("act", lambda e, t4=t4, nt_=nt_, pb=pb: e.activation(out=affT[:, t4 * 128:(t4 + nt_) * 128], in_=ps[pb][0:16, 0:nt_ * 128], func=AF.Copy), reads=[f"ps{pb}"], writes=["affT"])
            for (nm, tk0, n, cap) in sets:
                A = affT[:, tk0:tk0 + n]
                S.add("dve", lambda e: e.memset(sc_["lo"][:], 0.0), writes=["bs_lo"])
                S.add("dve", lambda e: e.memset(sc_["hi"][:], 1.0), writes=["bs_hi"])
                for it in range(30):
                    S.add("dve", lambda e: e.tensor_scalar(out=sc_["mid"][:], in0=sc_["lo"][:], scalar1=sc_["hi"][:, 0:1], scalar2=0.5, op0=ALU.add, op1=ALU.mult), reads=["bs_lo", "bs_hi"], writes=["bs_mid"])
                    S.add("dve", lambda e, A=A, n=n: e.tensor_scalar(out=junk[:, 0:n], in0=A, scalar1=sc_["mid"][:, 0:1], scalar2=0.0, op0=ALU.is_ge, op1=ALU.add, accum_out=sc_["cnt"][:]), reads=["affT", "bs_mid"], writes=["junk", "bs_cnt"])
                    S.add("dve", lambda e, cap=cap: e.tensor_scalar(out=sc_["ge"][:], in0=sc_["cnt"][:], scalar1=cap - 0.5, scalar2=None, op0=ALU.is_ge), reads=["bs_cnt"], writes=["bs_ge"])
                    S.add("dve", lambda e: e.scalar_tensor_tensor(out=sc_["lo"][:], in0=sc_["mid"][:], scalar=sc_["ge"][:, 0:1], in1=sc_["lo"][:], op0=ALU.mult, op1=ALU.max), reads=["bs_mid", "bs_ge", "bs_lo"], writes=["bs_lo"])
                    S.add("dve", lambda e: e.scalar_tensor_tensor(out=sc_["hi"][:], in0=sc_["hi"][:], scalar=sc_["ge"][:, 0:1], in1=sc_["mid"][:], op0=ALU.mult, op1=ALU.max), reads=["bs_hi", "bs_ge", "bs_mid"], writes=["bs_hi"])
                S.add("dve", lambda e, A=A, tk0=tk0, n=n: e.tensor_scalar(out=msk[:, tk0:tk0 + n], in0=A, scalar1=sc_["lo"][:, 0:1], scalar2=None, op0=ALU.is_ge), reads=["affT", "bs_lo"], writes=["msk"])
                S.add("dve", lambda e, tk0=tk0, n=n: e.tensor_tensor_scan(out=cum[:, tk0:tk0 + n], data0=ones[:, 0:n], data1=msk[:, tk0:tk0 + n], initial=0.0, op0=ALU.mult, op1=ALU.add), reads=["ones", "msk"], writes=["cum"])
                S.add("dve", lambda e, tk0=tk0, n=n: e.tensor_tensor(out=cum[:, tk0:tk0 + n], in0=cum[:, tk0:tk0 + n], in1=msk[:, tk0:tk0 + n], op=ALU.mult), reads=["cum", "msk"], writes=["cum"])
                S.add("pool", lambda e, tk0=tk0, n=n: e.tensor_scalar(out=posm[:, tk0:tk0 + n], in0=cum[:, tk0:tk0 + n], scalar1=-1.0, scalar2=None, op0=ALU.add), reads=["cum"], writes=["posm"])
                S.add("pool", lambda e, A=A, tk0=tk0, n=n: e.tensor_tensor(out=gv[:, tk0:tk0 + n], in0=A, in1=msk[:, tk0:tk0 + n], op=ALU.mult), reads=["affT", "msk"], writes=["gv"])
            for t4 in range(0, NT, 32):
                nt_ = min(32, NT - t4)
                pb = 2 + (t4 // 32) % 2
                for q in range(nt_):
                    S.add("pe", lambda e, t4=t4, q=q, pb=pb: e.transpose(ps[pb][:, q * 16:(q + 1) * 16], posm[:, (t4 + q) * 128:(t4 + q + 1) * 128], self.ident[0:16, 0:16]), reads=["posm", "ident"], writes=[f"ps{pb}"])
                S.add("act", lambda e, t4=t4, nt_=nt_, pb=pb: e.activation(out=ptok[:, t4:t4 + nt_, :].rearrange("p t e -> p (t e)"), in_=ps[pb][:, 0:nt_ * 16], func=AF.Copy), reads=[f"ps{pb}"], writes=["ptok"])
            S.add("sp", lambda e: e.dma_start(out=self.ptokd[:, :, :], in_=ptok[:]), reads=["ptok"], writes=["ptokd"], dma=True)
            pos16 = sbt("pos16", [16, T], FP16)
            gv16 = sbt("gv16", [16, T], BF16)
            S.add("dve", lambda e: e.tensor_copy(out=pos16[:], in_=posm[:]), reads=["posm"], writes=["pos16"])
            S.add("pool", lambda e: e.tensor_copy(out=gv16[:], in_=gv[:]), reads=["gv"], writes=["gv16"])
            S.add("sp", lambda e: e.dma_start(out=self.posd.rearrange("t e q -> e t q"), in_=pos16[:].rearrange("e (t q) -> e t q", q=128)), reads=["pos16"], writes=["posd"], dma=True)
            S.add("sp", lambda e: e.dma_start(out=self.gvd.rearrange("t e q -> e t q"), in_=gv16[:].rearrange("e (t q) -> e t q", q=128)), reads=["gv16"], writes=["gvd"], dma=True)
        self.S.barrier()
        if "stop_m2" in self.debug:
            return
        with ExitStack() as st:
            sbt = lambda name, shape, dt=F32: st.enter_context(nc.sbuf_tensor(self.uniq(name), shape, dt))
            h2l = sbt("h2l", [128, NT, D], BF16)
            ptok3 = sbt("ptok2", [128, NT, 16])
            iota = sbt("iota", [128, 512])
            selL = sbt("selL", [128, 32, 512], BF16)
            selC = sbt("selC", [128, 2, 32], BF16)
            wgu = [sbt(f"wgu{i}", [128, 8, 512], BF16) for i in range(4)]
            wdn = [sbt(f"wdn{i}", [128, 8, 512], BF16) for i in range(2)]
            xg = {"L": sbt("xgL", [128, 8, 512], BF16), "C": sbt("xgC", [128, 8, 32], BF16)}
            actT = {"L": sbt("actL", [128, 8, 512], BF16), "C": sbt("actC", [128, 8, 32], BF16)}
            sa = [sbt(f"sa{i}", [128, 512]) for i in range(2)]
            yt = [sbt(f"yt{i}", [128, D], BF16) for i in range(2)]
            for t8 in range(0, NT, 6):
                te = min(NT, t8 + 6)
                S.add("sp" if (t8 // 6) % 2 == 0 else "act", lambda e, t8=t8, te=te: e.dma_start(out=h2l[:, t8:te, :], in_=self.h2tok[t8 * 128:te * 128, :].rearrange("(t p) f -> p t f", p=128)), reads=["h2tok"], writes=["h2l"], dma=True)
            S.add("sp", lambda e: e.dma_start(out=ptok3[:], in_=self.ptokd[:, :, :]), reads=["ptokd"], writes=["ptok2"], dma=True)
            S.add("sp", lambda e: e.dma_start(out=iota[:], in_=self.cst["iota512"][:, :]), writes=["iota"], dma=True)
            pc = {"g": 0, "w": 0, "d": 0, "y": 0, "s": 0}
            for ex_ in range(16):
                for hh_ in range(2):
                    S.add("dve", lambda e, ex_=ex_, hh_=hh_: e.tensor_tensor(out=selL[:, hh_ * 16:(hh_ + 1) * 16, :], in0=iota[:].unsqueeze(1).broadcast_to([128, 16, 512]), in1=ptok3[:, 2 + hh_ * 16:18 + hh_ * 16, ex_:ex_ + 1].broadcast_to([128, 16, 512]), op=ALU.is_equal), reads=["iota", "ptok2"], writes=[f"selL{hh_}"])
                if with_ctx:
                    S.add("dve", lambda e, ex_=ex_: e.tensor_tensor(out=selC[:], in0=iota[:, 0:32].unsqueeze(1).broadcast_to([128, 2, 32]), in1=ptok3[:, 0:2, ex_:ex_ + 1].broadcast_to([128, 2, 32]), op=ALU.is_equal), reads=["iota", "ptok2"], writes=["selC"])
                for (nm, tk0, n, cap) in sets:
                    selt, sk = (selL, "selL") if nm == "L" else (selC, "selC")
                    skf = (lambda tt_: f"selL{tt_ // 16}") if nm == "L" else (lambda tt_: "selC")
                    tile0 = tk0 // 128
                    for fc in range(8):
                        pb = pc["g"] % 2
                        pc["g"] += 1
                        ntl = n // 128
                        for tt in range(ntl):
                            S.add("pe", lambda e, fc=fc, tt=tt, pb=pb, selt=selt, tile0=tile0, cap=cap, ntl=ntl: e.matmul(ps[pb][:, 0:cap], lhsT=h2l[:, tile0 + tt, fc * 128:(fc + 1) * 128], rhs=selt[:, tt, :], start=(tt == 0), stop=(tt == ntl - 1)), reads=["h2l", skf(tt)], writes=[f"ps{pb}"])
                        S.add("act", lambda e, fc=fc, pb=pb, nm=nm, cap=cap: e.activation(out=xg[nm][:, fc, :], in_=ps[pb][:, 0:cap], func=AF.Copy), reads=[f"ps{pb}"], writes=["xg" + nm])
                for hf in range(2):
                    wts = []
                    for wi, wsrc in enumerate([self.w_gate_e, self.w_up_e]):
                        wb = pc["w"] % 4
                        pc["w"] += 1
                        S.add("pool", lambda e, wb=wb, wsrc=wsrc, ex_=ex_, hf=hf: e.dma_start(out=wgu[wb][:], in_=wsrc[l, ex_].rearrange("(kc p) n -> p kc n", p=128)[:, :, hf * 512:(hf + 1) * 512]), writes=[f"wgu{wb}"], dma=True)
                        wts.append(wb)
                    for fq in range(4):
                        fpc = hf * 4 + fq
                        for (nm, tk0, n, cap) in sets:
                            for wi in range(2):
                                pb = 2 + wi + 2 * (pc["s"] % 2)
                                for kc in range(8):
                                    S.add("pe", lambda e, wi=wi, kc=kc, fq=fq, pb=pb, nm=nm, cap=cap, wts=tuple(wts): e.matmul(ps[pb][:, 0:cap], lhsT=wgu[wts[wi]][:, kc, fq * 128:(fq + 1) * 128], rhs=xg[nm][:, kc, :], start=(kc == 0), stop=(kc == 7)), reads=[f"wgu{wts[wi]}", "xg" + nm], writes=[f"ps{pb}"])
                            pa_ = 2 + 2 * (pc["s"] % 2)
                            si = pc["s"] % 2
                            pc["s"] += 1
                            S.add("act", lambda e, pa_=pa_, si=si, cap=cap: e.activation(out=sa[si][:, 0:cap], in_=ps[pa_][:, 0:cap], func=AF.Silu), reads=[f"ps{pa_}"], writes=[f"sa{si}"])
                            S.add("dve", lambda e, pa_=pa_, si=si, cap=cap, nm=nm, fpc=fpc: e.tensor_tensor(out=actT[nm][:, fpc, :], in0=sa[si][:, 0:cap], in1=ps[pa_ + 1][:, 0:cap], op=ALU.mult), reads=[f"sa{si}", f"ps{pa_ + 1}"], writes=["act" + nm])
                wds = []
                for hf in range(2):
                    wb = pc["d"] % 2
                    pc["d"] += 1
                    S.add("pool", lambda e, wb=wb, ex_=ex_, hf=hf: e.dma_start(out=wdn[wb][:], in_=self.w_down_e[l, ex_].rearrange("(kc p) n -> p kc n", p=128)[:, :, hf * 512:(hf + 1) * 512]), writes=[f"wdn{wb}"], dma=True)
                    wds.append(wb)
                for (nm, tk0, n, cap) in sets:
                    njc = max(1, cap // 128)
                    M = min(cap, 128)
                    for jc in range(njc):
                        yi = pc["y"] % 2
                        pc["y"] += 1
                        for hf in range(2):
                            pb = hf
                            for kc in range(8):
                                S.add("pe", lambda e, kc=kc, jc=jc, hf=hf, pb=pb, nm=nm, M=M, wds=tuple(wds): e.matmul(ps[pb][0:M, :], lhsT=actT[nm][:, kc, jc * 128:jc * 128 + M], rhs=wdn[wds[hf]][:, kc, :], start=(kc == 0), stop=(kc == 7)), reads=["act" + nm, f"wdn{wds[hf]}"], writes=[f"ps{pb}"])
                            if hf == 0:
                                S.add("act", lambda e, yi=yi, M=M: e.activation(out=yt[yi][0:M, 0:512], in_=ps[0][0:M, :], func=AF.Copy), reads=["ps0"], writes=[f"yt{yi}"])
                            else:
                                S.add("dve", lambda e, yi=yi, M=M: e.tensor_copy(out=yt[yi][0:M, 512:1024], in_=ps[1][0:M, :]), reads=["ps1"], writes=[f"yt{yi}"])
                        if nm == "L":
                            S.add("sp", lambda e, yi=yi, ex_=ex_, jc=jc: e.dma_start(out=self.ygL[ex_, jc * 128:(jc + 1) * 128, :], in_=yt[yi][:, :]), reads=[f"yt{yi}"], writes=["ygL"], dma=True)
                        else:
                            S.add("sp", lambda e, yi=yi, ex_=ex_: e.dma_start(out=self.ygC[ex_, :, :], in_=yt[yi][0:32, :]), reads=[f"yt{yi}"], writes=["ygC"], dma=True)
        self.S.barrier()
        if "stop_m3" in self.debug:
            return
        with ExitStack() as st:
            sbt = lambda name, shape, dt=F32: st.enter_context(nc.sbuf_tensor(self.uniq(name), shape, dt))
            ygh = sbt("ygh", [128, 64, 512], BF16)
            ygc = sbt("ygc", [32, 16, 512], BF16)
            iota4 = sbt("iota4", [128, 4])
            pbt = [sbt(f"pbt{i}", [128, 16, 128], FP16) for i in range(2)]
            gvt = [sbt(f"gvt{i}", [128, 16, 128], BF16) for i in range(2)]
            selT = [sbt(f"selT{i}", [128, 16, 4, 128], BF16) for i in range(2)]
            fh = [sbt(f"fh{i}", [128, 512]) for i in range(2)]
            f0 = [sbt(f"f0{i}", [128, 512]) for i in range(2)]
            xt = [sbt(f"cxt{i}", [128, D]) for i in range(2)]
            rt = [sbt(f"crt{i}", [128, D]) for i in range(2)]
            bc = {nm: sbt("cbc_" + nm, [128, D]) for nm in ["g2_0", "g2_1", "lng", "lnb"]}
            bst = sbt("cbst", [128, 2, 6])
            mv = sbt("cmv", [128, 2])
            rsd = sbt("crsd", [128, 1])
            S.add("sp", lambda e: e.dma_start(out=iota4[:], in_=self.cst["iota4"][:, :]), writes=["iota4"], dma=True)
            for s_ in range(2):
                self.bcast_mod("sp", bc[f"g2_{s_}"], 5, s_, f"cbc_g2_{s_}")
            S.add("sp", lambda e: e.dma_start(out=bc["lng"][:], in_=self.ln2_g[l:l + 1, :].partition_broadcast(128)), writes=["cbc_lng"], dma=True)
            S.add("sp", lambda e: e.dma_start(out=bc["lnb"][:], in_=self.ln2_b[l:l + 1, :].partition_broadcast(128)), writes=["cbc_lnb"], dma=True)
            tcnt = 0
            pend = []
            for (nm, tk0, n, cap) in sets:
                tile0 = tk0 // 128
                ntl = n // 128
                njc = max(1, cap // 128)
                M = min(cap, 128)
                s_ = 0 if nm == "L" else 1
                for hf in range(2):
                    if nm == "C":
                        S.add("sp", lambda e, hf=hf: e.dma_start(out=ygc[:], in_=self.ygC[:, :, hf * 512:(hf + 1) * 512].rearrange("e j f -> j e f")), reads=["ygC"], writes=["ygc"], dma=True)
                    if nm == "L":
                        for e4 in range(0, 16, 4):
                            S.add("sp" if (e4 // 4) % 2 == 0 else "act", lambda e, e4=e4, hf=hf: e.dma_start(out=ygh[:, e4 * 4:(e4 + 4) * 4, :], in_=self.ygL[e4:e4 + 4, :, hf * 512:(hf + 1) * 512].rearrange("e (jc p) f -> p (e jc) f", p=128)), reads=["ygL"], writes=["ygh"], dma=True)
                    for tt in range(ntl):
                        tile = tile0 + tt
                        tok = tile * 128
                        b2 = tcnt % 2
                        tcnt += 1
                        pb_, gv_, st_ = pbt[b2], gvt[b2], selT[b2]
                        S.add("sp", lambda e, pb_=pb_, tile=tile: e.dma_start(out=pb_[:].rearrange("p e q -> p (e q)"), in_=self.posd[tile].rearrange("e q -> (e q)").partition_broadcast(128)), reads=["posd"], writes=[f"pbt{b2}"], dma=True)
                        S.add("act", lambda e, gv_=gv_, tile=tile: e.dma_start(out=gv_[:].rearrange("p e q -> p (e q)"), in_=self.gvd[tile].rearrange("e q -> (e q)").partition_broadcast(128)), reads=["gvd"], writes=[f"gvt{b2}"], dma=True)
                        for jc in range(njc):
                            S.add("dve", lambda e, pb_=pb_, gv_=gv_, st_=st_, jc=jc: e.scalar_tensor_tensor(out=st_[:, :, jc, :], in0=pb_[:], scalar=iota4[:, jc:jc + 1], in1=gv_[:], op0=ALU.is_equal, op1=ALU.mult), reads=[f"pbt{b2}", f"gvt{b2}", "iota4"], writes=[f"selT{b2}"])
                        pbk = 4 + b2
                        nmm = 16 * njc
                        for i_, (ex_, jc) in enumerate([(a_, b_) for a_ in range(16) for b_ in range(njc)]):
                            if nm == "L":
                                S.add("pe", lambda e, st_=st_, ex_=ex_, jc=jc, i_=i_, pbk=pbk, nmm=nmm: e.matmul(ps[pbk][:, :], lhsT=st_[:, ex_, jc, :], rhs=ygh[:, ex_ * 4 + jc, :], start=(i_ == 0), stop=(i_ == nmm - 1)), reads=[f"selT{b2}", "ygh"], writes=[f"ps{pbk}"])
                            else:
                                S.add("pe", lambda e, st_=st_, ex_=ex_, i_=i_, pbk=pbk, nmm=nmm, hf=hf: e.matmul(ps[pbk][:, :], lhsT=st_[0:32, ex_, 0, :], rhs=ygc[:, ex_, :], start=(i_ == 0), stop=(i_ == nmm - 1)), reads=[f"selT{b2}", "ygc"], writes=[f"ps{pbk}"])

                        def post(b2=b2, tok=tok, pbk=pbk, hf=hf, s_=s_):
                            if hf == 0:
                                S.add("act", lambda e: e.activation(out=fh[b2][:], in_=ps[pbk][:, :], func=AF.Copy), reads=[f"ps{pbk}"], writes=[f"fh{b2}"])
                                S.add("sp", lambda e: e.dma_start(out=self.f0d[tok:tok + 128, :], in_=fh[b2][:]), reads=[f"fh{b2}"], writes=["f0d"], dma=True)
                                return
                            xb, xk = xt[b2], f"cxt{b2}"
                            rb, rk = rt[b2], f"crt{b2}"
                            S.add("sp", lambda e: e.dma_start(out=xb[:], in_=self.xres[tok:tok + 128, :]), reads=["xres"], writes=[xk], dma=True)
                            S.add("act", lambda e: e.dma_start(out=f0[b2][:], in_=self.f0d[tok:tok + 128, :]), reads=["f0d"], writes=[f"f0{b2}"], dma=True)
                            S.add("dve", lambda e: e.tensor_tensor(out=rb[:, 0:512], in0=f0[b2][:], in1=bc[f"g2_{s_}"][:, 0:512], op=ALU.mult), reads=[f"f0{b2}", f"cbc_g2_{s_}"], writes=[rk])
                            S.add("dve", lambda e: e.tensor_tensor(out=rb[:, 512:1024], in0=ps[pbk][:, :], in1=bc[f"g2_{s_}"][:, 512:1024], op=ALU.mult), reads=[f"ps{pbk}", f"cbc_g2_{s_}"], writes=[rk])
                            S.add("dve", lambda e: e.scalar_tensor_tensor(out=rb[:], in0=xb[:], scalar=ALPHA, in1=rb[:], op0=ALU.mult, op1=ALU.add), reads=[xk, rk], writes=[rk])
                            self.layer_norm_tile(rb, rk, xb, xk, bc["lng"], "cbc_lng", bc["lnb"], "cbc_lnb", bst, mv, rsd)
                            S.add("sp", lambda e: e.dma_start(out=self.xres[tok:tok + 128, :], in_=xb[:]), reads=[xk], writes=["xres"], dma=True)
                        if pend:
                            pend.pop()()
                        pend.append(post)
                    while pend:
                        pend.pop()()
        self.S.barrier()


    def phase_gdn(self, l):
        nc, S = self.nc, self.S
        ps = self.ps
        with_ctx = l < DEPTH - 1
        NCH = T // 64
        with ExitStack() as st:
            sbt = lambda name, shape, dt=F32: st.enter_context(nc.sbuf_tensor(self.uniq(name), shape, dt))
            cmask = sbt("cmask", [4, T])
            ones4 = sbt("ones4", [4, T])
            prm = sbt("prm", [4, 4])
            S.add("sp", lambda e: e.dma_start(out=cmask[:], in_=self.cst["chunkmask"][:, :]), writes=["cmask"], dma=True)
            S.add("pool", lambda e: e.memset(ones4[:], 1.0), writes=["ones4"])
            for d_ in range(2):
                S.add("sp", lambda e, d_=d_: e.dma_start(out=prm[:, d_:d_ + 1], in_=self.a_log[l, d_ * 4:(d_ + 1) * 4].rearrange("(h o) -> h o", o=1)), writes=["prm"], dma=True)
                S.add("sp", lambda e, d_=d_: e.dma_start(out=prm[:, 2 + d_:3 + d_], in_=self.dt_bias[l, d_ * 4:(d_ + 1) * 4].rearrange("(h o) -> h o", o=1)), writes=["prm"], dma=True)
            S.add("act", lambda e: e.activation(out=prm[:, 0:2], in_=prm[:, 0:2], func=AF.Exp), reads=["prm"], writes=["prm"])
            S.add("dve", lambda e: e.tensor_scalar(out=prm[:, 0:2], in0=prm[:, 0:2], scalar1=-1.0, scalar2=None, op0=ALU.mult), reads=["prm"], writes=["prm"])
            for d_ in range(2):
                a4 = sbt(f"a4_{d_}", [4, T])
                b4 = sbt(f"b4_{d_}", [4, T])
                g4 = sbt(f"g4_{d_}", [4, T])
                F4 = sbt(f"F4_{d_}", [4, T])
                w4 = sbt(f"w4_{d_}", [4, T])
                ak, bk, gk, Fk, wk = f"a4{d_}", f"b4{d_}", f"g4{d_}", f"F4{d_}", f"w4{d_}"
                S.add("sp", lambda e, a4=a4, d_=d_: e.dma_start(out=a4[:], in_=self.ag16[d_ * 8:d_ * 8 + 4, :]), reads=["ag16"], writes=[ak], dma=True)
                S.add("sp", lambda e, b4=b4, d_=d_: e.dma_start(out=b4[:], in_=self.ag16[d_ * 8 + 4:d_ * 8 + 8, :]), reads=["ag16"], writes=[bk], dma=True)
                S.add("act", lambda e, a4=a4, d_=d_: e.activation(out=a4[:], in_=a4[:], func=AF.Exp, bias=prm[:, 2 + d_:3 + d_], scale=1.0), reads=[ak, "prm"], writes=[ak])
                S.add("act", lambda e, a4=a4: e.activation(out=a4[:], in_=a4[:], func=AF.Ln, bias=1.0, scale=1.0), reads=[ak], writes=[ak])
                S.add("dve", lambda e, a4=a4, g4=g4, d_=d_: e.tensor_scalar(out=g4[:], in0=a4[:], scalar1=prm[:, d_:d_ + 1], scalar2=None, op0=ALU.mult), reads=[ak, "prm"], writes=[gk])
                S.add("act", lambda e, b4=b4: e.activation(out=b4[:], in_=b4[:], func=AF.Sigmoid), reads=[bk], writes=[bk])
                S.add("act", lambda e, b4=b4, a4=a4: e.activation(out=a4[:], in_=b4[:], func=AF.Ln), reads=[bk, ak], writes=[ak])
                S.add("dve", lambda e, F4=F4, g4=g4: e.tensor_tensor_scan(out=F4[:], data0=cmask[:], data1=g4[:], initial=0.0, op0=ALU.mult, op1=ALU.add), reads=["cmask", gk], writes=[Fk])
                F3 = F4[:].rearrange("h (n c) -> h n c", c=64)
                if d_ == 1:
                    S.add("dve", lambda e, w4=w4, F3=F3: e.tensor_tensor(out=w4[:].rearrange("h (n c) -> h n c", c=64), in0=F3[:, :, 63:64].broadcast_to([4, NCH, 64]), in1=F3, op=ALU.subtract), reads=[Fk], writes=[wk])
                    S.add("dve", lambda e, w4=w4, g4=g4, F4=F4: e.tensor_tensor(out=F4[:], in0=w4[:], in1=g4[:], op=ALU.add), reads=[wk, gk, Fk], writes=[Fk])
                    tot_ap = F3[:, :, 0:1]
                else:
                    tot_ap = F3[:, :, 63:64]
                G4 = F4
                Rd = self.Rd
                p0 = d_ * 4
                S.add("dve", lambda e, w4=w4, G4=G4, a4=a4: e.tensor_tensor(out=w4[:], in0=G4[:], in1=a4[:], op=ALU.add), reads=[Fk, ak, wk], writes=[wk])
                S.add("sp", lambda e, w4=w4, p0=p0: e.dma_start(out=Rd[0, 0, p0:p0 + 4, :], in_=w4[:]), reads=[wk], writes=["Rd"], dma=True)
                S.add("sp", lambda e, p0=p0: e.dma_start(out=Rd[0, 1, p0:p0 + 4, :], in_=ones4[:]), reads=["ones4"], writes=["Rd"], dma=True)
                S.add("sp", lambda e, p0=p0: e.dma_start(out=Rd[1, 0, p0:p0 + 4, :], in_=ones4[:]), reads=["ones4"], writes=["Rd"], dma=True)
                S.add("sp", lambda e, p0=p0, G4=G4: e.dma_start(out=Rd[2, 0, p0:p0 + 4, :], in_=G4[:]), reads=[Fk], writes=["Rd"], dma=True)
                S.add("sp", lambda e, p0=p0: e.dma_start(out=Rd[2, 1, p0:p0 + 4, :], in_=ones4[:]), reads=["ones4"], writes=["Rd"], dma=True)
                S.add("dve", lambda e, w4=w4, G4=G4: e.tensor_scalar(out=w4[:], in0=G4[:], scalar1=-1.0, scalar2=None, op0=ALU.mult), reads=[Fk, wk], writes=[wk])
                S.add("sp", lambda e, w4=w4, p0=p0: e.dma_start(out=Rd[1, 1, p0:p0 + 4, :], in_=w4[:]), reads=[wk], writes=["Rd"], dma=True)
                TSd = self.TSd
                S.add("sp", lambda e, b4=b4, p0=p0: e.dma_start(out=TSd[16 + p0:16 + p0 + 4, :], in_=b4[:]), reads=[bk], writes=["TSd"], dma=True)
                S.add("sp", lambda e, g4=g4, p0=p0: e.dma_start(out=TSd[32 + p0:32 + p0 + 4, :], in_=g4[:]), reads=[gk], writes=["TSd"], dma=True)
                S.add("dve", lambda e, w4=w4, tot_ap=tot_ap, G4=G4: e.tensor_tensor(out=w4[:].rearrange("h (n c) -> h n c", c=64), in0=tot_ap.broadcast_to([4, NCH, 64]), in1=G4[:].rearrange("h (n c) -> h n c", c=64), op=ALU.subtract), reads=[Fk, wk], writes=[wk])
                S.add("act", lambda e, w4=w4: e.activation(out=w4[:], in_=w4[:], func=AF.Exp), reads=[wk], writes=[wk])
                S.add("sp", lambda e, w4=w4, p0=p0: e.dma_start(out=TSd[24 + p0:24 + p0 + 4, :], in_=w4[:]), reads=[wk], writes=["TSd"], dma=True)
                S.add("act", lambda e, g4=g4, G4=G4: e.activation(out=g4[:], in_=G4[:], func=AF.Exp), reads=[Fk, gk], writes=[gk])
                S.add("sp", lambda e, g4=g4, p0=p0: e.dma_start(out=TSd[0 + p0:0 + p0 + 4, :], in_=g4[:]), reads=[gk], writes=["TSd"], dma=True)
                S.add("dve", lambda e, a4=a4, g4=g4, b4=b4: e.tensor_tensor(out=a4[:], in0=g4[:], in1=b4[:], op=ALU.mult), reads=[gk, bk, ak], writes=[ak])
                S.add("sp", lambda e, a4=a4, p0=p0: e.dma_start(out=TSd[8 + p0:8 + p0 + 4, :], in_=a4[:]), reads=[ak], writes=["TSd"], dma=True)
        self.S.barrier()
        with ExitStack() as st:
            sbt = lambda name, shape, dt=F32: st.enter_context(nc.sbuf_tensor(self.uniq(name), shape, dt))
            cw = sbt("cw", [128, 6, 5])
            bones = sbt("gbones", [128, 128])
            S.add("sp", lambda e: e.dma_start(out=cw[:], in_=self.conv_aT[l]), writes=["cw"], dma=True)
            S.add("sp", lambda e: e.dma_start(out=bones[:], in_=self.cst["bones64"][:, :]), writes=["gbones"], dma=True)
            ub = [sbt(f"ub{i}", [128, T + 8]) for i in range(2)]
            yb = [sbt(f"yb{i}", [128, T]) for i in range(2)]
            sq = [sbt(f"gsq{i}", [128, 512]) for i in range(2)]
            rs = [sbt(f"grs{i}", [128, 512]) for i in range(2)]
            tkb = [sbt(f"tkb{i}", [128, 128]) for i in range(2)]
            for i in range(2):
                S.add("pool", lambda e, i=i: e.memset(ub[i][:], 0.0), writes=[f"ub{i}"])
            tcn = 0
            for c_ in range(6):
                u, uk = ub[c_ % 2], f"ub{c_ % 2}"
                y, yk = yb[c_ % 2], f"yb{c_ % 2}"
                S.add("sp", lambda e, u=u, c_=c_: e.dma_start(out=u[:, 2:2 + NCTX], in_=self.aqkv[c_ * 128:(c_ + 1) * 128, 0:NCTX]), reads=["aqkv"], writes=[uk], dma=True)
                S.add("act", lambda e, u=u, c_=c_: e.dma_start(out=u[:, 6 + NCTX:6 + T], in_=self.aqkv[c_ * 128:(c_ + 1) * 128, NCTX:T]), reads=["aqkv"], writes=[uk], dma=True)
                for (o0, u0, n) in [(0, 0, NCTX), (NCTX, NCTX + 4, NLAT)]:
                    for j in range(5):
                        if j == 0:
                            S.add("dve", lambda e, y=y, u=u, o0=o0, u0=u0, n=n, c_=c_: e.tensor_scalar(out=y[:, o0:o0 + n], in0=u[:, u0:u0 + n], scalar1=cw[:, c_, 0:1], scalar2=None, op0=ALU.mult), reads=[uk, "cw"], writes=[yk])
                        else:
                            S.add("dve", lambda e, y=y, u=u, o0=o0, u0=u0, n=n, c_=c_, j=j: e.scalar_tensor_tensor(out=y[:, o0:o0 + n], in0=u[:, u0 + j:u0 + j + n], scalar=cw[:, c_, j:j + 1], in1=y[:, o0:o0 + n], op0=ALU.mult, op1=ALU.add), reads=[uk, "cw", yk], writes=[yk])
                S.add("act", lambda e, y=y: e.activation(out=y[:], in_=y[:], func=AF.Silu), reads=[yk], writes=[yk])
                if c_ < 4:
                    for (t0, W) in GROUPS:
                        b2 = tcn % 2
                        tcn += 1
                        S.add("act", lambda e, y=y, t0=t0, W=W, b2=b2: e.activation(out=sq[b2][:, 0:W], in_=y[:, t0:t0 + W], func=AF.Square), reads=[yk], writes=[f"gsq{b2}"])
                        S.add("pe", lambda e, W=W, b2=b2: e.matmul(ps[b2][:, 0:W], lhsT=bones[:], rhs=sq[b2][:, 0:W], start=True, stop=True), reads=[f"gsq{b2}", "gbones"], writes=[f"ps{b2}"])
                        S.add("act", lambda e, W=W, b2=b2: e.activation(out=rs[b2][:, 0:W], in_=ps[b2][:, 0:W], func=AF.Sqrt, bias=EPS, scale=1.0), reads=[f"ps{b2}"], writes=[f"grs{b2}"])
                        S.add("dve", lambda e, W=W, b2=b2: e.reciprocal(out=rs[b2][:, 0:W], in_=rs[b2][:, 0:W]), reads=[f"grs{b2}"], writes=[f"grs{b2}"])
                        qs = 0.125 if c_ < 2 else 1.0
                        S.add("dve", lambda e, y=y, t0=t0, W=W, b2=b2, qs=qs: e.scalar_tensor_tensor(out=y[:, t0:t0 + W], in0=y[:, t0:t0 + W], scalar=qs, in1=rs[b2][:, 0:W], op0=ALU.mult, op1=ALU.mult), reads=[yk, f"grs{b2}"], writes=[yk])
                    kind = c_ // 2
                    S.add("sp", lambda e, y=y, kind=kind, c_=c_: e.dma_start(out=self.qkhd[kind, (c_ % 2) * 2:(c_ % 2) * 2 + 2].rearrange("h d t -> (h d) t"), in_=y[:]), reads=[yk], writes=["qkhd"], dma=True)
                if c_ >= 2:
                    kv = 0 if c_ < 4 else 1
                    col0 = kv * 256 + (c_ % 2) * 128
                    for tt in range(NT):
                        pb = 2 + tt % 2
                        tb_ = tkb[tt % 2]
                        S.add("pe", lambda e, y=y, tt=tt, pb=pb: e.transpose(ps[pb][:, 0:128], y[:, tt * 128:(tt + 1) * 128], self.ident[:]), reads=[yk, "ident"], writes=[f"ps{pb}"])
                        S.add("act", lambda e, tb_=tb_, pb=pb: e.activation(out=tb_[:], in_=ps[pb][:, 0:128], func=AF.Copy), reads=[f"ps{pb}"], writes=[f"tkb{tt % 2}"])
                        S.add("sp", lambda e, tb_=tb_, tt=tt, col0=col0: e.dma_start(out=self.kvtok[tt * 128:(tt + 1) * 128, col0:col0 + 128], in_=tb_[:]), reads=[f"tkb{tt % 2}"], writes=["kvtok"], dma=True)
        self.S.barrier()
        order_b = [3, 2, 1, 0] + list(range(NCH - 1, 3, -1))
        with ExitStack() as st:
            sbt = lambda name, shape, dt=F32: st.enter_context(nc.sbuf_tensor(self.uniq(name), shape, dt))
            TS = sbt("TS", [40, T])
            S.add("sp", lambda e: e.dma_start(out=TS[:], in_=self.TSd[:, :]), reads=["TSd"], writes=["TS"], dma=True)
            cm = {}
            for nm in ["mask_a", "mask_at", "mask_pt", "eye8"]:
                cm[nm] = sbt("g_" + nm, [64, 512])
                S.add("sp", lambda e, nm=nm: e.dma_start(out=cm[nm][:], in_=self.cst[nm][:, :]), writes=["g_" + nm], dma=True)
            ones64 = sbt("g_ones64", [128, 64])
            S.add("pool", lambda e: e.memset(ones64[:], 0.0), writes=["g_ones64"])
            S.add("pool", lambda e: e.memset(ones64[0:64, :], 1.0), writes=["g_ones64"])
            class TV:
                def __init__(self, t):
                    self.t = t
                def __getitem__(self, idx):
                    if not isinstance(idx, tuple):
                        idx = (idx,)
                    return self.t[(slice(0, 64),) + tuple(idx[1:])]
                def k(self, *idx):
                    return self.t[(slice(None),) + tuple(idx)]

            def ztile(name, shape):
                t = sbt(name, [128] + list(shape[1:]))
                S.add("pool", lambda e, t=t: e.memset(t[:], 0.0), writes=[name])
                return TV(t)
            NS = 4
            rows_t = [ztile(f"rows{i}", [2, 3, 8, 64]) for i in range(NS)]
            qk_t = [ztile(f"qkt{i}", [64, 2, 8, 64]) for i in range(NS)]
            kv_t = [sbt(f"kvt{i}", [64, 2, 512]) for i in range(NS)]
            scal = [ztile(f"scal{i}", [64, 80]) for i in range(NS)]
            GLs = [sbt(f"GLs{i}", [64, 8]) for i in range(NS)]
            def t8(name, n=1):
                return [ztile(f"{name}{i}", [64, 512]) for i in range(n)]
            rw, rv = t8("rw", 2), t8("rv", 2)
            kd, PT, wT, uc = t8("kd", NS), t8("PT", NS), t8("wT", NS), t8("uc", NS)
            tA = [t8(f"tA{j}_", 3) for j in range(2)]
            Qa, QTa, Pm = [t8(f"Qa{j}_", 2) for j in range(2)], [t8(f"QTa{j}_", 2) for j in range(2)], [t8(f"Pm{j}_", 2) for j in range(2)]
            uu, tq, oo = t8("uu", 2), t8("tq", 2), t8("oo", 2)
            Sst = t8("Sst", 2)
            S.add("dve", lambda e: e.memset(Sst[0][:], 0.0), writes=["Sst0"])
            v3 = lambda ap: ap.rearrange("c (p e) -> c p e", p=8)
            kkS, qkS = t8("kkS", 2), t8("qkS", 2)

            def mk_nb(banks):
                st_ = {"i": 0}

                def f():
                    b = banks[st_["i"] % len(banks)]
                    st_["i"] += 1
                    return b
                return f
            nb_intra = [mk_nb([0, 1, 2]), mk_nb([3, 4, 5])]
            nb_scan = mk_nb([6, 7])

            def col(kind, d_):
                return d_ * 40 + kind * 8 + d_ * 4

            def intra_gen(step):
                cf, cb = step, order_b[step]
                tok = [cf * 64, cb * 64]
                sl = step % NS
                i2 = step % 2
                rt_, qt_, kt_ = rows_t[sl], qk_t[sl], kv_t[sl]
                rk_, qk_, kk_ = f"rows{sl}", f"qkt{sl}", f"kvt{sl}"
                nb = nb_intra[i2]
                for d_ in range(2):
                    t0 = tok[d_]
                    for k3 in range(3):
                        S.add("sp" if d_ == 0 else "act", lambda e, rt_=rt_, k3=k3, d_=d_, t0=t0: e.dma_start(out=rt_.t[0:2, k3, d_ * 4:d_ * 4 + 4, :], in_=self.Rd[k3, :, d_ * 4:d_ * 4 + 4, t0:t0 + 64]), reads=["Rd"], writes=[rk_], dma=True)
                    for kind in range(2):
                        S.add("sp" if kind == 0 else "act", lambda e, qt_=qt_, kind=kind, d_=d_, t0=t0: e.dma_start(out=qt_[:, kind, d_ * 4:d_ * 4 + 4, :], in_=self.qkhd[kind, :, :, t0:t0 + 64].rearrange("h d t -> d h t")), reads=["qkhd"], writes=[qk_], dma=True)
                    S.add("sp", lambda e, kt_=kt_, d_=d_, t0=t0: e.dma_start(out=kt_[:, d_, :], in_=self.kvtok[t0:t0 + 64, :]), reads=["kvtok"], writes=[kk_], dma=True)
                yield
                bs = nb()
                for d_ in range(2):
                    S.add("pe", lambda e, d_=d_, bs=bs, t0=tok[d_]: e.transpose(ps[bs][0:64, d_ * 40:(d_ + 1) * 40], TS[:, t0:t0 + 64], self.ident[0:40, 0:40]), reads=["TS", "ident"], writes=[f"ps{bs}"])
                sc = scal[sl]
                sck = f"scal{sl}"
                S.add("act", lambda e, sc=sc, bs=bs: e.activation(out=sc[:], in_=ps[bs][0:64, 0:80], func=AF.Copy), reads=[f"ps{bs}"], writes=[sck])
                yield
                for d_ in range(2):
                    c0 = col(4, d_)
                    S.add("pe", lambda e, d_=d_, bs=bs, c0=c0, sc=sc: e.matmul(ps[bs][0:64, 96 + d_ * 4:100 + d_ * 4], lhsT=ones64[:], rhs=sc.k(slice(c0, c0 + 4)), start=True, stop=True), reads=[sck, "g_ones64"], writes=[f"ps{bs}"])
                gl = GLs[sl]
                glk = f"GLs{sl}"
                S.add("act", lambda e, gl=gl, bs=bs: e.activation(out=gl[:], in_=ps[bs][0:64, 96:104], func=AF.Exp), reads=[f"ps{bs}"], writes=[glk])
                rw_, rv_, kd_ = rw[i2], rv[i2], kd[sl]
                for d_ in range(2):
                    bcol = lambda kind, d_=d_, sc=sc: sc[:, col(kind, d_):col(kind, d_) + 4].unsqueeze(2).broadcast_to([64, 4, 64])
                    kview = kt_[:, d_, 0:256].rearrange("c (h e) -> c h e", h=4)
                    vview = kt_[:, d_, 256:512].rearrange("c (h e) -> c h e", h=4)
                    S.add("dve", lambda e, rw_=rw_, d_=d_, kview=kview, bc=bcol(1): e.tensor_tensor(out=v3(rw_[:])[:, d_ * 4:d_ * 4 + 4, :], in0=kview, in1=bc, op=ALU.mult), reads=[kk_, sck], writes=[f"rw{i2}"])
                    S.add("pool", lambda e, rv_=rv_, d_=d_, vview=vview, bc=bcol(2): e.tensor_tensor(out=v3(rv_[:])[:, d_ * 4:d_ * 4 + 4, :], in0=vview, in1=bc, op=ALU.mult), reads=[kk_, sck], writes=[f"rv{i2}"])
                    S.add("pool", lambda e, kd_=kd_, d_=d_, kview=kview, bc=bcol(3): e.tensor_tensor(out=v3(kd_[:])[:, d_ * 4:d_ * 4 + 4, :], in0=kview, in1=bc, op=ALU.mult), reads=[kk_, sck], writes=[f"kd{sl}"])
                bKK, bQK = nb(), nb()
                for p in range(8):
                    kT_p = qt_.k(1, p, slice(None))
                    qT_p = qt_.k(0, p, slice(None))
                    cs = slice(p * 64, (p + 1) * 64)
                    S.add("pe", lambda e, kT_p=kT_p, cs=cs, bKK=bKK: e.matmul(ps[bKK][0:64, cs], lhsT=kT_p, rhs=kT_p, start=True, stop=True), reads=[qk_], writes=[f"ps{bKK}"])
                    S.add("pe", lambda e, kT_p=kT_p, qT_p=qT_p, cs=cs, bQK=bQK: e.matmul(ps[bQK][0:64, cs], lhsT=kT_p, rhs=qT_p, start=True, stop=True), reads=[qk_], writes=[f"ps{bQK}"])
                kk_s, qk_s = kkS[i2], qkS[i2]
                S.add("act", lambda e, kk_s=kk_s, bKK=bKK: e.activation(out=kk_s[:], in_=ps[bKK][0:64, :], func=AF.Copy), reads=[f"ps{bKK}"], writes=[f"kkS{i2}"])
                S.add("dve", lambda e, qk_s=qk_s, bQK=bQK: e.tensor_copy(out=qk_s[:], in_=ps[bQK][0:64, :]), reads=[f"ps{bQK}"], writes=[f"qkS{i2}"])
                yield
                Qs, QTs, Ps, tAs = Qa[i2], QTa[i2], Pm[i2], tA[i2]
                qn, qtn, pn, tn = f"Qa{i2}_", f"QTa{i2}_", f"Pm{i2}_", f"tA{i2}_"
                PT_ = PT[sl]
                for ti, (ra, rb_, mk, gsrc, gk_, dst, dk, neg) in enumerate([(0, 1, "mask_a", kk_s, f"kkS{i2}", QTs[0], qtn + "0", -1.0), (1, 0, "mask_at", kk_s, f"kkS{i2}", Qs[0], qn + "0", -1.0), (1, 2, "mask_pt", qk_s, f"qkS{i2}", PT_, f"PT{sl}", 1.0)]):
                    bD = nb()
                    for p in range(8):
                        cs = slice(p * 64, (p + 1) * 64)
                        S.add("pe", lambda e, la=rt_.k(ra, p, slice(None)), rb2=rt_.k(rb_, p, slice(None)), cs=cs, bD=bD: e.matmul(ps[bD][0:64, cs], lhsT=la, rhs=rb2, start=True, stop=True), reads=[rk_], writes=[f"ps{bD}"])
                    tt_ = tAs[ti]
                    S.add("dve", lambda e, tt_=tt_, bD=bD, mk=mk: e.tensor_tensor(out=tt_[:], in0=ps[bD][0:64, :], in1=cm[mk][:], op=ALU.add), reads=[f"ps{bD}", "g_" + mk], writes=[tn + str(ti)])
                    S.add("act", lambda e, tt_=tt_: e.activation(out=tt_[:], in_=tt_[:], func=AF.Exp), reads=[tn + str(ti)], writes=[tn + str(ti)])
                    S.add("pool", lambda e, tt_=tt_, gsrc=gsrc, dst=dst: e.tensor_tensor(out=dst[:], in0=tt_[:], in1=gsrc[:], op=ALU.mult), reads=[tn + str(ti), gk_], writes=[dk])
                    if neg < 0:
                        S.add("pool", lambda e, dst=dst: e.tensor_scalar(out=dst[:], in0=dst[:], scalar1=-1.0, scalar2=None, op0=ALU.mult), reads=[dk], writes=[dk])
                    yield
                S.add("pool", lambda e, Ps=Ps, Qs=Qs: e.tensor_tensor(out=Ps[0][:], in0=cm["eye8"][:], in1=Qs[0][:], op=ALU.add), reads=["g_eye8", qn + "0"], writes=[pn + "0"])
                cur = 0
                for lev in range(1, 6):
                    nxt = 1 - cur
                    Qc, QTc, Qn, QTn = Qs[cur], QTs[cur], Qs[nxt], QTs[nxt]
                    bQ, bQT, bPQ = nb(), nb(), nb()
                    for p in range(8):
                        cs = slice(p * 64, (p + 1) * 64)
                        if lev < 5:
                            S.add("pe", lambda e, Qc=Qc, QTc=QTc, cs=cs, bQ=bQ: e.matmul(ps[bQ][0:64, cs], lhsT=QTc.k(cs), rhs=Qc.k(cs), start=True, stop=True), reads=[qn + str(cur), qtn + str(cur)], writes=[f"ps{bQ}"])
                        S.add("pe", lambda e, Qc=Qc, QTc=QTc, cs=cs, bQT=bQT: e.matmul(ps[bQT][0:64, cs], lhsT=Qc.k(cs), rhs=QTc.k(cs), start=True, stop=True), reads=[qn + str(cur), qtn + str(cur)], writes=[f"ps{bQT}"])
                    if lev < 5:
                        S.add("act", lambda e, Qn=Qn, bQ=bQ: e.activation(out=Qn[:], in_=ps[bQ][0:64, :], func=AF.Copy), reads=[f"ps{bQ}"], writes=[qn + str(nxt)])
                    S.add("dve", lambda e, QTn=QTn, bQT=bQT: e.tensor_copy(out=QTn[:], in_=ps[bQT][0:64, :]), reads=[f"ps{bQT}"], writes=[qtn + str(nxt)])
                    yield
                    Pc, Pn = Ps[cur], Ps[nxt]
                    for p in range(8):
                        cs = slice(p * 64, (p + 1) * 64)
                        S.add("pe", lambda e, QTn=QTn, Pc=Pc, cs=cs, bPQ=bPQ: e.matmul(ps[bPQ][0:64, cs], lhsT=QTn.k(cs), rhs=Pc.k(cs), start=True, stop=True), reads=[qtn + str(nxt), pn + str(cur)], writes=[f"ps{bPQ}"])
                    S.add("dve", lambda e, Pc=Pc, Pn=Pn, bPQ=bPQ: e.tensor_tensor(out=Pn[:], in0=Pc[:], in1=ps[bPQ][0:64, :], op=ALU.add), reads=[pn + str(cur), f"ps{bPQ}"], writes=[pn + str(nxt)])
                    yield
                    cur = nxt
                MT, mtk = Ps[cur], pn + str(cur)
                bW, bU = nb(), nb()
                for p in range(8):
                    cs = slice(p * 64, (p + 1) * 64)
                    S.add("pe", lambda e, rw_=rw_, MT=MT, cs=cs, bW=bW: e.matmul(ps[bW][0:64, cs], lhsT=rw_.k(cs), rhs=MT.k(cs), start=True, stop=True), reads=[f"rw{i2}", mtk], writes=[f"ps{bW}"])
                    S.add("pe", lambda e, rv_=rv_, MT=MT, cs=cs, bU=bU: e.matmul(ps[bU][0:64, cs], lhsT=MT.k(cs), rhs=rv_.k(cs), start=True, stop=True), reads=[f"rv{i2}", mtk], writes=[f"ps{bU}"])
                wT_, uc_ = wT[sl], uc[sl]
                S.add("act", lambda e, wT_=wT_, bW=bW: e.activation(out=wT_[:], in_=ps[bW][0:64, :], func=AF.Copy), reads=[f"ps{bW}"], writes=[f"wT{sl}"])
                S.add("dve", lambda e, uc_=uc_, bU=bU: e.tensor_copy(out=uc_[:], in_=ps[bU][0:64, :]), reads=[f"ps{bU}"], writes=[f"uc{sl}"])
                yield

            def scan_gen(step):
                cf, cb = step, order_b[step]
                tok = [cf * 64, cb * 64]
                sl = step % NS
                i2 = step % 2
                qt_, qk_ = qk_t[sl], f"qkt{sl}"
                sc, sck = scal[sl], f"scal{sl}"
                gl, glk = GLs[sl], f"GLs{sl}"
                kd_, PT_, wT_, uc_ = kd[sl], PT[sl], wT[sl], uc[sl]
                uu_, tq_, oo_ = uu[i2], tq[i2], oo[i2]
                Sc, Sn = Sst[step % 2], Sst[(step + 1) % 2]
                sck_, snk_ = f"Sst{step % 2}", f"Sst{(step + 1) % 2}"
                bWS = nb_scan()
                for p in range(8):
                    cs = slice(p * 64, (p + 1) * 64)
                    S.add("pe", lambda e, wT_=wT_, Sc=Sc, cs=cs, bWS=bWS: e.matmul(ps[bWS][0:64, cs], lhsT=wT_.k(cs), rhs=Sc.k(cs), start=True, stop=True), reads=[f"wT{sl}", sck_], writes=[f"ps{bWS}"])
                S.add("dve", lambda e, uu_=uu_, uc_=uc_, bWS=bWS: e.tensor_tensor(out=uu_[:], in0=uc_[:], in1=ps[bWS][0:64, :], op=ALU.subtract), reads=[f"uc{sl}", f"ps{bWS}"], writes=[f"uu{i2}"])
                bQS = nb_scan()
                for p in range(8):
                    cs = slice(p * 64, (p + 1) * 64)
                    S.add("pe", lambda e, qT_p=qt_.k(0, p, slice(None)), Sc=Sc, cs=cs, bQS=bQS: e.matmul(ps[bQS][0:64, cs], lhsT=qT_p, rhs=Sc.k(cs), start=True, stop=True), reads=[qk_, sck_], writes=[f"ps{bQS}"])
                for d_ in range(2):
                    bc = sc[:, col(0, d_):col(0, d_) + 4].unsqueeze(2).broadcast_to([64, 4, 64])
                    S.add("dve", lambda e, tq_=tq_, d_=d_, bc=bc, bQS=bQS: e.tensor_tensor(out=v3(tq_[:])[:, d_ * 4:d_ * 4 + 4, :], in0=v3(ps[bQS][0:64, :])[:, d_ * 4:d_ * 4 + 4, :], in1=bc, op=ALU.mult), reads=[f"ps{bQS}", sck], writes=[f"tq{i2}"])
                yield
                bKU = nb_scan()
                for p in range(8):
                    cs = slice(p * 64, (p + 1) * 64)
                    S.add("pe", lambda e, kd_=kd_, uu_=uu_, cs=cs, bKU=bKU: e.matmul(ps[bKU][0:64, cs], lhsT=kd_.k(cs), rhs=uu_.k(cs), start=True, stop=True), reads=[f"kd{sl}", f"uu{i2}"], writes=[f"ps{bKU}"])
                S.add("pool", lambda e, Sn=Sn, Sc=Sc, gl=gl: e.tensor_tensor(out=v3(Sn[:]), in0=v3(Sc[:]), in1=gl[:].unsqueeze(2).broadcast_to([64, 8, 64]), op=ALU.mult), reads=[sck_, glk], writes=[snk_])
                S.add("dve", lambda e, Sn=Sn, bKU=bKU: e.tensor_tensor(out=Sn[:], in0=Sn[:], in1=ps[bKU][0:64, :], op=ALU.add), reads=[snk_, f"ps{bKU}"], writes=[snk_])
                bPU = nb_scan()
                for p in range(8):
                    cs = slice(p * 64, (p + 1) * 64)
                    S.add("pe", lambda e, PT_=PT_, uu_=uu_, cs=cs, bPU=bPU: e.matmul(ps[bPU][0:64, cs], lhsT=PT_.k(cs), rhs=uu_.k(cs), start=True, stop=True), reads=[f"PT{sl}", f"uu{i2}"], writes=[f"ps{bPU}"])
                S.add("dve", lambda e, oo_=oo_, tq_=tq_, bPU=bPU: e.tensor_tensor(out=oo_[:], in0=tq_[:], in1=ps[bPU][0:64, :], op=ALU.add), reads=[f"tq{i2}", f"ps{bPU}"], writes=[f"oo{i2}"])
                S.add("sp", lambda e, oo_=oo_, t0=tok[0]: e.dma_start(out=self.ofb[0, t0:t0 + 64, :], in_=oo_[:, 0:256]), reads=[f"oo{i2}"], writes=["ofb"], dma=True)
                S.add("act", lambda e, oo_=oo_, t0=tok[1]: e.dma_start(out=self.ofb[1, t0:t0 + 64, :], in_=oo_[:, 256:512]), reads=[f"oo{i2}"], writes=["ofb"], dma=True)
                yield

            def chain(*gs):
                for g in gs:
                    yield from g

            def interleave(gens):
                gens = list(gens)
                while gens:
                    for g in list(gens):
                        try:
                            next(g)
                        except StopIteration:
                            gens.remove(g)

            interleave([intra_gen(0), intra_gen(1)])
            for k in range(0, NCH, 2):
                gs = [chain(scan_gen(k), scan_gen(k + 1))]
                if k + 2 < NCH:
                    gs += [intra_gen(k + 2), intra_gen(k + 3)]
                interleave(gs)
        self.S.barrier()
        with ExitStack() as st:
            sbt = lambda name, shape, dt=F32: st.enter_context(nc.sbuf_tensor(self.uniq(name), shape, dt))
            gn = sbt("gn", [128, 4, 64])
            S.add("sp", lambda e: e.dma_start(out=gn[:, 0, :], in_=self.gdn_norm[l:l + 1, :].partition_broadcast(128)), writes=["gn"], dma=True)
            for h in range(1, 4):
                S.add("pool", lambda e, h=h: e.tensor_copy(out=gn[:, h, :], in_=gn[:, 0, :]), reads=["gn"], writes=["gn"])
            of = [sbt(f"of{i}", [128, 2, 256]) for i in range(2)]
            gt = [sbt(f"gt{i}", [128, 256], BF16) for i in range(2)]
            os_ = [sbt(f"os{i}", [128, 256]) for i in range(2)]
            fsq = [sbt(f"fsq{i}", [128, 256]) for i in range(2)]
            fss = [sbt(f"fss{i}", [128, 4]) for i in range(2)]
            ab = [sbt(f"fab{i}", [128, 256]) for i in range(2)]
            aT = [sbt(f"faT{i}", [128, 2, 128], BF16) for i in range(2)]
            for tt in range(0 if with_ctx else 2, NT):
                b2 = tt % 2
                tok = tt * 128
                S.add("sp", lambda e, b2=b2, tok=tok: e.dma_start(out=of[b2][:], in_=self.ofb[:, tok:tok + 128, :].rearrange("d t f -> t d f")), reads=["ofb"], writes=[f"of{b2}"], dma=True)
                S.add("act", lambda e, b2=b2, tok=tok: e.dma_start(out=gt[b2][:], in_=self.gateA[tok:tok + 128, :]), reads=["gateA"], writes=[f"gt{b2}"], dma=True)
                S.add("dve", lambda e, b2=b2: e.tensor_tensor(out=os_[b2][:], in0=of[b2][:, 0, :], in1=of[b2][:, 1, :], op=ALU.add), reads=[f"of{b2}"], writes=[f"os{b2}"])
                S.add("act", lambda e, b2=b2: e.activation(out=fsq[b2][:], in_=os_[b2][:], func=AF.Square), reads=[f"os{b2}"], writes=[f"fsq{b2}"])
                S.add("dve", lambda e, b2=b2: e.tensor_reduce(out=fss[b2][:], in_=fsq[b2][:].rearrange("p (h e) -> p h e", h=4), axis=AX.X, op=ALU.add), reads=[f"fsq{b2}"], writes=[f"fss{b2}"])
                S.add("act", lambda e, b2=b2: e.activation(out=fss[b2][:], in_=fss[b2][:], func=AF.Sqrt, scale=1.0 / 64.0, bias=EPS), reads=[f"fss{b2}"], writes=[f"fss{b2}"])
                S.add("dve", lambda e, b2=b2: e.reciprocal(out=fss[b2][:], in_=fss[b2][:]), reads=[f"fss{b2}"], writes=[f"fss{b2}"])
                o3 = os_[b2][:].rearrange("p (h e) -> p h e", h=4)
                S.add("dve", lambda e, b2=b2, o3=o3: e.tensor_tensor(out=o3, in0=o3, in1=fss[b2][:].unsqueeze(2).broadcast_to([128, 4, 64]), op=ALU.mult), reads=[f"os{b2}", f"fss{b2}"], writes=[f"os{b2}"])
                S.add("pool", lambda e, b2=b2, o3=o3: e.tensor_tensor(out=o3, in0=o3, in1=gn[:], op=ALU.mult), reads=[f"os{b2}", "gn"], writes=[f"os{b2}"])
                S.add("pool", lambda e, b2=b2: e.tensor_tensor(out=ab[b2][:], in0=os_[b2][:], in1=gt[b2][:], op=ALU.mult), reads=[f"os{b2}", f"gt{b2}"], writes=[f"fab{b2}"])
                pb = 2 * b2
                for c_ in range(2):
                    S.add("pe", lambda e, b2=b2, c_=c_, pb=pb: e.transpose(ps[pb + c_][:, 0:128], ab[b2][:, c_ * 128:(c_ + 1) * 128], self.ident[:]), reads=[f"fab{b2}", "ident"], writes=[f"ps{pb + c_}"])
                    S.add("act", lambda e, b2=b2, c_=c_, pb=pb: e.activation(out=aT[b2][:, c_, :], in_=ps[pb + c_][:, 0:128], func=AF.Copy), reads=[f"ps{pb + c_}"], writes=[f"faT{b2}"])
                S.add("sp", lambda e, b2=b2, tok=tok: e.dma_start(out=self.oT[0:2, :, tok:tok + 128].rearrange("c p t -> p c t"), in_=aT[b2][:]), reads=[f"faT{b2}"], writes=["oT"], dma=True)
        self.S.barrier()


def prep_inputs(inputs, b, layers):
    L = layers
    f = lambda a: np.ascontiguousarray(a, dtype=np.float32)
    m = {}
    m["x"] = f(inputs["x"][b])
    m["ctx"] = f(inputs["ctx"][b])
    cT = np.concatenate([inputs["c"][b].reshape(8, 128).T, inputs["c_ctx"].reshape(8, 128).T], axis=1)
    m["cT"] = f(cT)
    m["w_mod"] = f(inputs["w_mod"][L])
    m["b_modT"] = f(inputs["b_mod"][L].reshape(len(L), 48, 128).transpose(0, 2, 1))
    m["w_in"] = f(inputs["w_in"][L])
    m["conv_aT"] = f(inputs["conv_a"][L].reshape(len(L), 5, 6, 128).transpose(0, 3, 2, 1))
    m["a_log"] = f(inputs["a_log"][L].reshape(len(L), 8))
    m["dt_bias"] = f(inputs["dt_bias"][L].reshape(len(L), 8))
    m["gdn_norm"] = f(inputs["gdn_norm"][L])
    m["diff_lambda"] = f(inputs["diff_lambda"][L].reshape(len(L), 128))
    m["diff_norm"] = f(inputs["diff_norm"][L])
    m["qk_norm_c"] = f(inputs["qk_norm_c"][L].reshape(len(L), 128))
    m["sink_d"] = f(inputs["sink_d"][L])
    for k in ["w_br", "w_out", "ln1_g", "ln1_b", "w_router", "w_gate_e", "w_up_e", "w_down_e", "ln2_g", "ln2_b"]:
        m[k] = f(inputs[k][L])
    for k, v in host_consts().items():
        m["c_" + k] = f(v)
    return m


def kernel(**inputs):
    layers = list(range(DEPTH))
    bld = Builder(DEPTH)
    nc = bld.build()
    in_maps = [prep_inputs(inputs, b, layers) for b in range(8)]
    res = run_bass_kernel_spmd(nc, in_maps, core_ids=list(range(8)))
    return np.stack([res.results[b]["out"] for b in range(8)], axis=0).astype(np.float32)
```
